# Optimizing a Trainium2 kernel written in Bass

```python
import math
import jax, jax.numpy as jnp
from jax import lax
import numpy as np

D_MODEL = 4096
BATCH = 8
SEQ = 2048
DEPTH = 1

HEAD_DIM = 128
MOBA_HEADS = D_MODEL // (2 * HEAD_DIM)
DIFF_HEADS = D_MODEL // (4 * HEAD_DIM)
MOBA_WIDTH = MOBA_HEADS * HEAD_DIM
DIFF_QK_WIDTH = DIFF_HEADS * 2 * HEAD_DIM
DIFF_V_WIDTH = DIFF_HEADS * 2 * HEAD_DIM
MIX_WIDTH = MOBA_WIDTH + DIFF_V_WIDTH
IN_WIDTH = 3 * MOBA_WIDTH + 2 * DIFF_QK_WIDTH + DIFF_V_WIDTH
IN_SPLITS = (MOBA_WIDTH, 2 * MOBA_WIDTH, 3 * MOBA_WIDTH,
             3 * MOBA_WIDTH + DIFF_QK_WIDTH, 3 * MOBA_WIDTH + 2 * DIFF_QK_WIDTH)

MOBA_BLOCK = 256
MOBA_TOPK = 3
MOBA_Q_CHUNK = 8
DENSE_Q_BLOCK = 128
ROPE_THETA = 500000.0
ROT_DIM = HEAD_DIM // 4
MEM_LEN = 256
MEM_HEADS = 4
MEM_HEAD_DIM = 128
MEM_WIDTH = MEM_HEADS * MEM_HEAD_DIM
D_FF = -(-8 * D_MODEL // (3 * 256)) * 256
NORM_EPS = 1e-6

kernel_name = "hymba_moba_diffattn_hybrid_layer"


def rms_norm(x, g):
    xf = x.astype(jnp.float32)
    y = xf * lax.rsqrt(jnp.mean(xf * xf, axis=-1, keepdims=True) + NORM_EPS)
    return (y * g.astype(jnp.float32)).astype(x.dtype)


def rope_tables(seq):
    pos = jnp.arange(seq, dtype=jnp.float32)
    inv = ROPE_THETA ** (-jnp.arange(0, ROT_DIM, 2, dtype=jnp.float32) / ROT_DIM)
    ang = pos[:, None] * inv[None, :]
    return jnp.cos(ang), jnp.sin(ang)


def apply_partial_rope(x, cos, sin):
    half = ROT_DIM // 2
    x1 = x[..., :half].astype(jnp.float32)
    x2 = x[..., half:ROT_DIM].astype(jnp.float32)
    r1 = (x1 * cos - x2 * sin).astype(x.dtype)
    r2 = (x2 * cos + x1 * sin).astype(x.dtype)
    return jnp.concatenate([r1, r2, x[..., ROT_DIM:]], axis=-1)


def moba_attention(q, k, v):
    b, h, s, dh = q.shape
    nb = -(-s // MOBA_BLOCK)
    pad = nb * MOBA_BLOCK - s
    kp = jnp.pad(k, ((0, 0), (0, 0), (0, pad), (0, 0)))
    vp = jnp.pad(v, ((0, 0), (0, 0), (0, pad), (0, 0)))
    kb = kp.reshape(b, h, nb, MOBA_BLOCK, dh)
    vb = vp.reshape(b, h, nb, MOBA_BLOCK, dh)
    scale = dh ** -0.5
    n_sel = min(MOBA_TOPK, nb - 1)
    q_blk = jnp.arange(s) // MOBA_BLOCK
    if n_sel > 0:
        k_mean = jnp.mean(kb.astype(jnp.float32), axis=3)
        gate = jnp.einsum('bhsd,bhnd->bhsn', q.astype(jnp.float32), k_mean)
        past = jnp.arange(nb)[None, :] < q_blk[:, None]
        gate = jnp.where(past, gate, -jnp.inf)
        _, sel_idx = lax.top_k(gate, n_sel)
    bi = jnp.arange(b)[:, None, None, None]
    hi = jnp.arange(h)[None, :, None, None]
    C = MOBA_Q_CHUNK

    def chunk(c):
        start = c * C
        qc = lax.dynamic_slice_in_dim(q, start, C, axis=2)
        qpos = start + jnp.arange(C)
        own = start // MOBA_BLOCK
        k_own = lax.dynamic_index_in_dim(kb, own, axis=2, keepdims=False)
        v_own = lax.dynamic_index_in_dim(vb, own, axis=2, keepdims=False)
        kpos = own * MOBA_BLOCK + jnp.arange(MOBA_BLOCK)
        s_own = jnp.einsum('bhcd,bhnd->bhcn', qc, k_own).astype(jnp.float32) * scale
        s_own = jnp.where(kpos[None, :] <= qpos[:, None], s_own, -jnp.inf)
        if n_sel > 0:
            idx = lax.dynamic_slice_in_dim(sel_idx, start, C, axis=2)
            k_sel = kb[bi, hi, idx]
            v_sel = vb[bi, hi, idx]
            s_sel = jnp.einsum('bhcd,bhcjnd->bhcjn', qc, k_sel).astype(jnp.float32) * scale
            valid = idx < (qpos // MOBA_BLOCK)[None, None, :, None]
            s_sel = jnp.where(valid[..., None], s_sel, -jnp.inf)
            scores = jnp.concatenate(
                [s_sel.reshape(b, h, C, n_sel * MOBA_BLOCK), s_own], axis=-1)
            p = jax.nn.softmax(scores, axis=-1).astype(v.dtype)
            p_sel = p[..., :n_sel * MOBA_BLOCK].reshape(b, h, C, n_sel, MOBA_BLOCK)
            p_own = p[..., n_sel * MOBA_BLOCK:]
            return (jnp.einsum('bhcjn,bhcjnd->bhcd', p_sel, v_sel)
                    + jnp.einsum('bhcn,bhnd->bhcd', p_own, v_own))
        p_own = jax.nn.softmax(s_own, axis=-1).astype(v.dtype)
        return jnp.einsum('bhcn,bhnd->bhcd', p_own, v_own)

    outs = lax.map(chunk, jnp.arange(s // C))
    return outs.transpose(1, 2, 0, 3, 4).reshape(b, h, s, dh)


def diff_attention(q, k, v, lam, subln_g, lam_init):
    b, h, _, s, dh = q.shape
    scale = dh ** -0.5
    kpos = jnp.arange(s)
    QB = DENSE_Q_BLOCK

    def qblock(i):
        start = i * QB
        qb = lax.dynamic_slice_in_dim(q, start, QB, axis=3)
        sc = jnp.einsum('bhmqd,bhmkd->bhmqk', qb, k).astype(jnp.float32) * scale
        qpos = start + jnp.arange(QB)
        sc = jnp.where(kpos[None, :] <= qpos[:, None], sc, -jnp.inf)
        p = jax.nn.softmax(sc, axis=-1)
        w = p[:, :, 0] - lam * p[:, :, 1]
        return jnp.einsum('bhqk,bhke->bhqe', w.astype(v.dtype), v)

    o = lax.map(qblock, jnp.arange(s // QB))
    o = o.transpose(1, 2, 0, 3, 4).reshape(b, h, s, 2 * dh)
    return rms_norm(o, subln_g) * (1.0 - lam_init)


def memory_cross_attention(hn, mem_n, wq, wk, wv, wo, qn_g, kn_g):
    b, s, _ = hn.shape
    m = mem_n.shape[1]
    q = rms_norm((hn @ wq).reshape(b, s, MEM_HEADS, MEM_HEAD_DIM), qn_g)
    k = rms_norm((mem_n @ wk).reshape(b, m, MEM_HEADS, MEM_HEAD_DIM), kn_g)
    v = (mem_n @ wv).reshape(b, m, MEM_HEADS, MEM_HEAD_DIM)
    sc = jnp.einsum('bshd,bmhd->bhsm', q, k).astype(jnp.float32) * (MEM_HEAD_DIM ** -0.5)
    p = jax.nn.softmax(sc, axis=-1).astype(v.dtype)
    o = jnp.einsum('bhsm,bmhd->bshd', p, v).reshape(b, s, MEM_WIDTH)
    return o @ wo


def swiglu(hn, w_gate, w_up, w_down):
    return (jax.nn.silu(hn @ w_gate) * (hn @ w_up)) @ w_down


def setup_inputs(seed: int = 0) -> dict:
    key = jax.random.key(seed)
    ks = jax.random.split(key, 32)

    def nrm(k, shape, scale):
        return jax.random.normal(k, shape, jnp.float32) * scale

    def gain(k, shape):
        return 1.0 + 0.02 * jax.random.normal(k, shape, jnp.float32)

    L = DEPTH
    return {
        "x": nrm(ks[0], (BATCH, SEQ, D_MODEL), 1.0),
        "mem": nrm(ks[1], (BATCH, MEM_LEN, D_MODEL), 1.0),
        "norm_mix_g": gain(ks[2], (L, D_MODEL)),
        "w_in": nrm(ks[3], (L, D_MODEL, IN_WIDTH), D_MODEL ** -0.5),
        "q_norm_a": gain(ks[4], (L, HEAD_DIM)),
        "k_norm_a": gain(ks[5], (L, HEAD_DIM)),
        "q_norm_b": gain(ks[6], (L, HEAD_DIM)),
        "k_norm_b": gain(ks[7], (L, HEAD_DIM)),
        "lam_q1": nrm(ks[8], (L, HEAD_DIM), 0.1),
        "lam_k1": nrm(ks[9], (L, HEAD_DIM), 0.1),
        "lam_q2": nrm(ks[10], (L, HEAD_DIM), 0.1),
        "lam_k2": nrm(ks[11], (L, HEAD_DIM), 0.1),
        "diff_subln_g": gain(ks[12], (L, 2 * HEAD_DIM)),
        "w_out": nrm(ks[13], (L, MIX_WIDTH, D_MODEL), MIX_WIDTH ** -0.5),
        "norm_cross_g": gain(ks[14], (L, D_MODEL)),
        "norm_mem_g": gain(ks[15], (L, D_MODEL)),
        "w_mq": nrm(ks[16], (L, D_MODEL, MEM_WIDTH), D_MODEL ** -0.5),
        "w_mk": nrm(ks[17], (L, D_MODEL, MEM_WIDTH), D_MODEL ** -0.5),
        "w_mv": nrm(ks[18], (L, D_MODEL, MEM_WIDTH), D_MODEL ** -0.5),
        "w_mo": nrm(ks[19], (L, MEM_WIDTH, D_MODEL), MEM_WIDTH ** -0.5),
        "q_norm_m": gain(ks[20], (L, MEM_HEAD_DIM)),
        "k_norm_m": gain(ks[21], (L, MEM_HEAD_DIM)),
        "norm_ffn_g": gain(ks[22], (L, D_MODEL)),
        "w_gate": nrm(ks[23], (L, D_MODEL, D_FF), D_MODEL ** -0.5),
        "w_up": nrm(ks[24], (L, D_MODEL, D_FF), D_MODEL ** -0.5),
        "w_down": nrm(ks[25], (L, D_FF, D_MODEL), D_FF ** -0.5),
    }


def reference(x, mem, norm_mix_g, w_in, q_norm_a, k_norm_a, q_norm_b, k_norm_b,
              lam_q1, lam_k1, lam_q2, lam_k2, diff_subln_g, w_out,
              norm_cross_g, norm_mem_g, w_mq, w_mk, w_mv, w_mo, q_norm_m, k_norm_m,
              norm_ffn_g, w_gate, w_up, w_down):
    b, s, _ = x.shape
    cos, sin = rope_tables(s)
    h = x
    for l in range(DEPTH):
        lam_init = 0.8 - 0.6 * math.exp(-0.3 * l)
        xn = rms_norm(h, norm_mix_g[l])
        proj = xn @ w_in[l]
        qa, ka, va, qb, kb, vb = jnp.split(proj, IN_SPLITS, axis=-1)
        qa = qa.reshape(b, s, MOBA_HEADS, HEAD_DIM).transpose(0, 2, 1, 3)
        ka = ka.reshape(b, s, MOBA_HEADS, HEAD_DIM).transpose(0, 2, 1, 3)
        va = va.reshape(b, s, MOBA_HEADS, HEAD_DIM).transpose(0, 2, 1, 3)
        qa = apply_partial_rope(rms_norm(qa, q_norm_a[l]), cos, sin)
        ka = apply_partial_rope(rms_norm(ka, k_norm_a[l]), cos, sin)
        out_a = moba_attention(qa, ka, va)
        out_a = out_a.transpose(0, 2, 1, 3).reshape(b, s, MOBA_WIDTH)

        qb = qb.reshape(b, s, DIFF_HEADS, 2, HEAD_DIM).transpose(0, 2, 3, 1, 4)
        kb = kb.reshape(b, s, DIFF_HEADS, 2, HEAD_DIM).transpose(0, 2, 3, 1, 4)
        vb = vb.reshape(b, s, DIFF_HEADS, 2 * HEAD_DIM).transpose(0, 2, 1, 3)
        qb = apply_partial_rope(rms_norm(qb, q_norm_b[l]), cos, sin)
        kb = apply_partial_rope(rms_norm(kb, k_norm_b[l]), cos, sin)
        lam = (jnp.exp(jnp.sum(lam_q1[l].astype(jnp.float32) * lam_k1[l].astype(jnp.float32)))
               - jnp.exp(jnp.sum(lam_q2[l].astype(jnp.float32) * lam_k2[l].astype(jnp.float32)))
               + lam_init)
        out_b = diff_attention(qb, kb, vb, lam, diff_subln_g[l], lam_init)
        out_b = out_b.transpose(0, 2, 1, 3).reshape(b, s, DIFF_V_WIDTH)

        h = h + jnp.concatenate([out_a, out_b], axis=-1) @ w_out[l]
        hn = rms_norm(h, norm_cross_g[l])
        mem_n = rms_norm(mem, norm_mem_g[l])
        h = h + memory_cross_attention(hn, mem_n, w_mq[l], w_mk[l], w_mv[l], w_mo[l],
                                       q_norm_m[l], k_norm_m[l])
        h = h + swiglu(rms_norm(h, norm_ffn_g[l]), w_gate[l], w_up[l], w_down[l])
    return h
```

```python
import math
from contextlib import ExitStack

import numpy as np

import concourse.bass as bass
import concourse.mybir as mybir
from concourse.bass_utils import run_bass_kernel_spmd

F32 = mybir.dt.float32
BF16 = mybir.dt.bfloat16
AF = mybir.ActivationFunctionType
ALU = mybir.AluOpType
AX = mybir.AxisListType

EPS = 1e-6
NEG = -30000.0
ENGS = ("pe", "act", "dve", "pool", "sp")


class Cfg:
    def __init__(self, D=4096, S=2048, DFF=11008, M=256):
        self.D, self.S, self.DFF, self.M = D, S, DFF, M
        self.KC = D // 128
        self.HA = D // 256
        self.HB = D // 512
        self.WA = self.HA * 128
        self.NQK = 2 * self.HA + 4 * self.HB
        self.NVB = (self.WA + self.HB * 256) // 512
        self.FC = DFF // 128
        self.MH = 4
        self.MW = 512
        self.NB = S // 256
        c = 0
        self.c_gmix = c; c += self.KC
        self.c_gcross = c; c += self.KC
        self.c_gmem = c; c += self.KC
        self.c_gffn = c; c += self.KC
        self.c_qa = c; c += 1
        self.c_ka = c; c += 1
        self.c_qb = c; c += 1
        self.c_kb = c; c += 1
        self.c_qm = c; c += 1
        self.c_km = c; c += 1
        self.c_sub = c; c += 2
        self.c_lam = c; c += 4
        self.NPAR = c


class DSem:
    def __init__(self, h):
        self.h = h
        self.n = 0


class Phase:
    def __init__(self, K, name):
        self.K = K
        self.nc = K.nc
        self.name = name
        self.q = {e: [] for e in ENGS}
        self.sem = K.eng_sem()
        self.cnt = K.eng_cnt
        self.waited = {e: {} for e in ENGS}
        self.dsems = {e: {} for e in ENGS}

    def op(self, eng, method, signal=False, **kw):
        if signal:
            self.cnt[eng] += 1
            self.q[eng].append(("i", method, kw, self.sem[eng], 1))
            return (self.sem[eng], self.cnt[eng])
        self.q[eng].append(("i", method, kw, None, 0))
        return None

    def wait(self, eng, tok):
        if tok is None:
            return
        sem, val = tok
        if self.waited[eng].get(sem.name, 0) >= val:
            return
        self.waited[eng][sem.name] = val
        self.q[eng].append(("w", sem, val))

    def dma(self, eng, out, in_, ds):
        ds.n += 16
        self.q[eng].append(("i", "dma_start", dict(out=out, in_=in_), ds.h, 16))
        self.dsems[eng][ds.h.name] = ds
        return (ds.h, ds.n)

    def run(self):
        nc = self.nc
        for e in ENGS:
            for ds in self.dsems[e].values():
                self.wait(e, (ds.h, ds.n))

        def replay(E, items):
            for it in items:
                if it[0] == "w":
                    E.wait_ge(it[1], it[2])
                else:
                    ins = getattr(E, it[1])(**it[2])
                    if it[3] is not None:
                        ins.then_inc(it[3], it[4])

        self.K.release_dsems()
        q = self.q
        with nc.Block() as block:
            @block.tensor
            def _(E):
                replay(E, q["pe"])

            @block.scalar
            def _(E):
                replay(E, q["act"])

            @block.vector
            def _(E):
                replay(E, q["dve"])

            @block.gpsimd
            def _(E):
                replay(E, q["pool"])

            @block.sync
            def _(E):
                replay(E, q["sp"])


class Hist:
    def __init__(self):
        self.d = {}

    def __setitem__(self, k, v):
        self.d[k] = v

    def __getitem__(self, k):
        return self.d.get(k)


class Kern:
    def __init__(self, cfg, debug=False, upto=99):
        self.cfg = cfg
        self.debug = debug
        self.upto = upto
        self.nc = bass.Bass("TRN2", target_bir_lowering=False)
        self.es = ExitStack()
        self.nsem = 0

    def new_sem(self, name):
        self.nsem += 1
        return self.es.enter_context(self.nc.semaphore(f"s{self.nsem}_{name}"))

    def eng_sem(self):
        if not hasattr(self, "_eng_sem"):
            self._eng_sem = {e: self.new_sem(f"eng_{e}") for e in ENGS}
            self.eng_cnt = {e: 0 for e in ENGS}
        return self._eng_sem

    def dsem(self, name):
        if not hasattr(self, "_free_ds"):
            self._free_ds, self._used_ds = [], []
        d = self._free_ds.pop() if self._free_ds else DSem(self.new_sem("dma"))
        self._used_ds.append(d)
        return d

    def release_dsems(self):
        if hasattr(self, "_free_ds"):
            self._free_ds.extend(self._used_ds)
            self._used_ds = []

    def dram_in(self, name, shape, dt=F32):
        return self.nc.dram_tensor(name, list(shape), dt, kind="ExternalInput").ap()

    def dram_scratch(self, name, shape, dt):
        kind = "ExternalOutput" if self.debug else "Internal"
        return self.nc.dram_tensor(name, list(shape), dt, kind=kind).ap()

    def sb(self, name, shape, dt):
        return self.es.enter_context(self.nc.sbuf_tensor(name, list(shape), dt))

    def A(self, k, a, b):
        S = self.cfg.S
        return self.act[:, k * S + a:k * S + b]

    def build(self):
        c = self.cfg
        nc = self.nc
        D, S, KC = c.D, c.S, c.KC
        with self.es:
            self.xT = self.dram_in("xT", [D, S])
            self.memT = self.dram_in("memT", [D, c.M])
            self.par_d = self.dram_in("par", [128, c.NPAR])
            self.rope_d = self.dram_in("rope", [128, 2 * S])
            self.cst_d = self.dram_in("cst", [128, 128 * 4 + 64])
            self.wqk = self.dram_in("wqk", [c.NQK, 128, KC * 128])
            self.wv = self.dram_in("wv", [c.NVB, 128, KC * 512])
            self.wo = self.dram_in("wo", [KC, 128, KC * 128])
            self.wmq = self.dram_in("wmq", [4, 128, KC * 128])
            self.wmk = self.dram_in("wmk", [4, 128, KC * 128])
            self.wmv = self.dram_in("wmv", [1, 128, KC * 512])
            self.wmo = self.dram_in("wmo", [128, 4 * D])
            self.wg = self.dram_in("wg", [c.FC, 128, KC * 128])
            self.wu = self.dram_in("wu", [c.FC, 128, KC * 128])
            self.wd = self.dram_in("wd", [D // 512, 128, c.FC * 512])
            self.yT = nc.dram_tensor("yT", [D, S], F32, kind="ExternalOutput").ap()

            self.qkT = self.dram_scratch("qkT", [c.NQK, 128, S], BF16)
            self.vtok = self.dram_scratch("vtok", [S, c.NVB * 512], BF16)
            self.attnT = self.dram_scratch("attnT", [D, S], BF16)
            self.h1T = self.dram_scratch("h1T", [D, S], F32)
            self.h2T = self.dram_scratch("h2T", [D, S], F32)
            self.actT = self.dram_scratch("actT", [c.DFF, S], BF16)
            self.qaf = self.dram_scratch("qaf", [c.HA, 128, S // 2], F32)
            self.rstd1 = self.dram_scratch("rstd1", [128, S], F32)
            self.rstd2 = self.dram_scratch("rstd2", [128, S], F32)

            self.act_elems = max(KC * S, c.FC * 512 + 16 * (S // 2), 4 * S + 4 * D, 32768)
            self.act = self.sb("act", [128, self.act_elems], BF16)
            self.par = self.sb("par_sb", [128, c.NPAR], F32)
            self.identf = self.sb("identf", [128, 128], F32)
            self.RT = self.sb("RT", [128, 128], F32)
            self.trif = self.sb("trif", [128, 128], F32)
            self.cmask = self.sb("cmask", [128, 64], F32)
            self.identb = self.sb("identb", [128, 128], BF16)
            self.trib = self.sb("trib", [128, 128], BF16)
            self.onesb = self.sb("onesb", [128, 128], BF16)
            self.onesf = self.sb("onesf", [128, 128], F32)
            self.selb = self.sb("selb", [128, 8 * 128], BF16)
            self.kmt = self.sb("kmt", [128, c.HA * 8], F32)
            self.kmb = self.sb("kmb", [128, c.HA * 8], BF16)
            self.lamc = self.sb("lamc", [128, 8], F32)
            self.es_early = ExitStack()
            self.rope = self.es_early.enter_context(nc.sbuf_tensor("rope_sb", [128, 2 * S], F32))
            self.RTb = self.es_early.enter_context(nc.sbuf_tensor("RTb", [128, 128], BF16))

            self.phase0()
            if self.upto >= 1:
                self.norm_phase("p1", self.xT, S, c.c_gmix)
            if self.upto >= 2:
                self.qk_phase("p2a", self.wqk, c.NQK, self.qkT, self.slot_kinds_main())
                self.es_early.close()
                self.v_phase("p2b", RW=2)
            if self.upto >= 3:
                self.moba_phase("p3")
            if self.upto >= 4:
                self.diff_phase("p4")
            if self.upto < 2:
                self.es_early.close()
            if self.upto >= 5:
                self.proj_res_phase("p5", self.wo, KC, KC, lambda k, a, b: self.A(k, a, b), self.xT, self.h1T,
                                    rstd_out=self.rstd1, preload=(self.attnT, KC))
            if self.upto >= 6:
                self.cross_phase()
            if self.upto >= 7:
                self.norm_phase("p7a", self.h2T, S, c.c_gffn, rstd_src=self.rstd2)
                self.ffn_up_phase("p7")
            if self.upto >= 8:
                self.ffn_down_phase("p8")
            self.final_phase()
        return nc

    def phase0(self):
        c = self.cfg
        P = Phase(self, "p0")
        d = self.dsem("p0ld")
        P.dma("sp", self.par[:, :], self.par_d[:, :], d)
        P.dma("sp", self.rope[:, :], self.rope_d[:, :], d)
        P.dma("sp", self.identf[:, :], self.cst_d[:, 0:128], d)
        P.dma("sp", self.RT[:, :], self.cst_d[:, 128:256], d)
        P.dma("sp", self.trif[:, :], self.cst_d[:, 256:384], d)
        tk = P.dma("sp", self.cmask[:, :], self.cst_d[:, 512:576], d)
        P.wait("act", tk)
        P.op("act", "activation", out=self.RTb[:, :], in_=self.RT[:, :], func=AF.Copy)
        P.wait("dve", tk)
        P.op("dve", "memset", ap=self.onesb[:, :], constant=1.0)
        P.op("dve", "memset", ap=self.onesf[:, :], constant=1.0)
        P.op("dve", "memset", ap=self.selb[:, :], constant=0.0)
        P.op("dve", "memset", ap=self.kmt[:, :], constant=0.0)
        P.op("dve", "tensor_copy", out=self.identb[:, :], in_=self.identf[:, :])
        t0 = P.op("dve", "tensor_copy", out=self.trib[:, :], in_=self.trif[:, :], signal=True)
        P.wait("dve", t0)
        for j in range(8):
            P.op("dve", "tensor_scalar", out=self.selb[:, j * 128:(j + 1) * 128],
                 in0=self.onesb[:, :], scalar1=self.identf[:, j:j + 1], scalar2=None,
                 op0=ALU.mult)
        cl = c.c_lam
        P.op("dve", "tensor_tensor", out=self.lamc[:, 0:1], in0=self.par[:, cl:cl + 1],
             in1=self.par[:, cl + 1:cl + 2], op=ALU.mult)
        t1 = P.op("dve", "tensor_tensor", out=self.lamc[:, 1:2], in0=self.par[:, cl + 2:cl + 3],
                  in1=self.par[:, cl + 3:cl + 4], op=ALU.mult, signal=True)
        with self.nc.psum_tensor("ps0", [128, 512], F32) as ps0:
            P.wait("pe", t1)
            t2 = P.op("pe", "matmul", out=ps0[:, 0:2], lhsT=self.onesf[:, :], rhs=self.lamc[:, 0:2],
                      start=True, stop=True, signal=True)
            P.wait("act", t2)
            t3 = P.op("act", "activation", out=self.lamc[:, 2:4], in_=ps0[:, 0:2], func=AF.Exp,
                      signal=True)
            P.wait("dve", t3)
            t4 = P.op("dve", "tensor_tensor", out=self.lamc[:, 4:5], in0=self.lamc[:, 3:4],
                      in1=self.lamc[:, 2:3], op=ALU.subtract, signal=True)
            P.wait("dve", t4)
            t5 = P.op("dve", "tensor_scalar", out=self.lamc[:, 5:6], in0=self.lamc[:, 4:5],
                      scalar1=-self.lam_init(), scalar2=None, op0=ALU.add, signal=True)
            P.op("dve", "tensor_scalar", out=self.par[:, c.c_sub:c.c_sub + 2],
                 in0=self.par[:, c.c_sub:c.c_sub + 2], scalar1=1.0 - self.lam_init(), scalar2=None,
                 op0=ALU.mult)
            P.run()

    @staticmethod
    def lam_init():
        return 0.8 - 0.6 * math.exp(-0.3 * 0)

    def norm_phase(self, name, src, S, gcol, dst_fn=None, rstd_src=None):
        c = self.cfg
        KC, D = c.KC, c.D
        nc = self.nc
        dst_fn = (lambda k: self.A(k, 0, S)) if dst_fn is None else dst_fn
        P = Phase(self, name)
        R = 3
        NW = 512 if S >= 512 else S
        NT = S // NW
        with ExitStack() as es:
            xs = es.enter_context(nc.sbuf_tensor(f"{name}_xs", [128, R, S], F32))
            sq = es.enter_context(nc.sbuf_tensor(f"{name}_sq", [128, 2, S], BF16))
            sd = es.enter_context(nc.sbuf_tensor(f"{name}_sd", [128, S], F32))
            rstd = es.enter_context(nc.sbuf_tensor(f"{name}_rstd", [128, S], F32))
            ps = es.enter_context(nc.psum_tensor(f"{name}_ps", [128, max(S, 512)], F32))
            lds = [self.dsem(f"{name}_ld{i}") for i in range(R)]
            sqt, mmt, use = Hist(), Hist(), Hist()
            n = 0
            if rstd_src is not None:
                t2 = P.dma("sp", rstd[:, :], rstd_src[:, :], self.dsem(f"{name}_rl"))
            else:
                for k in range(KC):
                    slot = n % R
                    P.wait("sp", use[n - R])
                    ld = P.dma("sp", xs[:, slot, :], src[k * 128:(k + 1) * 128, :], lds[slot])
                    P.wait("act", ld)
                    P.wait("act", mmt[k - 2])
                    sqt[k] = P.op("act", "activation", out=sq[:, k % 2, :], in_=xs[:, slot, :],
                                  func=AF.Square, signal=True)
                    use[n] = sqt[k]
                    P.wait("pe", sqt[k])
                    for t in range(NT):
                        tk = P.op("pe", "matmul", out=ps[:, t * NW:(t + 1) * NW], lhsT=self.onesb[:, :],
                                  rhs=sq[:, k % 2, t * NW:(t + 1) * NW], start=(k == 0),
                                  stop=(k == KC - 1), signal=(t == NT - 1))
                    mmt[k] = tk
                    n += 1
                P.wait("act", mmt[KC - 1])
                t1 = P.op("act", "activation", out=sd[:, :], in_=ps[:, 0:S], func=AF.Sqrt, bias=EPS,
                          scale=1.0 / D, signal=True)
                P.wait("dve", t1)
                t2 = P.op("dve", "reciprocal", out=rstd[:, :], in_=sd[:, :], signal=True)
            for k in range(KC):
                slot = n % R
                P.wait("sp", use[n - R])
                ld = P.dma("sp", xs[:, slot, :], src[k * 128:(k + 1) * 128, :], lds[slot])
                P.wait("dve", ld)
                P.wait("dve", t2)
                use[n] = P.op("dve", "scalar_tensor_tensor", out=dst_fn(k), in0=xs[:, slot, :],
                              scalar=self.par[:, gcol + k:gcol + k + 1], in1=rstd[:, :],
                              op0=ALU.mult, op1=ALU.mult, signal=True)
                n += 1
            P.run()

    def slot_kinds_main(self):
        c = self.cfg
        kinds = []
        for h in range(c.HA):
            kinds.append((c.c_qa, True, None, h))
        for h in range(c.HA):
            kinds.append((c.c_ka, True, h, None))
        for h in range(2 * c.HB):
            kinds.append((c.c_qb, True, None, None))
        for h in range(2 * c.HB):
            kinds.append((c.c_kb, True, None, None))
        return kinds

    def qk_phase(self, name, wdram, nslots, dst, kinds, S=None, rhs_fn=None, dst_sb=None, RW=3):
        c = self.cfg
        nc = self.nc
        KC = c.KC
        S = c.S if S is None else S
        rhs_fn = self.A if rhs_fn is None else rhs_fn
        TW = min(1024, S)
        NH = S // TW
        NW = min(512, TW)
        NTT = TW // NW
        NU = nslots * NH
        P = Phase(self, name)
        with ExitStack() as es:
            w = es.enter_context(nc.sbuf_tensor(f"{name}_w", [128, RW, KC * 128], BF16))
            sqb = es.enter_context(nc.sbuf_tensor(f"{name}_sqb", [128, 2, TW], BF16))
            sd = es.enter_context(nc.sbuf_tensor(f"{name}_sd", [128, TW], F32))
            rs = es.enter_context(nc.sbuf_tensor(f"{name}_rs", [128, TW], F32))
            qn = es.enter_context(nc.sbuf_tensor(f"{name}_qn", [128, 2, TW], F32))
            t2b = es.enter_context(nc.sbuf_tensor(f"{name}_t2", [128, TW], F32))
            qkb = es.enter_context(nc.sbuf_tensor(f"{name}_qkb", [128, 2, TW], BF16))
            any_rope = any(k_[1] for k_ in kinds)
            if any_rope:
                hib = es.enter_context(nc.sbuf_tensor(f"{name}_hib", [128, TW], BF16))
                lob = es.enter_context(nc.sbuf_tensor(f"{name}_lob", [128, TW], BF16))
            ps = es.enter_context(nc.psum_tensor(f"{name}_ps", [128, 4096], F32))
            psP = lambda g: ps[:, g * 1024:g * 1024 + TW]
            psS = ps[:, 2048:2048 + TW]
            psR = ps[:, 3072:3072 + TW]
            wld = [self.dsem(f"{name}_w{i}") for i in range(RW)]
            std = [self.dsem(f"{name}_st{i}") for i in range(2)]
            A, B1, C, D1, D2, E, Fm, G, H, ST, WL, GL, STF, HI, LO = (Hist() for _ in range(15))
            ZT = None
            if any_rope:
                P.op("dve", "memset", ap=hib[:, :], constant=0.0)
                ZT = P.op("dve", "memset", ap=lob[:, :], constant=0.0, signal=True)
            stf = [self.dsem(f"{name}_stf{i}") for i in range(2)]
            KH = KC // 2 if KC >= 2 else KC

            def emitA(u, k0, k1, last):
                hs = u // NH
                tb = (u % NH) * TW
                slot = hs % RW
                g = u % 2
                tk = None
                for k in range(k0, k1):
                    for t in range(NTT):
                        tk = P.op("pe", "matmul", out=ps[:, g * 1024 + t * NW:g * 1024 + (t + 1) * NW],
                                  lhsT=w[:, slot, k * 128:(k + 1) * 128],
                                  rhs=rhs_fn(k, tb + t * NW, tb + (t + 1) * NW),
                                  start=(k == 0), stop=(k == KC - 1),
                                  signal=(last and k == k1 - 1 and t == NTT - 1))
                return tk

            for step in range(NU + 3):
                u = step
                if u < NU and u % NH == 0:
                    hs = u // NH
                    slot = hs % RW
                    P.wait("pool", A[(hs - RW) * NH + NH - 1])
                    WL[hs] = P.dma("pool", w[:, slot, :], wdram[hs], wld[slot])
                v = u - 2
                if 0 <= v < NU and kinds[v // NH][1]:
                    P.wait("pe", LO[v])
                    P.wait("pe", ZT)
                    P.wait("pe", G[v - 1])
                    for t in range(NTT):
                        P.op("pe", "matmul", out=ps[:, 3072 + t * NW:3072 + (t + 1) * NW],
                             lhsT=self.RTb[:, :], rhs=hib[:, t * NW:(t + 1) * NW], start=True, stop=False)
                        Fm[v] = P.op("pe", "matmul", out=ps[:, 3072 + t * NW:3072 + (t + 1) * NW],
                                     lhsT=self.RTb[:, :], rhs=lob[:, t * NW:(t + 1) * NW],
                                     start=False, stop=True, signal=(t == NTT - 1))
                if u < NU:
                    P.wait("pe", WL[u // NH])
                    P.wait("pe", E[u - 2])
                    emitA(u, 0, KH, KH == KC)
                    if KH == KC:
                        A[u] = (P.sem["pe"], P.cnt["pe"])
                v = u - 1
                if 0 <= v < NU:
                    P.wait("pe", B1[v])
                    P.wait("pe", D1[v - 1])
                    for t in range(NTT):
                        C[v] = P.op("pe", "matmul", out=ps[:, 2048 + t * NW:2048 + (t + 1) * NW],
                                    lhsT=self.onesb[:, :], rhs=sqb[:, v % 2, t * NW:(t + 1) * NW],
                                    start=True, stop=True, signal=(t == NTT - 1))
                if u < NU and KH < KC:
                    A[u] = emitA(u, KH, KC, True)
                v = u - 2
                if 0 <= v < NU:
                    gc, rope, kmh, qfh = kinds[v // NH]
                    tb = (v % NH) * TW
                    last = E[v]
                    if rope:
                        P.wait("dve", Fm[v])
                        P.wait("dve", E[v])
                        g1 = P.op("dve", "tensor_tensor", out=qn[0:32, v % 2, :], in0=qn[0:32, v % 2, :],
                                  in1=self.rope[0:32, tb:tb + TW], op=ALU.mult, signal=True)
                        g2 = P.op("dve", "tensor_tensor", out=t2b[0:32, :], in0=psR[0:32, :],
                                  in1=self.rope[0:32, c.S + tb:c.S + tb + TW], op=ALU.mult,
                                  signal=True)
                        P.wait("dve", g2)
                        last = P.op("dve", "tensor_tensor", out=qn[0:32, v % 2, :],
                                    in0=qn[0:32, v % 2, :], in1=t2b[0:32, :], op=ALU.add, signal=True)
                        G[v] = last
                    if kmh is not None:
                        P.wait("dve", last)
                        nb = TW // 256
                        b0 = kmh * 8 + tb // 256
                        last = P.op("dve", "tensor_reduce", out=self.kmt[:, b0:b0 + nb],
                                    in_=qn[:, v % 2, :].rearrange("p (b n) -> p b n", n=256),
                                    axis=AX.X, op=ALU.add, signal=True)
                        G[v] = last
                    GL[v] = last
                    if qfh is not None and tb >= c.S // 2 and c.NB > 4:
                        P.wait("sp", last)
                        STF[v] = P.dma("sp", self.qaf[qfh][:, tb - c.S // 2:tb - c.S // 2 + TW], qn[:, v % 2, :], stf[v % 2])
                v = u - 1
                if 0 <= v < NU:
                    P.wait("act", C[v])
                    P.wait("act", E[v - 1])
                    D1[v] = P.op("act", "activation", out=sd[:, :], in_=psS, func=AF.Ln, bias=EPS,
                                 scale=1.0 / 128, signal=True)
                    P.wait("act", D1[v])
                    D2[v] = P.op("act", "activation", out=rs[:, :], in_=sd[:, :], func=AF.Exp, scale=-0.5,
                                 signal=True)
                    P.wait("dve", D2[v])
                    P.wait("dve", A[v])
                    P.wait("dve", H[v - 2])
                    P.wait("dve", Fm[v - 2])
                    P.wait("dve", STF[v - 2])
                    gc = kinds[v // NH][0]
                    E[v] = P.op("dve", "scalar_tensor_tensor", out=qn[:, v % 2, :], in0=psP(v % 2),
                                scalar=self.par[:, gc:gc + 1], in1=rs[:, :], op0=ALU.mult,
                                op1=ALU.mult, signal=True)
                    if kinds[v // NH][1]:
                        P.wait("act", E[v])
                        P.wait("act", Fm[v - 1])
                        HI[v] = P.op("act", "activation", out=hib[0:32, :], in_=qn[0:32, v % 2, :],
                                     func=AF.Copy, signal=True)
                        P.wait("dve", HI[v])
                        P.wait("dve", Fm[v - 1])
                        LO[v] = P.op("dve", "tensor_tensor", out=lob[0:32, :], in0=qn[0:32, v % 2, :],
                                     in1=hib[0:32, :], op=ALU.subtract, signal=True)
                v = u - 2
                if 0 <= v < NU:
                    tb = (v % NH) * TW
                    P.wait("act", GL[v])
                    P.wait("act", ST[v - 2])
                    if dst_sb is not None:
                        H[v] = P.op("act", "activation", out=dst_sb(v // NH, tb, tb + TW), in_=qn[:, v % 2, :],
                                    func=AF.Copy, signal=True)
                    else:
                        H[v] = P.op("act", "activation", out=qkb[:, v % 2, :], in_=qn[:, v % 2, :],
                                    func=AF.Copy, signal=True)
                        P.wait("sp", H[v])
                        ST[v] = P.dma("sp", dst[v // NH][:, tb:tb + TW], qkb[:, v % 2, :], std[v % 2])
                if u < NU:
                    P.wait("act", A[u])
                    P.wait("act", C[u - 2])
                    B1[u] = P.op("act", "activation", out=sqb[:, u % 2, :], in_=psP(u % 2),
                                 func=AF.Square, signal=True)
            P.run()

    def v_phase(self, name, wdram=None, ncb=None, S=None, lhs_fn=None, dst_sb=None, RW=1):
        c = self.cfg
        nc = self.nc
        KC = c.KC
        S = c.S if S is None else S
        wdram = self.wv if wdram is None else wdram
        ncb = c.NVB if ncb is None else ncb
        lhs_fn = self.A if lhs_fn is None else lhs_fn
        P = Phase(self, name)
        NTK = S // 128
        with ExitStack() as es:
            w = es.enter_context(nc.sbuf_tensor(f"{name}_w", [128, RW, KC * 512], BF16))
            vsb = es.enter_context(nc.sbuf_tensor(f"{name}_vsb", [128, 4, 512], BF16))
            ps = es.enter_context(nc.psum_tensor(f"{name}_ps", [128, 4096], F32))
            wld = [self.dsem(f"{name}_w{i}") for i in range(RW)]
            std = [self.dsem(f"{name}_st{i}") for i in range(4)]
            MM, EV, ST, CBEND, WLD = Hist(), Hist(), Hist(), Hist(), Hist()
            n = 0
            for cb in range(min(RW, ncb)):
                WLD[cb] = P.dma("pool", w[:, cb % RW, :], wdram[cb], wld[cb % RW])
            for cb in range(ncb):
                ws = cb % RW
                P.wait("pe", WLD[cb])
                for t in range(NTK):
                    b = n % 8
                    P.wait("pe", EV[n - 8])
                    for k in range(KC):
                        tk = P.op("pe", "matmul", out=ps[:, b * 512:(b + 1) * 512],
                                  lhsT=lhs_fn(k, t * 128, (t + 1) * 128),
                                  rhs=w[:, ws, k * 512:(k + 1) * 512], start=(k == 0), stop=(k == KC - 1),
                                  signal=(k == KC - 1))
                    MM[n] = tk
                    eng = "act" if n % 2 == 0 else "dve"
                    P.wait(eng, MM[n])
                    P.wait(eng, ST[n - 4])
                    o = vsb[:, n % 4, :] if dst_sb is None else dst_sb(t)
                    if eng == "act":
                        EV[n] = P.op("act", "activation", out=o, in_=ps[:, b * 512:(b + 1) * 512],
                                     func=AF.Copy, signal=True)
                    else:
                        EV[n] = P.op("dve", "tensor_copy", out=o, in_=ps[:, b * 512:(b + 1) * 512],
                                     signal=True)
                    if dst_sb is None:
                        P.wait("sp", EV[n])
                        ST[n] = P.dma("sp", self.vtok[t * 128:(t + 1) * 128, cb * 512:(cb + 1) * 512],
                                      vsb[:, n % 4, :], std[n % 4])
                    n += 1
                CBEND[cb] = MM[n - 1]
                if cb + RW < ncb:
                    P.wait("pool", CBEND[cb])
                    WLD[cb + RW] = P.dma("pool", w[:, ws, :], wdram[cb + RW], wld[ws])
            P.run()


    def attn_tiles(self):
        S = self.cfg.S
        out = []
        for qc in range(S // 512):
            for kt in range(4 * qc + 4):
                q0 = max(512 * qc, 128 * kt)
                out.append((qc, kt, q0, 512 * (qc + 1) - q0))
        return out

    def moba_phase(self, name):
        c = self.cfg
        nc = self.nc
        S, HA = c.S, c.HA
        P = Phase(self, name)
        scale = 128 ** -0.5
        vt = self.vtok.rearrange("(t p) c -> p t c", p=128)
        NT16 = S // 128
        SEL = (c.NB > 4)
        with ExitStack() as es:
            A_ = self.act
            qs = A_[:, 0:3 * S].rearrange("p (a s) -> p a s", a=3)
            ks = A_[:, 3 * S:6 * S].rearrange("p (a s) -> p a s", a=3)
            v4 = A_[:, 6 * S:6 * S + 2 * NT16 * 512].rearrange("p (a t c) -> p a t c", a=2, t=NT16)
            e0 = 6 * S + 2 * NT16 * 512
            eb = A_[:, e0:e0 + 2048].rearrange("p (a s) -> p a s", a=4)
            gm = es.enter_context(nc.sbuf_tensor(f"{name}_gm", [128, 2, 64], F32))
            top = es.enter_context(nc.sbuf_tensor(f"{name}_top", [128, 2, 64], F32))
            negm = es.enter_context(nc.sbuf_tensor(f"{name}_negm", [128, 2, 8, 128], F32))
            negmT = es.enter_context(nc.sbuf_tensor(f"{name}_negmT", [128, 2, 1024], BF16))
            rden = es.enter_context(nc.sbuf_tensor(f"{name}_rden", [128, 2, 512], F32))
            osb = es.enter_context(nc.sbuf_tensor(f"{name}_osb", [128, 2, 512], BF16))
            qf = es.enter_context(nc.sbuf_tensor(f"{name}_qf", [128, 2, S // 2], F32))
            ps = es.enter_context(nc.psum_tensor(f"{name}_ps", [128, 4096], F32))
            qfl = [self.dsem(f"{name}_qf{i}") for i in range(2)]
            QFL, GT = Hist(), Hist()
            qld = [self.dsem(f"{name}_q{i}") for i in range(3)]
            vld = [self.dsem(f"{name}_v{i}") for i in range(2)]
            std = [self.dsem(f"{name}_st{i}") for i in range(2)]
            t0 = P.op("dve", "memset", ap=negm[:, :, :, :], constant=0.0)
            t0 = P.op("dve", "tensor_copy", out=self.kmb[:, :], in_=self.kmt[:, :], signal=True)
            LD, VL, HEADEND, SEL1, SEL2, NORM, ST = (Hist() for _ in range(7))
            RING, EXPT, PVT = Hist(), Hist(), Hist()
            st = dict(si=0, ei=0, qci=0, vg=-1, vn=0)
            tiles = self.attn_tiles()

            def loads(h):
                if h >= HA:
                    return
                P.wait("sp", HEADEND[h - 3])
                P.dma("sp", qs[:, h % 3, :], self.qkT[h], qld[h % 3])
                LD[h] = P.dma("sp", ks[:, h % 3, :], self.qkT[HA + h], qld[h % 3])
                if SEL:
                    P.wait("sp", GT[h - 2])
                    QFL[h] = P.dma("sp", qf[:, h % 2, :], self.qaf[h], qfl[h % 2])
                g = (h * 128) // 512
                if g != st["vg"]:
                    st["vg"] = g
                    st["vn"] += 1
                    vs = st["vn"] % 2
                    P.wait("sp", HEADEND[(g - 1) * 4 - 1] if g >= 2 else None)
                    VL[g] = P.dma("sp", v4[:, vs, :, :], vt[:, :, g * 512:(g + 1) * 512], vld[vs])
                    st[("vs", g)] = vs

            def sel1(h):
                if h >= HA or not SEL:
                    return
                P.wait("pe", QFL[h])
                P.wait("pe", SEL2[h - 1])
                for i in range(8):
                    tk = P.op("pe", "matmul", out=ps[:, 3072 + i * 8:3072 + (i + 1) * 8],
                              lhsT=qf[:, h % 2, i * 128:(i + 1) * 128], rhs=self.kmt[:, h * 8:(h + 1) * 8],
                              start=True, stop=True, signal=(i == 7))
                GT[h] = tk
                P.wait("dve", tk)
                P.wait("dve", SEL2[h - 2])
                a = P.op("dve", "tensor_tensor", out=gm[:, h % 2, :], in0=ps[:, 3072:3072 + 64],
                         in1=self.cmask[:, :], op=ALU.add, signal=True)
                P.wait("dve", a)
                for i in range(8):
                    b = P.op("dve", "max", out=top[:, h % 2, i * 8:(i + 1) * 8], in_=gm[:, h % 2, i * 8:(i + 1) * 8],
                             signal=(i == 7))
                P.wait("dve", b)
                for i in range(8):
                    d = P.op("dve", "tensor_scalar", out=negm[:, h % 2, i, 0:8], in0=gm[:, h % 2, i * 8:(i + 1) * 8],
                             scalar1=top[:, h % 2, i * 8 + 3:i * 8 + 4], scalar2=NEG, op0=ALU.is_lt,
                             op1=ALU.mult, signal=(i == 7))
                SEL1[h] = d

            def sel2(h):
                if h >= HA or not SEL:
                    return
                P.wait("pe", SEL1[h])
                for i in range(8):
                    tk = P.op("pe", "transpose", out=ps[:, 3072 + i * 128:3072 + (i + 1) * 128],
                              in_=negm[:, h % 2, i, :], identity=self.identf[:, :], signal=(i == 7))
                P.wait("act", tk)
                P.wait("act", HEADEND[h - 2])
                SEL2[h] = P.op("act", "activation", out=negmT[:, h % 2, :], in_=ps[:, 3072:4096],
                               func=AF.Copy, signal=True)

            def emitS(h, tile):
                qc, kt, q0, wq = tile
                si = st["si"]
                st["si"] += 1
                bank = 4 + si % 2
                P.wait("pe", RING[si - 2])
                P.wait("pe", LD[h])
                need_sel = SEL and qc >= 2 and kt < 4 * qc + 2
                diag = kt >= 4 * qc
                if need_sel:
                    P.wait("pe", SEL2[h])
                o = ps[:, bank * 512:bank * 512 + wq]
                tk = P.op("pe", "matmul", out=o, lhsT=ks[:, h % 3, kt * 128:(kt + 1) * 128],
                          rhs=qs[:, h % 3, q0:q0 + wq], start=True, stop=not (need_sel or diag),
                          signal=not (need_sel or diag))
                if need_sel:
                    j = kt // 2
                    tk = P.op("pe", "matmul", out=o, lhsT=self.selb[:, j * 128:(j + 1) * 128],
                              rhs=negmT[:, h % 2, q0 - 1024:q0 - 1024 + wq], start=False, stop=not diag,
                              signal=not diag)
                if diag:
                    tk = P.op("pe", "matmul", out=ps[:, bank * 512:bank * 512 + 128], lhsT=self.identb[:, :],
                              rhs=self.trib[:, :], start=False, stop=True, signal=True)
                return si, tk

            def emitExp(si, stok, wq):
                ei = st["ei"]
                st["ei"] += 1
                bank = 4 + si % 2
                P.wait("act", stok)
                P.wait("act", PVT[ei - 4])
                EXPT[ei] = P.op("act", "activation", out=eb[:, ei % 4, 0:wq], in_=ps[:, bank * 512:bank * 512 + wq],
                                func=AF.Exp, scale=scale, signal=True)
                RING[si] = EXPT[ei]
                return ei

            def emitPV(h, tile, ei):
                qc, kt, q0, wq = tile
                qci = st["qci"]
                ob = qci % 2
                g = (h * 128) // 512
                vs = st[("vs", g)]
                vc = (h * 128) % 512
                P.wait("pe", EXPT[ei])
                P.wait("pe", VL[g])
                if kt == 0:
                    P.wait("pe", NORM[qci - 2])
                c0 = q0 - 512 * qc
                last = (kt == 4 * qc + 3)
                P.op("pe", "matmul", out=ps[:, ob * 512 + c0:ob * 512 + c0 + wq],
                     lhsT=v4[:, vs, kt, vc:vc + 128], rhs=eb[:, ei % 4, 0:wq], start=(kt == 0), stop=last)
                PVT[ei] = P.op("pe", "matmul", out=ps[:, (2 + ob) * 512 + c0:(2 + ob) * 512 + c0 + wq],
                               lhsT=self.onesb[:, :], rhs=eb[:, ei % 4, 0:wq], start=(kt == 0), stop=last,
                               signal=True)
                if last:
                    P.wait("dve", PVT[ei])
                    P.wait("dve", NORM[qci - 2])
                    a = P.op("dve", "reciprocal", out=rden[:, ob, :], in_=ps[:, (2 + ob) * 512:(3 + ob) * 512],
                             signal=True)
                    P.wait("dve", a)
                    P.wait("dve", ST[qci - 2])
                    NORM[qci] = P.op("dve", "tensor_tensor", out=osb[:, ob, :], in0=ps[:, ob * 512:(ob + 1) * 512],
                                     in1=rden[:, ob, :], op=ALU.mult, signal=True)
                    P.wait("sp", NORM[qci])
                    ST[qci] = P.dma("sp", self.attnT[h * 128:(h + 1) * 128, qc * 512:(qc + 1) * 512],
                                    osb[:, ob, :], std[ob])
                    st["qci"] += 1

            loads(0)
            loads(1)
            sel1(0)
            sel2(0)
            for h in range(HA):
                loads(h + 2)
                sel1(h + 1)
                nt = len(tiles)
                pend = emitS(h, tiles[0])
                for i in range(nt):
                    si, stok = pend
                    ei = emitExp(si, stok, tiles[i][3])
                    if i + 1 < nt:
                        pend = emitS(h, tiles[i + 1])
                    emitPV(h, tiles[i], ei)
                    if i == 11:
                        sel2(h + 1)
                HEADEND[h] = PVT[st["ei"] - 1]
            P.run()

    def diff_phase(self, name):
        c = self.cfg
        nc = self.nc
        S, HA, HB, WA = c.S, c.HA, c.HB, c.WA
        P = Phase(self, name)
        scale = 128 ** -0.5
        vt = self.vtok.rearrange("(t p) c -> p t c", p=128)
        NT16 = S // 128
        q_base = 2 * HA
        k_base = 2 * HA + 2 * HB
        with ExitStack() as es:
            A_ = self.act
            qs = A_[:, 0:3 * S].rearrange("p (a s) -> p a s", a=3)
            ks = A_[:, 3 * S:6 * S].rearrange("p (a s) -> p a s", a=3)
            v2 = A_[:, 6 * S:6 * S + 2 * NT16 * 256].rearrange("p (a t c) -> p a t c", a=2, t=NT16)
            e0 = 6 * S + 2 * NT16 * 256
            eb = A_[:, e0:e0 + 2048].rearrange("p (a s) -> p a s", a=4)
            on0 = es.enter_context(nc.sbuf_tensor(f"{name}_on0", [128, 2, S], F32))
            rden = es.enter_context(nc.sbuf_tensor(f"{name}_rden", [128, 2, 512], F32))
            o1 = es.enter_context(nc.sbuf_tensor(f"{name}_o1", [128, 2, 2, 512], F32))
            comb = es.enter_context(nc.sbuf_tensor(f"{name}_comb", [128, 2, 2, 512], F32))
            sqb = es.enter_context(nc.sbuf_tensor(f"{name}_sqb", [128, 2, 2, 512], BF16))
            sd = es.enter_context(nc.sbuf_tensor(f"{name}_sd", [128, 512], F32))
            rs = es.enter_context(nc.sbuf_tensor(f"{name}_rs", [128, 512], F32))
            osb = es.enter_context(nc.sbuf_tensor(f"{name}_osb", [128, 2, 2, 512], BF16))
            ps = es.enter_context(nc.psum_tensor(f"{name}_ps", [128, 4096], F32))
            qld = [self.dsem(f"{name}_q{i}") for i in range(3)]
            vld = [self.dsem(f"{name}_v{i}") for i in range(2)]
            std = [self.dsem(f"{name}_st{i}") for i in range(2)]
            LD, VL, UEND, NORM, ST, FIN = (Hist() for _ in range(6))
            RING, EXPT, PVT = Hist(), Hist(), Hist()
            st = dict(si=0, ei=0, qci=0, tcount=0, fin=0)
            tiles = self.attn_tiles()
            pending = []
            NU = 2 * HB

            def loads(u):
                if u >= NU:
                    return
                h, m = u // 2, u % 2
                P.wait("sp", UEND[u - 3])
                P.dma("sp", qs[:, u % 3, :], self.qkT[q_base + u], qld[u % 3])
                LD[u] = P.dma("sp", ks[:, u % 3, :], self.qkT[k_base + u], qld[u % 3])
                if m == 0:
                    P.wait("sp", UEND[u - 3])
                    VL[h] = P.dma("sp", v2[:, h % 2, :, :], vt[:, :, WA + h * 256:WA + (h + 1) * 256], vld[h % 2])

            def emitS(u, tile):
                qc, kt, q0, wq = tile
                si = st["si"]
                st["si"] += 1
                bank = 6 + si % 2
                P.wait("pe", RING[si - 2])
                P.wait("pe", LD[u])
                diag = kt >= 4 * qc
                tk = P.op("pe", "matmul", out=ps[:, bank * 512:bank * 512 + wq],
                          lhsT=ks[:, u % 3, kt * 128:(kt + 1) * 128], rhs=qs[:, u % 3, q0:q0 + wq],
                          start=True, stop=not diag, signal=not diag)
                if diag:
                    tk = P.op("pe", "matmul", out=ps[:, bank * 512:bank * 512 + 128], lhsT=self.identb[:, :],
                              rhs=self.trib[:, :], start=False, stop=True, signal=True)
                return si, tk

            def emitExp(si, stok, wq):
                ei = st["ei"]
                st["ei"] += 1
                bank = 6 + si % 2
                P.wait("act", stok)
                P.wait("act", PVT[ei - 4])
                EXPT[ei] = P.op("act", "activation", out=eb[:, ei % 4, 0:wq], in_=ps[:, bank * 512:bank * 512 + wq],
                                func=AF.Exp, scale=scale, signal=True)
                RING[si] = EXPT[ei]
                return ei

            def fin_stage1(h, qc, fi):
                def f():
                    P.wait("act", NORM[("comb", fi)])
                    P.wait("act", FIN[("ssq", fi - 2)])
                    FIN[("sq", fi)] = P.op("act", "activation", out=sqb[:, fi % 2, :, :], in_=comb[:, fi % 2, :, :],
                                           func=AF.Square, signal=True)
                return f

            def fin_stage2(h, qc, fi):
                def f():
                    si = st["si"]
                    st["si"] += 1
                    bank = 6 + si % 2
                    P.wait("pe", RING[si - 2])
                    P.wait("pe", FIN[("sq", fi)])
                    P.op("pe", "matmul", out=ps[:, bank * 512:(bank + 1) * 512], lhsT=self.onesb[:, :],
                         rhs=sqb[:, fi % 2, 0, :], start=True, stop=False)
                    tk = P.op("pe", "matmul", out=ps[:, bank * 512:(bank + 1) * 512], lhsT=self.onesb[:, :],
                              rhs=sqb[:, fi % 2, 1, :], start=False, stop=True, signal=True)
                    FIN[("ssq", fi)] = tk
                    P.wait("act", tk)
                    P.wait("act", FIN[("rs", fi - 1)])
                    P.wait("act", FIN[("out", fi - 1)])
                    a = P.op("act", "activation", out=sd[:, :], in_=ps[:, bank * 512:(bank + 1) * 512],
                             func=AF.Ln, bias=EPS, scale=1.0 / 256, signal=True)
                    RING[si] = a
                    P.wait("act", a)
                    b = P.op("act", "activation", out=rs[:, :], in_=sd[:, :], func=AF.Exp, scale=-0.5,
                             signal=True)
                    FIN[("rs", fi)] = b
                    P.wait("dve", b)
                    P.wait("dve", ST[fi - 2])
                    for half in range(2):
                        d = P.op("dve", "scalar_tensor_tensor", out=osb[:, fi % 2, half, :],
                                 in0=comb[:, fi % 2, half, :],
                                 scalar=self.par[:, c.c_sub + half:c.c_sub + half + 1], in1=rs[:, :],
                                 op0=ALU.mult, op1=ALU.mult, signal=(half == 1))
                    FIN[("out", fi)] = d
                    P.wait("sp", d)
                    for half in range(2):
                        r0 = WA + h * 256 + half * 128
                        ST[fi] = P.dma("sp", self.attnT[r0:r0 + 128, qc * 512:(qc + 1) * 512],
                                       osb[:, fi % 2, half, :], std[fi % 2])
                return f

            def emitPV(u, tile, ei):
                h, m = u // 2, u % 2
                qc, kt, q0, wq = tile
                qci = st["qci"]
                ob = qci % 2
                P.wait("pe", EXPT[ei])
                P.wait("pe", VL[h])
                if kt == 0:
                    P.wait("pe", NORM[qci - 2])
                c0 = q0 - 512 * qc
                last = (kt == 4 * qc + 3)
                for half in range(2):
                    P.op("pe", "matmul", out=ps[:, (2 * ob + half) * 512 + c0:(2 * ob + half) * 512 + c0 + wq],
                         lhsT=v2[:, h % 2, kt, half * 128:(half + 1) * 128], rhs=eb[:, ei % 4, 0:wq],
                         start=(kt == 0), stop=last)
                PVT[ei] = P.op("pe", "matmul", out=ps[:, (4 + ob) * 512 + c0:(4 + ob) * 512 + c0 + wq],
                               lhsT=self.onesb[:, :], rhs=eb[:, ei % 4, 0:wq], start=(kt == 0), stop=last,
                               signal=True)
                if last:
                    P.wait("dve", PVT[ei])
                    a = P.op("dve", "reciprocal", out=rden[:, ob, :], in_=ps[:, (4 + ob) * 512:(5 + ob) * 512],
                             signal=True)
                    P.wait("dve", a)
                    if m == 0:
                        for half in range(2):
                            d = P.op("dve", "tensor_tensor", out=on0[:, half, qc * 512:(qc + 1) * 512],
                                     in0=ps[:, (2 * ob + half) * 512:(2 * ob + half + 1) * 512],
                                     in1=rden[:, ob, :], op=ALU.mult, signal=(half == 1))
                        NORM[qci] = d
                    else:
                        fi = st["fin"]
                        st["fin"] += 1
                        P.wait("dve", FIN[("out", fi - 2)])
                        P.wait("dve", FIN[("sq", fi - 2)])
                        for half in range(2):
                            d = P.op("dve", "tensor_tensor", out=o1[:, fi % 2, half, :],
                                     in0=ps[:, (2 * ob + half) * 512:(2 * ob + half + 1) * 512],
                                     in1=rden[:, ob, :], op=ALU.mult, signal=(half == 1))
                        NORM[qci] = d
                        P.wait("dve", d)
                        for half in range(2):
                            e = P.op("dve", "scalar_tensor_tensor", out=comb[:, fi % 2, half, :],
                                     in0=o1[:, fi % 2, half, :], scalar=self.lamc[:, 5:6],
                                     in1=on0[:, half, qc * 512:(qc + 1) * 512], op0=ALU.mult, op1=ALU.add,
                                     signal=(half == 1))
                        NORM[("comb", fi)] = e
                        pending.append((st["tcount"] + 8, fin_stage1(h, qc, fi)))
                        pending.append((st["tcount"] + 12, fin_stage2(h, qc, fi)))
                    st["qci"] += 1

            def run_pending(force=False):
                keep = []
                for due, f in pending:
                    if force or due <= st["tcount"]:
                        f()
                    else:
                        keep.append((due, f))
                pending[:] = keep

            loads(0)
            loads(1)
            for u in range(NU):
                loads(u + 2)
                nt = len(tiles)
                pend = emitS(u, tiles[0])
                for i in range(nt):
                    si, stok = pend
                    ei = emitExp(si, stok, tiles[i][3])
                    if i + 1 < nt:
                        pend = emitS(u, tiles[i + 1])
                    emitPV(u, tiles[i], ei)
                    st["tcount"] += 1
                    run_pending()
                UEND[u] = PVT[st["ei"] - 1]
            run_pending(force=True)
            P.run()


    def load_act_phase(self, name, srcT, nk):
        S = self.cfg.S
        P = Phase(self, name)
        ds = [self.dsem(f"{name}_l{i}") for i in range(4)]
        for k in range(nk):
            P.dma("sp", self.A(k, 0, S), srcT[k * 128:(k + 1) * 128, :], ds[k % 4])
        P.run()

    def proj_res_phase(self, name, wdram, nblk, nk, rhs_fn, res_dram, dst_dram, w_res=None, rstd_out=None, preload=None):
        c = self.cfg
        nc = self.nc
        S, D = c.S, c.D
        HT = S // 2
        NU = nblk * 2
        P = Phase(self, name)
        RW = 3
        with ExitStack() as es:
            if w_res is None:
                w = es.enter_context(nc.sbuf_tensor(f"{name}_w", [128, RW, nk * 128], BF16))
                wld = [self.dsem(f"{name}_w{i}") for i in range(RW)]
            rsb = es.enter_context(nc.sbuf_tensor(f"{name}_res", [128, 2, HT], F32))
            yb = es.enter_context(nc.sbuf_tensor(f"{name}_yb", [128, 2, HT], F32))
            sqb = es.enter_context(nc.sbuf_tensor(f"{name}_sqb", [128, 2, HT], BF16))
            sd = es.enter_context(nc.sbuf_tensor(f"{name}_sd", [128, S], F32))
            rs = es.enter_context(nc.sbuf_tensor(f"{name}_rs", [128, S], F32))
            ps = es.enter_context(nc.psum_tensor(f"{name}_ps", [128, 4096], F32))
            rld = [self.dsem(f"{name}_r{i}") for i in range(2)]
            std = [self.dsem(f"{name}_s{i}") for i in range(2)]
            MM, WL, RL, ADD, SQ, SS, ST = (Hist() for _ in range(7))
            pre_toks = []
            if preload is not None:
                pds = [self.dsem(f"{name}_pl{i}") for i in range(4)]
                for k in range(preload[1]):
                    P.dma("sp", self.A(k, 0, S), preload[0][k * 128:(k + 1) * 128, :], pds[k % 4])
                pre_toks = [(d_.h, d_.n) for d_ in pds]

            def emit_ss(v):
                if not (0 <= v < NU):
                    return
                ob, hh = v // 2, v % 2
                P.wait("pe", SQ[v])
                for t in range(HT // 512):
                    SS[v] = P.op("pe", "matmul", out=ps[:, 2048 + hh * HT + t * 512:2048 + hh * HT + (t + 1) * 512],
                                 lhsT=self.onesb[:, :], rhs=sqb[:, v % 2, t * 512:(t + 1) * 512],
                                 start=(ob == 0), stop=(ob == nblk - 1), signal=(t == HT // 512 - 1))

            for u in range(NU):
                ob, hh = u // 2, u % 2
                g = u % 2
                if w_res is None:
                    slot = ob % RW
                    if u == 0:
                        for o2 in range(min(RW - 1, nblk)):
                            WL[o2] = P.dma("pool", w[:, o2 % RW, :], wdram[o2], wld[o2 % RW])
                    if hh == 0 and ob + RW - 1 < nblk:
                        o2 = ob + RW - 1
                        P.wait("pool", MM[2 * (ob - 1) + 1])
                        WL[o2] = P.dma("pool", w[:, o2 % RW, :], wdram[o2], wld[o2 % RW])
                    P.wait("pe", WL[ob])
                    lhs = (lambda slot: (lambda k: w[:, slot, k * 128:(k + 1) * 128]))(slot)
                else:
                    lhs = (lambda ob: (lambda k: w_res(k, ob)))(ob)
                P.wait("sp", ADD[u - 2])
                RL[u] = P.dma("sp", rsb[:, g, :], res_dram[ob * 128:(ob + 1) * 128, hh * HT:(hh + 1) * HT], rld[g])
                P.wait("pe", ADD[u - 2])
                if u == 0:
                    for tk_ in pre_toks:
                        P.wait("pe", tk_)
                for k in range(nk):
                    for t in range(HT // 512):
                        tk = P.op("pe", "matmul", out=ps[:, g * HT + t * 512:g * HT + (t + 1) * 512],
                                  lhsT=lhs(k), rhs=rhs_fn(k, hh * HT + t * 512, hh * HT + (t + 1) * 512),
                                  start=(k == 0), stop=(k == nk - 1),
                                  signal=(k == nk - 1 and t == HT // 512 - 1))
                MM[u] = tk
                emit_ss(u - 1)
                P.wait("dve", MM[u])
                P.wait("dve", RL[u])
                P.wait("dve", ST[u - 2])
                P.wait("dve", SQ[u - 2])
                ADD[u] = P.op("dve", "tensor_tensor", out=yb[:, g, :], in0=ps[:, g * HT:(g + 1) * HT],
                              in1=rsb[:, g, :], op=ALU.add, signal=True)
                P.wait("act", ADD[u])
                P.wait("act", SS[u - 2])
                SQ[u] = P.op("act", "activation", out=sqb[:, g, :], in_=yb[:, g, :], func=AF.Square, signal=True)
                ST[u] = P.dma("act", dst_dram[ob * 128:(ob + 1) * 128, hh * HT:(hh + 1) * HT], yb[:, g, :], std[g])
            emit_ss(NU - 1)
            if rstd_out is not None:
                P.wait("act", SS[NU - 1])
                P.wait("act", SS[NU - 2])
                a1 = P.op("act", "activation", out=sd[:, :], in_=ps[:, 2048:2048 + S], func=AF.Ln, bias=EPS,
                          scale=1.0 / D, signal=True)
                P.wait("act", a1)
                a2 = P.op("act", "activation", out=rs[:, :], in_=sd[:, :], func=AF.Exp, scale=-0.5, signal=True)
                P.wait("sp", a2)
                P.dma("sp", rstd_out[:, :], rs[:, :], rld[0])
            P.run()

    def cross_phase(self):
        c = self.cfg
        nc = self.nc
        S, D, KC, M = c.S, c.D, c.KC, c.M
        scale = 128 ** -0.5
        with ExitStack() as es6:
            km = es6.enter_context(nc.sbuf_tensor("p6_km", [128, 4, M], BF16))
            vm = es6.enter_context(nc.sbuf_tensor("p6_vm", [128, M // 128, 512], BF16))
            with nc.sbuf_tensor("p6_memn", [128, KC, M], BF16) as memn:
                self.norm_phase("p6a", self.memT, M, c.c_gmem, dst_fn=lambda k: memn[:, k, :])
                mfn = lambda k, a, b: memn[:, k, a:b]
                self.qk_phase("p6b", self.wmk, 4, None, [(c.c_km, False, None, None)] * 4, S=M, rhs_fn=mfn,
                              dst_sb=lambda s, a, b: km[:, s, a:b], RW=2)
                self.v_phase("p6c", wdram=self.wmv, ncb=1, S=M, lhs_fn=mfn, dst_sb=lambda t: vm[:, t, :])
            qm = es6.enter_context(nc.sbuf_tensor("p6_qm", [128, 4, S], BF16))
            self.norm_phase("p6d", self.h1T, S, c.c_gcross, rstd_src=self.rstd1)
            self.qk_phase("p6e", self.wmq, 4, None, [(c.c_qm, False, None, None)] * 4,
                          dst_sb=lambda s, a, b: qm[:, s, a:b], RW=2)
            om = lambda k, a, b: self.act[:, k * S + a:k * S + b]
            wmo_off = 4 * S
            P = Phase(self, "p6f")
            with ExitStack() as es:
                eb = es.enter_context(nc.sbuf_tensor("p6_e", [128, 4, 512], BF16))
                rden = es.enter_context(nc.sbuf_tensor("p6_rden", [128, 2, 512], F32))
                ps = es.enter_context(nc.psum_tensor("p6_ps", [128, 4096], F32))
                wl = self.dsem("p6_wmo")
                WMO = P.dma("pool", self.act[:, wmo_off:wmo_off + 4 * D], self.wmo[:, :], wl)
                RING, EXPT, PVT, NORM = (Hist() for _ in range(4))
                si = ei = qci = 0
                MT = M // 128
                for h in range(4):
                    for qc in range(S // 512):
                        ob = qci % 2
                        pend = []
                        for mt in range(MT):
                            bank = 4 + si % 4
                            P.wait("pe", RING[si - 4])
                            tk = P.op("pe", "matmul", out=ps[:, bank * 512:(bank + 1) * 512],
                                      lhsT=km[:, h, mt * 128:(mt + 1) * 128], rhs=qm[:, h, qc * 512:(qc + 1) * 512],
                                      start=True, stop=True, signal=True)
                            P.wait("act", tk)
                            P.wait("act", PVT[ei - 4])
                            EXPT[ei] = P.op("act", "activation", out=eb[:, ei % 4, :], in_=ps[:, bank * 512:(bank + 1) * 512],
                                            func=AF.Exp, scale=scale, signal=True)
                            RING[si] = EXPT[ei]
                            pend.append((mt, ei))
                            si += 1
                            ei += 1
                        for mt, e in pend:
                            P.wait("pe", EXPT[e])
                            if mt == 0:
                                P.wait("pe", NORM[qci - 2])
                            P.op("pe", "matmul", out=ps[:, ob * 512:(ob + 1) * 512], lhsT=vm[:, mt, h * 128:(h + 1) * 128],
                                 rhs=eb[:, e % 4, :], start=(mt == 0), stop=(mt == MT - 1))
                            PVT[e] = P.op("pe", "matmul", out=ps[:, (2 + ob) * 512:(3 + ob) * 512], lhsT=self.onesb[:, :],
                                          rhs=eb[:, e % 4, :], start=(mt == 0), stop=(mt == MT - 1), signal=True)
                        P.wait("dve", PVT[pend[-1][1]])
                        a = P.op("dve", "reciprocal", out=rden[:, ob, :], in_=ps[:, (2 + ob) * 512:(3 + ob) * 512], signal=True)
                        P.wait("dve", a)
                        NORM[qci] = P.op("dve", "tensor_tensor", out=om(h, qc * 512, (qc + 1) * 512),
                                         in0=ps[:, ob * 512:(ob + 1) * 512], in1=rden[:, ob, :], op=ALU.mult, signal=True)
                        qci += 1
                P.run()
            self.proj_res_phase("p6g", None, KC, 4, om, self.h1T, self.h2T, rstd_out=self.rstd2,
                                w_res=lambda k, ob: self.act[:, wmo_off + k * D + ob * 128:wmo_off + k * D + (ob + 1) * 128])

    def ffn_up_phase(self, name):
        c = self.cfg
        nc = self.nc
        S, KC, FC = c.S, c.KC, c.FC
        P = Phase(self, name)
        RW = 4
        HT = S // 2
        with ExitStack() as es:
            w = es.enter_context(nc.sbuf_tensor(f"{name}_w", [128, RW, KC * 128], BF16))
            sg = es.enter_context(nc.sbuf_tensor(f"{name}_sg", [128, 2, HT], F32))
            ob_ = es.enter_context(nc.sbuf_tensor(f"{name}_o", [128, 2, HT], BF16))
            ps = es.enter_context(nc.psum_tensor(f"{name}_ps", [128, 4096], F32))
            wld = [self.dsem(f"{name}_w{i}") for i in range(RW)]
            std = [self.dsem(f"{name}_s{i}") for i in range(2)]
            MMG, MMU, WLG, WLU, SIL, MUL, ST = (Hist() for _ in range(7))
            n = 0
            for f in range(FC):
                sg_, su_ = (2 * f) % RW, (2 * f + 1) % RW
                P.wait("pool", MMG[f - 2])
                WLG[f] = P.dma("pool", w[:, sg_, :], self.wg[f], wld[sg_])
                P.wait("pool", MMU[f - 2])
                WLU[f] = P.dma("pool", w[:, su_, :], self.wu[f], wld[su_])
                for which, slot, WL, MMT in ((0, sg_, WLG, MMG), (1, su_, WLU, MMU)):
                    P.wait("pe", WL[f])
                    if which == 0:
                        P.wait("pe", SIL[2 * (f - 1) + 1])
                    else:
                        P.wait("pe", MUL[2 * (f - 1) + 1])
                    for k in range(KC):
                        for t in range(S // 512):
                            tk = P.op("pe", "matmul", out=ps[:, which * 2048 + t * 512:which * 2048 + (t + 1) * 512],
                                      lhsT=w[:, slot, k * 128:(k + 1) * 128], rhs=self.A(k, t * 512, (t + 1) * 512),
                                      start=(k == 0), stop=(k == KC - 1),
                                      signal=(k == KC - 1 and t == S // 512 - 1))
                    MMT[f] = tk
                for hh in range(2):
                    P.wait("act", MMG[f])
                    P.wait("act", MUL[n - 2])
                    SIL[n] = P.op("act", "activation", out=sg[:, n % 2, :], in_=ps[:, hh * HT:(hh + 1) * HT],
                                  func=AF.Silu, signal=True)
                    P.wait("dve", SIL[n])
                    P.wait("dve", MMU[f])
                    P.wait("dve", ST[n - 2])
                    MUL[n] = P.op("dve", "tensor_tensor", out=ob_[:, n % 2, :], in0=ps[:, 2048 + hh * HT:2048 + (hh + 1) * HT],
                                  in1=sg[:, n % 2, :], op=ALU.mult, signal=True)
                    P.wait("sp", MUL[n])
                    ST[n] = P.dma("sp", self.actT[f * 128:(f + 1) * 128, hh * HT:(hh + 1) * HT], ob_[:, n % 2, :], std[n % 2])
                    n += 1
            P.run()

    def ffn_down_phase(self, name):
        c = self.cfg
        nc = self.nc
        S, D, FC = c.S, c.D, c.FC
        P = Phase(self, name)
        NP = D // 512
        HT = S // 2
        RC = 4
        G = 8 if FC >= 16 else (4 if FC >= 8 else 1)
        kb = [int(round(i * FC / G)) for i in range(G + 1)]
        grp_of_end = {kb[g + 1] - 1: g for g in range(G)}
        grp_of_start = {kb[g]: g for g in range(G)}
        coff = FC * 512
        with ExitStack() as es:
            rsb = es.enter_context(nc.sbuf_tensor(f"{name}_res", [128, 4, HT], F32))
            yb = es.enter_context(nc.sbuf_tensor(f"{name}_yb", [128, 4, HT], F32))
            tmp = es.enter_context(nc.sbuf_tensor(f"{name}_tmp", [128, 2, HT], F32))
            ps = es.enter_context(nc.psum_tensor(f"{name}_ps", [128, 4096], F32))
            wt = lambda k, ob: self.act[:, k * 512 + ob * 128:k * 512 + (ob + 1) * 128]
            ch = lambda s, a, b_: self.act[:, coff + s * 4 * HT + a:coff + s * 4 * HT + b_]
            wld = [self.dsem(f"{name}_w{i}") for i in range(G)]
            cld = [self.dsem(f"{name}_c{i}") for i in range(RC)]
            rld = [self.dsem(f"{name}_r{i}") for i in range(4)]
            std = [self.dsem(f"{name}_s{i}") for i in range(4)]
            MM, WL, CL, RL, ADD, ST, FREE = (Hist() for _ in range(7))
            n = 0
            ei = 0
            for g in range(G):
                WL[(0, g)] = P.dma("pool", self.act[:, kb[g] * 512:kb[g + 1] * 512],
                                   self.wd[0][:, kb[g] * 512:kb[g + 1] * 512], wld[g])
            CK = 4
            NG = (FC + CK - 1) // CK
            for pb in range(NP):
                for th in range(2):
                    for kg in range(NG):
                        k0 = kg * CK
                        nk = min(CK, FC - k0)
                        cs = n % RC
                        if kg == max(0, NG - 6):
                            for j in range(4):
                                e = ei + j
                                P.wait("act", ADD[e - 4])
                                RL[e] = P.dma("act", rsb[:, e % 4, :],
                                              self.h2T[(pb * 4 + j) * 128:(pb * 4 + j + 1) * 128, th * HT:(th + 1) * HT], rld[e % 4])
                        P.wait("sp", MM[n - RC])
                        CL[n] = P.dma("sp", ch(cs, 0, nk * HT).rearrange("p (a t) -> p a t", a=nk),
                                      self.actT[k0 * 128:(k0 + nk) * 128, th * HT:(th + 1) * HT].rearrange("(a p) t -> p a t", p=128),
                                      cld[cs])
                        P.wait("pe", CL[n])
                        for kk in range(nk):
                            k = k0 + kk
                            if th == 0 and k in grp_of_start:
                                P.wait("pe", WL[(pb, grp_of_start[k])])
                            for ob in range(4):
                                if k == 0:
                                    P.wait("pe", FREE[ei - 4 + ob])
                                for t in range(HT // 512):
                                    endg = (th == 1 and k in grp_of_end and pb + 1 < NP)
                                    tk = P.op("pe", "matmul", out=ps[:, ob * HT + t * 512:ob * HT + (t + 1) * 512],
                                              lhsT=wt(k, ob), rhs=ch(cs, kk * HT + t * 512, kk * HT + (t + 1) * 512),
                                              start=(k == 0), stop=(k == FC - 1),
                                              signal=(ob == 3 and t == HT // 512 - 1 and (kk == nk - 1 or endg)))
                            if th == 1 and k in grp_of_end and pb + 1 < NP:
                                g = grp_of_end[k]
                                P.wait("pool", tk)
                                WL[(pb + 1, g)] = P.dma("pool", self.act[:, kb[g] * 512:kb[g + 1] * 512],
                                                        self.wd[pb + 1][:, kb[g] * 512:kb[g + 1] * 512], wld[g])
                        MM[n] = tk
                        n += 1
                    last = tk

                    for ob in (2, 3):
                        e = ei + ob
                        P.wait("act", last)
                        P.wait("act", ADD[e - 4])
                        FREE[e] = P.op("act", "activation", out=tmp[:, ob - 2, :], in_=ps[:, ob * HT:(ob + 1) * HT],
                                       func=AF.Copy, signal=True)
                    for ob in range(4):
                        r0 = (pb * 4 + ob) * 128
                        e = ei + ob
                        P.wait("dve", last)
                        P.wait("dve", RL[e])
                        P.wait("dve", ST[e - 4])
                        if ob < 2:
                            src_ap = ps[:, ob * HT:(ob + 1) * HT]
                        else:
                            P.wait("dve", FREE[e])
                            src_ap = tmp[:, ob - 2, :]
                        ADD[e] = P.op("dve", "tensor_tensor", out=yb[:, e % 4, :], in0=src_ap,
                                      in1=rsb[:, e % 4, :], op=ALU.add, signal=True)
                        if ob < 2:
                            FREE[e] = ADD[e]
                        P.wait("act", ADD[e])
                        ST[e] = P.dma("act", self.yT[r0:r0 + 128, th * HT:(th + 1) * HT], yb[:, e % 4, :], std[e % 4])
                    ei += 4
            P.run()

    def final_phase(self):
        if self.upto >= 8:
            return
        nc = self.nc
        P = Phase(self, "pfin")
        with nc.sbuf_tensor("fin_x", [128, self.cfg.S], F32) as fx:
            d1 = self.dsem("fin1")
            t = P.dma("sp", fx[:, :], self.xT[0:128, :], d1)
            P.wait("sp", t)
            P.dma("sp", self.yT[0:128, :], fx[:, :], d1)
            P.run()


def tile_w(w, C):
    K, N = w.shape
    return np.ascontiguousarray(
        w.reshape(K // 128, 128, N // C, C).transpose(2, 1, 0, 3).reshape(N // C, 128, (K // 128) * C))


def host_consts(c):
    S = c.S
    pos = np.arange(S, dtype=np.float32)
    inv = (np.float32(500000.0) ** (-np.arange(0, 32, 2, dtype=np.float32) / np.float32(32))).astype(np.float32)
    ang = (pos[:, None] * inv[None, :]).astype(np.float32)
    cos = np.cos(ang).astype(np.float32).T
    sin = np.sin(ang).astype(np.float32).T
    rope = np.zeros((128, 2 * S), np.float32)
    rope[0:16, 0:S] = cos
    rope[16:32, 0:S] = cos
    rope[0:16, S:] = sin
    rope[16:32, S:] = sin
    cst = np.zeros((128, 128 * 4 + 64), np.float32)
    cst[:, 0:128] = np.eye(128, dtype=np.float32)
    RT = np.zeros((128, 128), np.float32)
    for p in range(16):
        RT[p + 16, p] = -1.0
        RT[p, p + 16] = 1.0
    cst[:, 128:256] = RT
    tri = np.zeros((128, 128), np.float32)
    kk, qq = np.meshgrid(np.arange(128), np.arange(128), indexing="ij")
    tri[kk > qq] = NEG
    cst[:, 256:384] = tri
    cm = np.zeros((8, 8), np.float32)
    for qt in range(8, 16):
        i = qt // 2
        for j in range(8):
            cm[qt - 8, j] = 0.0 if j < i else (1e30 if j == i else -1e30)
    cst[:, 512:576] = cm.reshape(1, 64)
    return rope, cst


def host_prep(c, inp):
    D, KC = c.D, c.KC
    g = lambda k: np.asarray(inp[k], dtype=np.float32)[0]
    w_in = g("w_in")
    WA = c.WA
    WB = c.HB * 256
    o_qa, o_ka, o_va = 0, WA, 2 * WA
    o_qb, o_kb, o_vb = 3 * WA, 3 * WA + WB, 3 * WA + 2 * WB
    wqk_cols = np.concatenate([w_in[:, o_qa:o_qa + WA], w_in[:, o_ka:o_ka + WA],
                               w_in[:, o_qb:o_qb + WB], w_in[:, o_kb:o_kb + WB]], axis=1)
    wv_cols = np.concatenate([w_in[:, o_va:o_va + WA], w_in[:, o_vb:o_vb + WB]], axis=1)
    shared = {
        "wqk": tile_w(wqk_cols, 128),
        "wv": tile_w(wv_cols, 512),
        "wo": tile_w(g("w_out"), 128),
        "wmq": tile_w(g("w_mq"), 128),
        "wmk": tile_w(g("w_mk"), 128),
        "wmv": tile_w(g("w_mv"), 512),
        "wmo": np.ascontiguousarray(g("w_mo").reshape(4, 128, D).transpose(1, 0, 2).reshape(128, 4 * D)),
        "wg": tile_w(g("w_gate"), 128),
        "wu": tile_w(g("w_up"), 128),
        "wd": tile_w(g("w_down"), 512),
    }
    par = np.zeros((128, c.NPAR), np.float32)
    col = lambda v: v.reshape(-1, 128).T
    par[:, c.c_gmix:c.c_gmix + KC] = col(g("norm_mix_g"))
    par[:, c.c_gcross:c.c_gcross + KC] = col(g("norm_cross_g"))
    par[:, c.c_gmem:c.c_gmem + KC] = col(g("norm_mem_g"))
    par[:, c.c_gffn:c.c_gffn + KC] = col(g("norm_ffn_g"))
    par[:, c.c_qa] = g("q_norm_a")
    par[:, c.c_ka] = g("k_norm_a")
    par[:, c.c_qb] = g("q_norm_b")
    par[:, c.c_kb] = g("k_norm_b")
    par[:, c.c_qm] = g("q_norm_m")
    par[:, c.c_km] = g("k_norm_m")
    par[:, c.c_sub:c.c_sub + 2] = col(g("diff_subln_g"))
    par[:, c.c_lam + 0] = g("lam_q1")
    par[:, c.c_lam + 1] = g("lam_k1")
    par[:, c.c_lam + 2] = g("lam_q2")
    par[:, c.c_lam + 3] = g("lam_k2")
    rope, cst = host_consts(c)
    shared.update(par=par, rope=rope, cst=cst)
    x = np.asarray(inp["x"], dtype=np.float32)
    mem = np.asarray(inp["mem"], dtype=np.float32)
    in_maps = []
    for b in range(x.shape[0]):
        m = dict(shared)
        m["xT"] = np.ascontiguousarray(x[b].T)
        m["memT"] = np.ascontiguousarray(mem[b].T)
        in_maps.append(m)
    return in_maps


def run(cfg, inp, debug=False, upto=99, trace=False):
    K = Kern(cfg, debug=debug, upto=upto)
    nc = K.build()
    in_maps = host_prep(cfg, inp)
    res = run_bass_kernel_spmd(nc, in_maps, core_ids=list(range(len(in_maps))), trace=trace)
    return res


def kernel(**inputs):
    cfg = Cfg()
    res = run(cfg, inputs)
    out = np.stack([np.ascontiguousarray(r["yT"].T) for r in res.results], axis=0)
    return out.astype(np.float32)
```

```python
import math
from contextlib import ExitStack

import numpy as np

import concourse.bass as bass
import concourse.mybir as mybir
from concourse.bass_utils import run_bass_kernel_spmd

F32 = mybir.dt.float32
BF16 = mybir.dt.bfloat16
AF = mybir.ActivationFunctionType
ALU = mybir.AluOpType
AX = mybir.AxisListType

EPS = 1e-6
NEG = -30000.0
ENGS = ("pe", "act", "dve", "pool", "sp")


class Cfg:
    def __init__(self, D=4096, S=2048, DFF=11008, M=256):
        self.D, self.S, self.DFF, self.M = D, S, DFF, M
        self.KC = D // 128
        self.HA = D // 256
        self.HB = D // 512
        self.WA = self.HA * 128
        self.NQK = 2 * self.HA + 4 * self.HB
        self.NVB = (self.WA + self.HB * 256) // 512
        self.FC = DFF // 128
        self.MH = 4
        self.MW = 512
        self.NB = S // 256
        c = 0
        self.c_gmix = c; c += self.KC
        self.c_gcross = c; c += self.KC
        self.c_gmem = c; c += self.KC
        self.c_gffn = c; c += self.KC
        self.c_qa = c; c += 1
        self.c_ka = c; c += 1
        self.c_qb = c; c += 1
        self.c_kb = c; c += 1
        self.c_qm = c; c += 1
        self.c_km = c; c += 1
        self.c_sub = c; c += 2
        self.c_lam = c; c += 4
        self.NPAR = c


class DSem:
    def __init__(self, h):
        self.h = h
        self.n = 0


class Phase:
    def __init__(self, K, name):
        self.K = K
        self.nc = K.nc
        self.name = name
        self.q = {e: [] for e in ENGS}
        self.sem = K.eng_sem()
        self.cnt = K.eng_cnt
        self.waited = {e: {} for e in ENGS}
        self.dsems = {e: {} for e in ENGS}

    def op(self, eng, method, signal=False, **kw):
        if signal:
            self.cnt[eng] += 1
            self.q[eng].append(("i", method, kw, self.sem[eng], 1))
            return (self.sem[eng], self.cnt[eng])
        self.q[eng].append(("i", method, kw, None, 0))
        return None

    def wait(self, eng, tok):
        if tok is None:
            return
        sem, val = tok
        if self.waited[eng].get(sem.name, 0) >= val:
            return
        self.waited[eng][sem.name] = val
        self.q[eng].append(("w", sem, val))

    def dma(self, eng, out, in_, ds):
        ds.n += 16
        self.q[eng].append(("i", "dma_start", dict(out=out, in_=in_), ds.h, 16))
        self.dsems[eng][ds.h.name] = ds
        return (ds.h, ds.n)

    def run(self):
        nc = self.nc
        for e in ENGS:
            for ds in self.dsems[e].values():
                self.wait(e, (ds.h, ds.n))

        def replay(E, items):
            for it in items:
                if it[0] == "w":
                    E.wait_ge(it[1], it[2])
                else:
                    ins = getattr(E, it[1])(**it[2])
                    if it[3] is not None:
                        ins.then_inc(it[3], it[4])

        self.K.release_dsems()
        q = self.q
        with nc.Block() as block:
            @block.tensor
            def _(E):
                replay(E, q["pe"])

            @block.scalar
            def _(E):
                replay(E, q["act"])

            @block.vector
            def _(E):
                replay(E, q["dve"])

            @block.gpsimd
            def _(E):
                replay(E, q["pool"])

            @block.sync
            def _(E):
                replay(E, q["sp"])


class Hist:
    def __init__(self):
        self.d = {}

    def __setitem__(self, k, v):
        self.d[k] = v

    def __getitem__(self, k):
        return self.d.get(k)


class Kern:
    def __init__(self, cfg, debug=False, upto=99):
        self.cfg = cfg
        self.debug = debug
        self.upto = upto
        self.nc = bass.Bass("TRN2", target_bir_lowering=False)
        self.es = ExitStack()
        self.nsem = 0

    def new_sem(self, name):
        self.nsem += 1
        return self.es.enter_context(self.nc.semaphore(f"s{self.nsem}_{name}"))

    def eng_sem(self):
        if not hasattr(self, "_eng_sem"):
            self._eng_sem = {e: self.new_sem(f"eng_{e}") for e in ENGS}
            self.eng_cnt = {e: 0 for e in ENGS}
        return self._eng_sem

    def dsem(self, name):
        if not hasattr(self, "_free_ds"):
            self._free_ds, self._used_ds = [], []
        d = self._free_ds.pop() if self._free_ds else DSem(self.new_sem("dma"))
        self._used_ds.append(d)
        return d

    def release_dsems(self):
        if hasattr(self, "_free_ds"):
            self._free_ds.extend(self._used_ds)
            self._used_ds = []

    def dram_in(self, name, shape, dt=F32):
        return self.nc.dram_tensor(name, list(shape), dt, kind="ExternalInput").ap()

    def dram_scratch(self, name, shape, dt):
        kind = "ExternalOutput" if self.debug else "Internal"
        return self.nc.dram_tensor(name, list(shape), dt, kind=kind).ap()

    def sb(self, name, shape, dt):
        return self.es.enter_context(self.nc.sbuf_tensor(name, list(shape), dt))

    def A(self, k, a, b):
        S = self.cfg.S
        return self.act[:, k * S + a:k * S + b]

    def build(self):
        c = self.cfg
        nc = self.nc
        D, S, KC = c.D, c.S, c.KC
        with self.es:
            self.xT = self.dram_in("xT", [D, S])
            self.memT = self.dram_in("memT", [D, c.M])
            self.par_d = self.dram_in("par", [128, c.NPAR])
            self.rope_d = self.dram_in("rope", [128, 2 * S])
            self.cst_d = self.dram_in("cst", [128, 128 * 4 + 64])
            self.wqk = self.dram_in("wqk", [c.NQK, 128, KC * 128])
            self.wv = self.dram_in("wv", [c.NVB, 128, KC * 512])
            self.wo = self.dram_in("wo", [KC, 128, KC * 128])
            self.wmq = self.dram_in("wmq", [4, 128, KC * 128])
            self.wmk = self.dram_in("wmk", [4, 128, KC * 128])
            self.wmv = self.dram_in("wmv", [1, 128, KC * 512])
            self.wmo = self.dram_in("wmo", [128, 4 * D])
            self.wg = self.dram_in("wg", [c.FC, 128, KC * 128])
            self.wu = self.dram_in("wu", [c.FC, 128, KC * 128])
            self.wd = self.dram_in("wd", [D // 512, 128, c.FC * 512])
            self.yT = nc.dram_tensor("yT", [D, S], F32, kind="ExternalOutput").ap()

            self.qkT = self.dram_scratch("qkT", [c.NQK, 128, S], BF16)
            self.vtok = self.dram_scratch("vtok", [S, c.NVB * 512], BF16)
            self.attnT = self.dram_scratch("attnT", [D, S], BF16)
            self.h1T = self.dram_scratch("h1T", [D, S], F32)
            self.h2T = self.dram_scratch("h2T", [D, S], F32)
            self.actT = self.dram_scratch("actT", [c.DFF, S], BF16)
            self.qaf = self.dram_scratch("qaf", [c.HA, 128, S // 2], F32)
            self.rstd1 = self.dram_scratch("rstd1", [128, S], F32)
            self.rstd2 = self.dram_scratch("rstd2", [128, S], F32)

            self.act_elems = max(KC * S, c.FC * 512 + 16 * (S // 2), 4 * S + 4 * D, 32768)
            self.act = self.sb("act", [128, self.act_elems], BF16)
            self.par = self.sb("par_sb", [128, c.NPAR], F32)
            self.identf = self.sb("identf", [128, 128], F32)
            self.RT = self.sb("RT", [128, 128], F32)
            self.trif = self.sb("trif", [128, 128], F32)
            self.cmask = self.sb("cmask", [128, 64], F32)
            self.identb = self.sb("identb", [128, 128], BF16)
            self.trib = self.sb("trib", [128, 128], BF16)
            self.onesb = self.sb("onesb", [128, 128], BF16)
            self.onesf = self.sb("onesf", [128, 128], F32)
            self.selb = self.sb("selb", [128, 8 * 128], BF16)
            self.kmt = self.sb("kmt", [128, c.HA * 8], F32)
            self.kmb = self.sb("kmb", [128, c.HA * 8], BF16)
            self.lamc = self.sb("lamc", [128, 8], F32)
            self.es_early = ExitStack()
            self.rope = self.es_early.enter_context(nc.sbuf_tensor("rope_sb", [128, 2 * S], F32))
            self.RTb = self.es_early.enter_context(nc.sbuf_tensor("RTb", [128, 128], BF16))

            self.phase0()
            if self.upto >= 1:
                self.norm_phase("p1", self.xT, S, c.c_gmix)
            if self.upto >= 2:
                self.qk_phase("p2a", self.wqk, c.NQK, self.qkT, self.slot_kinds_main())
                self.es_early.close()
                self.v_phase("p2b", RW=2)
            if self.upto >= 3:
                self.moba_phase("p3")
            if self.upto >= 4:
                self.diff_phase("p4")
            if self.upto < 2:
                self.es_early.close()
            if self.upto >= 5:
                self.proj_res_phase("p5", self.wo, KC, KC, lambda k, a, b: self.A(k, a, b), self.xT, self.h1T,
                                    rstd_out=self.rstd1, preload=(self.attnT, KC))
            if self.upto >= 6:
                self.cross_phase()
            if self.upto >= 7:
                self.norm_phase("p7a", self.h2T, S, c.c_gffn, rstd_src=self.rstd2)
                self.ffn_up_phase("p7")
            if self.upto >= 8:
                self.ffn_down_phase("p8")
            self.final_phase()
        return nc

    def phase0(self):
        c = self.cfg
        P = Phase(self, "p0")
        d = self.dsem("p0ld")
        P.dma("sp", self.par[:, :], self.par_d[:, :], d)
        P.dma("sp", self.rope[:, :], self.rope_d[:, :], d)
        P.dma("sp", self.identf[:, :], self.cst_d[:, 0:128], d)
        P.dma("sp", self.RT[:, :], self.cst_d[:, 128:256], d)
        P.dma("sp", self.trif[:, :], self.cst_d[:, 256:384], d)
        tk = P.dma("sp", self.cmask[:, :], self.cst_d[:, 512:576], d)
        P.wait("act", tk)
        P.op("act", "activation", out=self.RTb[:, :], in_=self.RT[:, :], func=AF.Copy)
        P.wait("dve", tk)
        P.op("dve", "memset", ap=self.onesb[:, :], constant=1.0)
        P.op("dve", "memset", ap=self.onesf[:, :], constant=1.0)
        P.op("dve", "memset", ap=self.selb[:, :], constant=0.0)
        P.op("dve", "memset", ap=self.kmt[:, :], constant=0.0)
        P.op("dve", "tensor_copy", out=self.identb[:, :], in_=self.identf[:, :])
        t0 = P.op("dve", "tensor_copy", out=self.trib[:, :], in_=self.trif[:, :], signal=True)
        P.wait("dve", t0)
        for j in range(8):
            P.op("dve", "tensor_scalar", out=self.selb[:, j * 128:(j + 1) * 128],
                 in0=self.onesb[:, :], scalar1=self.identf[:, j:j + 1], scalar2=None,
                 op0=ALU.mult)
        cl = c.c_lam
        P.op("dve", "tensor_tensor", out=self.lamc[:, 0:1], in0=self.par[:, cl:cl + 1],
             in1=self.par[:, cl + 1:cl + 2], op=ALU.mult)
        t1 = P.op("dve", "tensor_tensor", out=self.lamc[:, 1:2], in0=self.par[:, cl + 2:cl + 3],
                  in1=self.par[:, cl + 3:cl + 4], op=ALU.mult, signal=True)
        with self.nc.psum_tensor("ps0", [128, 512], F32) as ps0:
            P.wait("pe", t1)
            t2 = P.op("pe", "matmul", out=ps0[:, 0:2], lhsT=self.onesf[:, :], rhs=self.lamc[:, 0:2],
                      start=True, stop=True, signal=True)
            P.wait("act", t2)
            t3 = P.op("act", "activation", out=self.lamc[:, 2:4], in_=ps0[:, 0:2], func=AF.Exp,
                      signal=True)
            P.wait("dve", t3)
            t4 = P.op("dve", "tensor_tensor", out=self.lamc[:, 4:5], in0=self.lamc[:, 3:4],
                      in1=self.lamc[:, 2:3], op=ALU.subtract, signal=True)
            P.wait("dve", t4)
            t5 = P.op("dve", "tensor_scalar", out=self.lamc[:, 5:6], in0=self.lamc[:, 4:5],
                      scalar1=-self.lam_init(), scalar2=None, op0=ALU.add, signal=True)
            P.op("dve", "tensor_scalar", out=self.par[:, c.c_sub:c.c_sub + 2],
                 in0=self.par[:, c.c_sub:c.c_sub + 2], scalar1=1.0 - self.lam_init(), scalar2=None,
                 op0=ALU.mult)
            P.run()

    @staticmethod
    def lam_init():
        return 0.8 - 0.6 * math.exp(-0.3 * 0)

    def norm_phase(self, name, src, S, gcol, dst_fn=None, rstd_src=None):
        c = self.cfg
        KC, D = c.KC, c.D
        nc = self.nc
        dst_fn = (lambda k: self.A(k, 0, S)) if dst_fn is None else dst_fn
        P = Phase(self, name)
        R = 4
        NW = 512 if S >= 512 else S
        NT = S // NW
        with ExitStack() as es:
            xs = es.enter_context(nc.sbuf_tensor(f"{name}_xs", [128, R, S], F32))
            if rstd_src is None:
                sq = es.enter_context(nc.sbuf_tensor(f"{name}_sq", [128, 2, S], BF16))
                sd = es.enter_context(nc.sbuf_tensor(f"{name}_sd", [128, S], F32))
                ps = es.enter_context(nc.psum_tensor(f"{name}_ps", [128, max(S, 512)], F32))
            rstd = es.enter_context(nc.sbuf_tensor(f"{name}_rstd", [128, S], F32))
            lds = [self.dsem(f"{name}_ld{i}") for i in range(R)]
            sqt, mmt, use = Hist(), Hist(), Hist()
            n = 0
            if rstd_src is not None:
                t2 = P.dma("sp", rstd[:, :], rstd_src[:, :], self.dsem(f"{name}_rl"))
            else:
                for k in range(KC):
                    slot = n % R
                    P.wait("sp", use[n - R])
                    ld = P.dma("sp", xs[:, slot, :], src[k * 128:(k + 1) * 128, :], lds[slot])
                    P.wait("act", ld)
                    P.wait("act", mmt[k - 2])
                    sqt[k] = P.op("act", "activation", out=sq[:, k % 2, :], in_=xs[:, slot, :],
                                  func=AF.Square, signal=True)
                    use[n] = sqt[k]
                    P.wait("pe", sqt[k])
                    for t in range(NT):
                        tk = P.op("pe", "matmul", out=ps[:, t * NW:(t + 1) * NW], lhsT=self.onesb[:, :],
                                  rhs=sq[:, k % 2, t * NW:(t + 1) * NW], start=(k == 0),
                                  stop=(k == KC - 1), signal=(t == NT - 1))
                    mmt[k] = tk
                    n += 1
                P.wait("act", mmt[KC - 1])
                t1 = P.op("act", "activation", out=sd[:, :], in_=ps[:, 0:S], func=AF.Sqrt, bias=EPS,
                          scale=1.0 / D, signal=True)
                P.wait("dve", t1)
                t2 = P.op("dve", "reciprocal", out=rstd[:, :], in_=sd[:, :], signal=True)
            for k in range(KC):
                slot = n % R
                P.wait("sp", use[n - R])
                ld = P.dma("sp", xs[:, slot, :], src[k * 128:(k + 1) * 128, :], lds[slot])
                P.wait("dve", ld)
                P.wait("dve", t2)
                use[n] = P.op("dve", "scalar_tensor_tensor", out=dst_fn(k), in0=xs[:, slot, :],
                              scalar=self.par[:, gcol + k:gcol + k + 1], in1=rstd[:, :],
                              op0=ALU.mult, op1=ALU.mult, signal=True)
                n += 1
            P.run()

    def slot_kinds_main(self):
        c = self.cfg
        kinds = []
        for h in range(c.HA):
            kinds.append((c.c_qa, True, None, h))
        for h in range(c.HA):
            kinds.append((c.c_ka, True, h, None))
        for h in range(2 * c.HB):
            kinds.append((c.c_qb, True, None, None))
        for h in range(2 * c.HB):
            kinds.append((c.c_kb, True, None, None))
        return kinds

    def qk_phase(self, name, wdram, nslots, dst, kinds, S=None, rhs_fn=None, dst_sb=None, RW=3):
        c = self.cfg
        nc = self.nc
        KC = c.KC
        S = c.S if S is None else S
        rhs_fn = self.A if rhs_fn is None else rhs_fn
        TW = min(1024, S)
        NH = S // TW
        NW = min(512, TW)
        NTT = TW // NW
        NU = nslots * NH
        P = Phase(self, name)
        with ExitStack() as es:
            w = es.enter_context(nc.sbuf_tensor(f"{name}_w", [128, RW, KC * 128], BF16))
            sqb = es.enter_context(nc.sbuf_tensor(f"{name}_sqb", [128, 2, TW], BF16))
            sd = es.enter_context(nc.sbuf_tensor(f"{name}_sd", [128, TW], F32))
            rs = es.enter_context(nc.sbuf_tensor(f"{name}_rs", [128, TW], F32))
            qn = es.enter_context(nc.sbuf_tensor(f"{name}_qn", [128, 2, TW], F32))
            t2b = es.enter_context(nc.sbuf_tensor(f"{name}_t2", [128, TW], F32))
            qkb = es.enter_context(nc.sbuf_tensor(f"{name}_qkb", [128, 2, TW], BF16))
            any_rope = any(k_[1] for k_ in kinds)
            if any_rope:
                hib = es.enter_context(nc.sbuf_tensor(f"{name}_hib", [128, TW], BF16))
                lob = es.enter_context(nc.sbuf_tensor(f"{name}_lob", [128, TW], BF16))
            ps = es.enter_context(nc.psum_tensor(f"{name}_ps", [128, 4096], F32))
            psP = lambda g: ps[:, g * 1024:g * 1024 + TW]
            psS = ps[:, 2048:2048 + TW]
            psR = ps[:, 3072:3072 + TW]
            wld = [self.dsem(f"{name}_w{i}") for i in range(RW)]
            std = [self.dsem(f"{name}_st{i}") for i in range(2)]
            A, B1, C, D1, D2, E, Fm, G, H, ST, WL, GL, STF, HI, LO = (Hist() for _ in range(15))
            ZT = None
            if any_rope:
                P.op("dve", "memset", ap=hib[:, :], constant=0.0)
                ZT = P.op("dve", "memset", ap=lob[:, :], constant=0.0, signal=True)
            stf = [self.dsem(f"{name}_stf{i}") for i in range(2)]
            KH = KC // 2 if KC >= 2 else KC

            def emitA(u, k0, k1, last):
                hs = u // NH
                tb = (u % NH) * TW
                slot = hs % RW
                g = u % 2
                tk = None
                for k in range(k0, k1):
                    for t in range(NTT):
                        tk = P.op("pe", "matmul", out=ps[:, g * 1024 + t * NW:g * 1024 + (t + 1) * NW],
                                  lhsT=w[:, slot, k * 128:(k + 1) * 128],
                                  rhs=rhs_fn(k, tb + t * NW, tb + (t + 1) * NW),
                                  start=(k == 0), stop=(k == KC - 1),
                                  signal=(last and k == k1 - 1 and t == NTT - 1))
                return tk

            for step in range(NU + 3):
                u = step
                if u < NU and u % NH == 0:
                    hs = u // NH
                    slot = hs % RW
                    P.wait("pool", A[(hs - RW) * NH + NH - 1])
                    WL[hs] = P.dma("pool", w[:, slot, :], wdram[hs], wld[slot])
                v = u - 2
                if 0 <= v < NU and kinds[v // NH][1]:
                    P.wait("pe", LO[v])
                    P.wait("pe", ZT)
                    P.wait("pe", G[v - 1])
                    for t in range(NTT):
                        P.op("pe", "matmul", out=ps[:, 3072 + t * NW:3072 + (t + 1) * NW],
                             lhsT=self.RTb[:, :], rhs=hib[:, t * NW:(t + 1) * NW], start=True, stop=False)
                        Fm[v] = P.op("pe", "matmul", out=ps[:, 3072 + t * NW:3072 + (t + 1) * NW],
                                     lhsT=self.RTb[:, :], rhs=lob[:, t * NW:(t + 1) * NW],
                                     start=False, stop=True, signal=(t == NTT - 1))
                if u < NU:
                    P.wait("pe", WL[u // NH])
                    P.wait("pe", E[u - 2])
                    emitA(u, 0, KH, KH == KC)
                    if KH == KC:
                        A[u] = (P.sem["pe"], P.cnt["pe"])
                v = u - 1
                if 0 <= v < NU:
                    P.wait("pe", B1[v])
                    P.wait("pe", D1[v - 1])
                    for t in range(NTT):
                        C[v] = P.op("pe", "matmul", out=ps[:, 2048 + t * NW:2048 + (t + 1) * NW],
                                    lhsT=self.onesb[:, :], rhs=sqb[:, v % 2, t * NW:(t + 1) * NW],
                                    start=True, stop=True, signal=(t == NTT - 1))
                if u < NU and KH < KC:
                    A[u] = emitA(u, KH, KC, True)
                v = u - 2
                if 0 <= v < NU:
                    gc, rope, kmh, qfh = kinds[v // NH]
                    tb = (v % NH) * TW
                    last = E[v]
                    if rope:
                        P.wait("dve", Fm[v])
                        P.wait("dve", E[v])
                        g1 = P.op("dve", "tensor_tensor", out=qn[0:32, v % 2, :], in0=qn[0:32, v % 2, :],
                                  in1=self.rope[0:32, tb:tb + TW], op=ALU.mult, signal=True)
                        g2 = P.op("dve", "tensor_tensor", out=t2b[0:32, :], in0=psR[0:32, :],
                                  in1=self.rope[0:32, c.S + tb:c.S + tb + TW], op=ALU.mult,
                                  signal=True)
                        P.wait("dve", g2)
                        last = P.op("dve", "tensor_tensor", out=qn[0:32, v % 2, :],
                                    in0=qn[0:32, v % 2, :], in1=t2b[0:32, :], op=ALU.add, signal=True)
                        G[v] = last
                    if kmh is not None:
                        P.wait("dve", last)
                        nb = TW // 256
                        b0 = kmh * 8 + tb // 256
                        last = P.op("dve", "tensor_reduce", out=self.kmt[:, b0:b0 + nb],
                                    in_=qn[:, v % 2, :].rearrange("p (b n) -> p b n", n=256),
                                    axis=AX.X, op=ALU.add, signal=True)
                        G[v] = last
                    GL[v] = last
                    if qfh is not None and tb >= c.S // 2 and c.NB > 4:
                        P.wait("sp", last)
                        STF[v] = P.dma("sp", self.qaf[qfh][:, tb - c.S // 2:tb - c.S // 2 + TW], qn[:, v % 2, :], stf[v % 2])
                v = u - 1
                if 0 <= v < NU:
                    P.wait("act", C[v])
                    P.wait("act", E[v - 1])
                    D1[v] = P.op("act", "activation", out=sd[:, :], in_=psS, func=AF.Ln, bias=EPS,
                                 scale=1.0 / 128, signal=True)
                    P.wait("act", D1[v])
                    D2[v] = P.op("act", "activation", out=rs[:, :], in_=sd[:, :], func=AF.Exp, scale=-0.5,
                                 signal=True)
                    P.wait("dve", D2[v])
                    P.wait("dve", A[v])
                    P.wait("dve", H[v - 2])
                    P.wait("dve", Fm[v - 2])
                    P.wait("dve", STF[v - 2])
                    gc = kinds[v // NH][0]
                    E[v] = P.op("dve", "scalar_tensor_tensor", out=qn[:, v % 2, :], in0=psP(v % 2),
                                scalar=self.par[:, gc:gc + 1], in1=rs[:, :], op0=ALU.mult,
                                op1=ALU.mult, signal=True)
                    if kinds[v // NH][1]:
                        P.wait("act", E[v])
                        P.wait("act", Fm[v - 1])
                        HI[v] = P.op("act", "activation", out=hib[0:32, :], in_=qn[0:32, v % 2, :],
                                     func=AF.Copy, signal=True)
                        P.wait("dve", HI[v])
                        P.wait("dve", Fm[v - 1])
                        LO[v] = P.op("dve", "tensor_tensor", out=lob[0:32, :], in0=qn[0:32, v % 2, :],
                                     in1=hib[0:32, :], op=ALU.subtract, signal=True)
                v = u - 2
                if 0 <= v < NU:
                    tb = (v % NH) * TW
                    P.wait("act", GL[v])
                    P.wait("act", ST[v - 2])
                    if dst_sb is not None:
                        H[v] = P.op("act", "activation", out=dst_sb(v // NH, tb, tb + TW), in_=qn[:, v % 2, :],
                                    func=AF.Copy, signal=True)
                    else:
                        H[v] = P.op("act", "activation", out=qkb[:, v % 2, :], in_=qn[:, v % 2, :],
                                    func=AF.Copy, signal=True)
                        P.wait("sp", H[v])
                        ST[v] = P.dma("sp", dst[v // NH][:, tb:tb + TW], qkb[:, v % 2, :], std[v % 2])
                if u < NU:
                    P.wait("act", A[u])
                    P.wait("act", C[u - 2])
                    B1[u] = P.op("act", "activation", out=sqb[:, u % 2, :], in_=psP(u % 2),
                                 func=AF.Square, signal=True)
            P.run()

    def v_phase(self, name, wdram=None, ncb=None, S=None, lhs_fn=None, dst_sb=None, RW=1):
        c = self.cfg
        nc = self.nc
        KC = c.KC
        S = c.S if S is None else S
        wdram = self.wv if wdram is None else wdram
        ncb = c.NVB if ncb is None else ncb
        lhs_fn = self.A if lhs_fn is None else lhs_fn
        P = Phase(self, name)
        NTK = S // 128
        with ExitStack() as es:
            w = es.enter_context(nc.sbuf_tensor(f"{name}_w", [128, RW, KC * 512], BF16))
            vsb = es.enter_context(nc.sbuf_tensor(f"{name}_vsb", [128, 4, 512], BF16))
            ps = es.enter_context(nc.psum_tensor(f"{name}_ps", [128, 4096], F32))
            wld = [self.dsem(f"{name}_w{i}") for i in range(RW)]
            std = [self.dsem(f"{name}_st{i}") for i in range(4)]
            MM, EV, ST, CBEND, WLD = Hist(), Hist(), Hist(), Hist(), Hist()
            n = 0
            for cb in range(min(RW, ncb)):
                WLD[cb] = P.dma("pool", w[:, cb % RW, :], wdram[cb], wld[cb % RW])
            for cb in range(ncb):
                ws = cb % RW
                P.wait("pe", WLD[cb])
                for t in range(NTK):
                    b = n % 8
                    P.wait("pe", EV[n - 8])
                    for k in range(KC):
                        tk = P.op("pe", "matmul", out=ps[:, b * 512:(b + 1) * 512],
                                  lhsT=lhs_fn(k, t * 128, (t + 1) * 128),
                                  rhs=w[:, ws, k * 512:(k + 1) * 512], start=(k == 0), stop=(k == KC - 1),
                                  signal=(k == KC - 1))
                    MM[n] = tk
                    eng = "act" if n % 2 == 0 else "dve"
                    P.wait(eng, MM[n])
                    P.wait(eng, ST[n - 4])
                    o = vsb[:, n % 4, :] if dst_sb is None else dst_sb(t)
                    if eng == "act":
                        EV[n] = P.op("act", "activation", out=o, in_=ps[:, b * 512:(b + 1) * 512],
                                     func=AF.Copy, signal=True)
                    else:
                        EV[n] = P.op("dve", "tensor_copy", out=o, in_=ps[:, b * 512:(b + 1) * 512],
                                     signal=True)
                    if dst_sb is None:
                        P.wait("sp", EV[n])
                        ST[n] = P.dma("sp", self.vtok[t * 128:(t + 1) * 128, cb * 512:(cb + 1) * 512],
                                      vsb[:, n % 4, :], std[n % 4])
                    n += 1
                CBEND[cb] = MM[n - 1]
                if cb + RW < ncb:
                    P.wait("pool", CBEND[cb])
                    WLD[cb + RW] = P.dma("pool", w[:, ws, :], wdram[cb + RW], wld[ws])
            P.run()


    def attn_tiles(self):
        S = self.cfg.S
        out = []
        for qc in range(S // 512):
            for kt in range(4 * qc + 4):
                q0 = max(512 * qc, 128 * kt)
                out.append((qc, kt, q0, 512 * (qc + 1) - q0))
        return out

    def moba_phase(self, name):
        c = self.cfg
        nc = self.nc
        S, HA = c.S, c.HA
        P = Phase(self, name)
        scale = 128 ** -0.5
        vt = self.vtok.rearrange("(t p) c -> p t c", p=128)
        NT16 = S // 128
        SEL = (c.NB > 4)
        with ExitStack() as es:
            A_ = self.act
            qs = A_[:, 0:3 * S].rearrange("p (a s) -> p a s", a=3)
            ks = A_[:, 3 * S:6 * S].rearrange("p (a s) -> p a s", a=3)
            v4 = A_[:, 6 * S:6 * S + 2 * NT16 * 512].rearrange("p (a t c) -> p a t c", a=2, t=NT16)
            e0 = 6 * S + 2 * NT16 * 512
            eb = A_[:, e0:e0 + 2048].rearrange("p (a s) -> p a s", a=4)
            gm = es.enter_context(nc.sbuf_tensor(f"{name}_gm", [128, 2, 64], F32))
            top = es.enter_context(nc.sbuf_tensor(f"{name}_top", [128, 2, 64], F32))
            negm = es.enter_context(nc.sbuf_tensor(f"{name}_negm", [128, 2, 8, 128], F32))
            negmT = es.enter_context(nc.sbuf_tensor(f"{name}_negmT", [128, 2, 1024], BF16))
            rden = es.enter_context(nc.sbuf_tensor(f"{name}_rden", [128, 2, 512], F32))
            osb = es.enter_context(nc.sbuf_tensor(f"{name}_osb", [128, 2, 512], BF16))
            qf = es.enter_context(nc.sbuf_tensor(f"{name}_qf", [128, 2, S // 2], F32))
            ps = es.enter_context(nc.psum_tensor(f"{name}_ps", [128, 4096], F32))
            qfl = [self.dsem(f"{name}_qf{i}") for i in range(2)]
            QFL, GT = Hist(), Hist()
            qld = [self.dsem(f"{name}_q{i}") for i in range(3)]
            vld = [self.dsem(f"{name}_v{i}") for i in range(2)]
            std = [self.dsem(f"{name}_st{i}") for i in range(2)]
            t0 = P.op("dve", "memset", ap=negm[:, :, :, :], constant=0.0)
            t0 = P.op("dve", "tensor_copy", out=self.kmb[:, :], in_=self.kmt[:, :], signal=True)
            LD, VL, HEADEND, SEL1, SEL2, NORM, ST = (Hist() for _ in range(7))
            RING, EXPT, PVT = Hist(), Hist(), Hist()
            st = dict(si=0, ei=0, qci=0, vg=-1, vn=0)
            tiles = self.attn_tiles()

            def loads(h):
                if h >= HA:
                    return
                P.wait("sp", HEADEND[h - 3])
                P.dma("sp", qs[:, h % 3, :], self.qkT[h], qld[h % 3])
                LD[h] = P.dma("sp", ks[:, h % 3, :], self.qkT[HA + h], qld[h % 3])
                if SEL:
                    P.wait("sp", GT[h - 2])
                    QFL[h] = P.dma("sp", qf[:, h % 2, :], self.qaf[h], qfl[h % 2])
                g = (h * 128) // 512
                if g != st["vg"]:
                    st["vg"] = g
                    st["vn"] += 1
                    vs = st["vn"] % 2
                    P.wait("sp", HEADEND[(g - 1) * 4 - 1] if g >= 2 else None)
                    VL[g] = P.dma("sp", v4[:, vs, :, :], vt[:, :, g * 512:(g + 1) * 512], vld[vs])
                    st[("vs", g)] = vs

            def sel1(h):
                if h >= HA or not SEL:
                    return
                P.wait("pe", QFL[h])
                P.wait("pe", SEL2[h - 1])
                for i in range(8):
                    tk = P.op("pe", "matmul", out=ps[:, 3072 + i * 8:3072 + (i + 1) * 8],
                              lhsT=qf[:, h % 2, i * 128:(i + 1) * 128], rhs=self.kmt[:, h * 8:(h + 1) * 8],
                              start=True, stop=True, signal=(i == 7))
                GT[h] = tk
                P.wait("dve", tk)
                P.wait("dve", SEL2[h - 2])
                a = P.op("dve", "tensor_tensor", out=gm[:, h % 2, :], in0=ps[:, 3072:3072 + 64],
                         in1=self.cmask[:, :], op=ALU.add, signal=True)
                P.wait("dve", a)
                for i in range(8):
                    b = P.op("dve", "max", out=top[:, h % 2, i * 8:(i + 1) * 8], in_=gm[:, h % 2, i * 8:(i + 1) * 8],
                             signal=(i == 7))
                P.wait("dve", b)
                for i in range(8):
                    d = P.op("dve", "tensor_scalar", out=negm[:, h % 2, i, 0:8], in0=gm[:, h % 2, i * 8:(i + 1) * 8],
                             scalar1=top[:, h % 2, i * 8 + 3:i * 8 + 4], scalar2=NEG, op0=ALU.is_lt,
                             op1=ALU.mult, signal=(i == 7))
                SEL1[h] = d

            def sel2(h):
                if h >= HA or not SEL:
                    return
                P.wait("pe", SEL1[h])
                for i in range(8):
                    tk = P.op("pe", "transpose", out=ps[:, 3072 + i * 128:3072 + (i + 1) * 128],
                              in_=negm[:, h % 2, i, :], identity=self.identf[:, :], signal=(i == 7))
                P.wait("act", tk)
                P.wait("act", HEADEND[h - 2])
                SEL2[h] = P.op("act", "activation", out=negmT[:, h % 2, :], in_=ps[:, 3072:4096],
                               func=AF.Copy, signal=True)

            def emitS(h, tile):
                qc, kt, q0, wq = tile
                si = st["si"]
                st["si"] += 1
                bank = 4 + si % 2
                P.wait("pe", RING[si - 2])
                P.wait("pe", LD[h])
                need_sel = SEL and qc >= 2 and kt < 4 * qc + 2
                diag = kt >= 4 * qc
                if need_sel:
                    P.wait("pe", SEL2[h])
                o = ps[:, bank * 512:bank * 512 + wq]
                tk = P.op("pe", "matmul", out=o, lhsT=ks[:, h % 3, kt * 128:(kt + 1) * 128],
                          rhs=qs[:, h % 3, q0:q0 + wq], start=True, stop=not (need_sel or diag),
                          signal=not (need_sel or diag))
                if need_sel:
                    j = kt // 2
                    tk = P.op("pe", "matmul", out=o, lhsT=self.selb[:, j * 128:(j + 1) * 128],
                              rhs=negmT[:, h % 2, q0 - 1024:q0 - 1024 + wq], start=False, stop=not diag,
                              signal=not diag)
                if diag:
                    tk = P.op("pe", "matmul", out=ps[:, bank * 512:bank * 512 + 128], lhsT=self.identb[:, :],
                              rhs=self.trib[:, :], start=False, stop=True, signal=True)
                return si, tk

            def emitExp(si, stok, wq):
                ei = st["ei"]
                st["ei"] += 1
                bank = 4 + si % 2
                P.wait("act", stok)
                P.wait("act", PVT[ei - 4])
                EXPT[ei] = P.op("act", "activation", out=eb[:, ei % 4, 0:wq], in_=ps[:, bank * 512:bank * 512 + wq],
                                func=AF.Exp, scale=scale, signal=True)
                RING[si] = EXPT[ei]
                return ei

            def emitPV(h, tile, ei):
                qc, kt, q0, wq = tile
                qci = st["qci"]
                ob = qci % 2
                g = (h * 128) // 512
                vs = st[("vs", g)]
                vc = (h * 128) % 512
                P.wait("pe", EXPT[ei])
                P.wait("pe", VL[g])
                if kt == 0:
                    P.wait("pe", NORM[qci - 2])
                c0 = q0 - 512 * qc
                last = (kt == 4 * qc + 3)
                P.op("pe", "matmul", out=ps[:, ob * 512 + c0:ob * 512 + c0 + wq],
                     lhsT=v4[:, vs, kt, vc:vc + 128], rhs=eb[:, ei % 4, 0:wq], start=(kt == 0), stop=last)
                PVT[ei] = P.op("pe", "matmul", out=ps[:, (2 + ob) * 512 + c0:(2 + ob) * 512 + c0 + wq],
                               lhsT=self.onesb[:, :], rhs=eb[:, ei % 4, 0:wq], start=(kt == 0), stop=last,
                               signal=True)
                if last:
                    P.wait("dve", PVT[ei])
                    P.wait("dve", NORM[qci - 2])
                    a = P.op("dve", "reciprocal", out=rden[:, ob, :], in_=ps[:, (2 + ob) * 512:(3 + ob) * 512],
                             signal=True)
                    P.wait("dve", a)
                    P.wait("dve", ST[qci - 2])
                    NORM[qci] = P.op("dve", "tensor_tensor", out=osb[:, ob, :], in0=ps[:, ob * 512:(ob + 1) * 512],
                                     in1=rden[:, ob, :], op=ALU.mult, signal=True)
                    P.wait("sp", NORM[qci])
                    ST[qci] = P.dma("sp", self.attnT[h * 128:(h + 1) * 128, qc * 512:(qc + 1) * 512],
                                    osb[:, ob, :], std[ob])
                    st["qci"] += 1

            loads(0)
            loads(1)
            sel1(0)
            sel2(0)
            for h in range(HA):
                loads(h + 2)
                sel1(h + 1)
                nt = len(tiles)
                pend = emitS(h, tiles[0])
                for i in range(nt):
                    si, stok = pend
                    ei = emitExp(si, stok, tiles[i][3])
                    if i + 1 < nt:
                        pend = emitS(h, tiles[i + 1])
                    emitPV(h, tiles[i], ei)
                    if i == 11:
                        sel2(h + 1)
                HEADEND[h] = PVT[st["ei"] - 1]
            P.run()

    def diff_phase(self, name):
        c = self.cfg
        nc = self.nc
        S, HA, HB, WA = c.S, c.HA, c.HB, c.WA
        P = Phase(self, name)
        scale = 128 ** -0.5
        vt = self.vtok.rearrange("(t p) c -> p t c", p=128)
        NT16 = S // 128
        q_base = 2 * HA
        k_base = 2 * HA + 2 * HB
        with ExitStack() as es:
            A_ = self.act
            qs = A_[:, 0:3 * S].rearrange("p (a s) -> p a s", a=3)
            ks = A_[:, 3 * S:6 * S].rearrange("p (a s) -> p a s", a=3)
            v2 = A_[:, 6 * S:6 * S + 2 * NT16 * 256].rearrange("p (a t c) -> p a t c", a=2, t=NT16)
            e0 = 6 * S + 2 * NT16 * 256
            eb = A_[:, e0:e0 + 2048].rearrange("p (a s) -> p a s", a=4)
            on0 = es.enter_context(nc.sbuf_tensor(f"{name}_on0", [128, 2, S], F32))
            rden = es.enter_context(nc.sbuf_tensor(f"{name}_rden", [128, 2, 512], F32))
            o1 = es.enter_context(nc.sbuf_tensor(f"{name}_o1", [128, 2, 2, 512], F32))
            comb = es.enter_context(nc.sbuf_tensor(f"{name}_comb", [128, 2, 2, 512], F32))
            sqb = es.enter_context(nc.sbuf_tensor(f"{name}_sqb", [128, 2, 2, 512], BF16))
            sd = es.enter_context(nc.sbuf_tensor(f"{name}_sd", [128, 512], F32))
            rs = es.enter_context(nc.sbuf_tensor(f"{name}_rs", [128, 512], F32))
            osb = es.enter_context(nc.sbuf_tensor(f"{name}_osb", [128, 2, 2, 512], BF16))
            ps = es.enter_context(nc.psum_tensor(f"{name}_ps", [128, 4096], F32))
            qld = [self.dsem(f"{name}_q{i}") for i in range(3)]
            vld = [self.dsem(f"{name}_v{i}") for i in range(2)]
            std = [self.dsem(f"{name}_st{i}") for i in range(2)]
            LD, VL, UEND, NORM, ST, FIN = (Hist() for _ in range(6))
            RING, EXPT, PVT = Hist(), Hist(), Hist()
            st = dict(si=0, ei=0, qci=0, tcount=0, fin=0)
            tiles = self.attn_tiles()
            pending = []
            NU = 2 * HB

            def loads(u):
                if u >= NU:
                    return
                h, m = u // 2, u % 2
                P.wait("sp", UEND[u - 3])
                P.dma("sp", qs[:, u % 3, :], self.qkT[q_base + u], qld[u % 3])
                LD[u] = P.dma("sp", ks[:, u % 3, :], self.qkT[k_base + u], qld[u % 3])
                if m == 0:
                    P.wait("sp", UEND[u - 3])
                    VL[h] = P.dma("sp", v2[:, h % 2, :, :], vt[:, :, WA + h * 256:WA + (h + 1) * 256], vld[h % 2])

            def emitS(u, tile):
                qc, kt, q0, wq = tile
                si = st["si"]
                st["si"] += 1
                bank = 6 + si % 2
                P.wait("pe", RING[si - 2])
                P.wait("pe", LD[u])
                diag = kt >= 4 * qc
                tk = P.op("pe", "matmul", out=ps[:, bank * 512:bank * 512 + wq],
                          lhsT=ks[:, u % 3, kt * 128:(kt + 1) * 128], rhs=qs[:, u % 3, q0:q0 + wq],
                          start=True, stop=not diag, signal=not diag)
                if diag:
                    tk = P.op("pe", "matmul", out=ps[:, bank * 512:bank * 512 + 128], lhsT=self.identb[:, :],
                              rhs=self.trib[:, :], start=False, stop=True, signal=True)
                return si, tk

            def emitExp(si, stok, wq):
                ei = st["ei"]
                st["ei"] += 1
                bank = 6 + si % 2
                P.wait("act", stok)
                P.wait("act", PVT[ei - 4])
                EXPT[ei] = P.op("act", "activation", out=eb[:, ei % 4, 0:wq], in_=ps[:, bank * 512:bank * 512 + wq],
                                func=AF.Exp, scale=scale, signal=True)
                RING[si] = EXPT[ei]
                return ei

            def fin_stage1(h, qc, fi):
                def f():
                    P.wait("act", NORM[("comb", fi)])
                    P.wait("act", FIN[("ssq", fi - 2)])
                    FIN[("sq", fi)] = P.op("act", "activation", out=sqb[:, fi % 2, :, :], in_=comb[:, fi % 2, :, :],
                                           func=AF.Square, signal=True)
                return f

            def fin_stage2(h, qc, fi):
                def f():
                    si = st["si"]
                    st["si"] += 1
                    bank = 6 + si % 2
                    P.wait("pe", RING[si - 2])
                    P.wait("pe", FIN[("sq", fi)])
                    P.op("pe", "matmul", out=ps[:, bank * 512:(bank + 1) * 512], lhsT=self.onesb[:, :],
                         rhs=sqb[:, fi % 2, 0, :], start=True, stop=False)
                    tk = P.op("pe", "matmul", out=ps[:, bank * 512:(bank + 1) * 512], lhsT=self.onesb[:, :],
                              rhs=sqb[:, fi % 2, 1, :], start=False, stop=True, signal=True)
                    FIN[("ssq", fi)] = tk
                    P.wait("act", tk)
                    P.wait("act", FIN[("rs", fi - 1)])
                    P.wait("act", FIN[("out", fi - 1)])
                    a = P.op("act", "activation", out=sd[:, :], in_=ps[:, bank * 512:(bank + 1) * 512],
                             func=AF.Ln, bias=EPS, scale=1.0 / 256, signal=True)
                    RING[si] = a
                    P.wait("act", a)
                    b = P.op("act", "activation", out=rs[:, :], in_=sd[:, :], func=AF.Exp, scale=-0.5,
                             signal=True)
                    FIN[("rs", fi)] = b
                    P.wait("dve", b)
                    P.wait("dve", ST[fi - 2])
                    for half in range(2):
                        d = P.op("dve", "scalar_tensor_tensor", out=osb[:, fi % 2, half, :],
                                 in0=comb[:, fi % 2, half, :],
                                 scalar=self.par[:, c.c_sub + half:c.c_sub + half + 1], in1=rs[:, :],
                                 op0=ALU.mult, op1=ALU.mult, signal=(half == 1))
                    FIN[("out", fi)] = d
                    P.wait("sp", d)
                    for half in range(2):
                        r0 = WA + h * 256 + half * 128
                        ST[fi] = P.dma("sp", self.attnT[r0:r0 + 128, qc * 512:(qc + 1) * 512],
                                       osb[:, fi % 2, half, :], std[fi % 2])
                return f

            def emitPV(u, tile, ei):
                h, m = u // 2, u % 2
                qc, kt, q0, wq = tile
                qci = st["qci"]
                ob = qci % 2
                P.wait("pe", EXPT[ei])
                P.wait("pe", VL[h])
                if kt == 0:
                    P.wait("pe", NORM[qci - 2])
                c0 = q0 - 512 * qc
                last = (kt == 4 * qc + 3)
                for half in range(2):
                    P.op("pe", "matmul", out=ps[:, (2 * ob + half) * 512 + c0:(2 * ob + half) * 512 + c0 + wq],
                         lhsT=v2[:, h % 2, kt, half * 128:(half + 1) * 128], rhs=eb[:, ei % 4, 0:wq],
                         start=(kt == 0), stop=last)
                PVT[ei] = P.op("pe", "matmul", out=ps[:, (4 + ob) * 512 + c0:(4 + ob) * 512 + c0 + wq],
                               lhsT=self.onesb[:, :], rhs=eb[:, ei % 4, 0:wq], start=(kt == 0), stop=last,
                               signal=True)
                if last:
                    P.wait("dve", PVT[ei])
                    a = P.op("dve", "reciprocal", out=rden[:, ob, :], in_=ps[:, (4 + ob) * 512:(5 + ob) * 512],
                             signal=True)
                    P.wait("dve", a)
                    if m == 0:
                        for half in range(2):
                            d = P.op("dve", "tensor_tensor", out=on0[:, half, qc * 512:(qc + 1) * 512],
                                     in0=ps[:, (2 * ob + half) * 512:(2 * ob + half + 1) * 512],
                                     in1=rden[:, ob, :], op=ALU.mult, signal=(half == 1))
                        NORM[qci] = d
                    else:
                        fi = st["fin"]
                        st["fin"] += 1
                        P.wait("dve", FIN[("out", fi - 2)])
                        P.wait("dve", FIN[("sq", fi - 2)])
                        for half in range(2):
                            d = P.op("dve", "tensor_tensor", out=o1[:, fi % 2, half, :],
                                     in0=ps[:, (2 * ob + half) * 512:(2 * ob + half + 1) * 512],
                                     in1=rden[:, ob, :], op=ALU.mult, signal=(half == 1))
                        NORM[qci] = d
                        P.wait("dve", d)
                        for half in range(2):
                            e = P.op("dve", "scalar_tensor_tensor", out=comb[:, fi % 2, half, :],
                                     in0=o1[:, fi % 2, half, :], scalar=self.lamc[:, 5:6],
                                     in1=on0[:, half, qc * 512:(qc + 1) * 512], op0=ALU.mult, op1=ALU.add,
                                     signal=(half == 1))
                        NORM[("comb", fi)] = e
                        pending.append((st["tcount"] + 8, fin_stage1(h, qc, fi)))
                        pending.append((st["tcount"] + 12, fin_stage2(h, qc, fi)))
                    st["qci"] += 1

            def run_pending(force=False):
                keep = []
                for due, f in pending:
                    if force or due <= st["tcount"]:
                        f()
                    else:
                        keep.append((due, f))
                pending[:] = keep

            loads(0)
            loads(1)
            for u in range(NU):
                loads(u + 2)
                nt = len(tiles)
                pend = emitS(u, tiles[0])
                for i in range(nt):
                    si, stok = pend
                    ei = emitExp(si, stok, tiles[i][3])
                    if i + 1 < nt:
                        pend = emitS(u, tiles[i + 1])
                    emitPV(u, tiles[i], ei)
                    st["tcount"] += 1
                    run_pending()
                UEND[u] = PVT[st["ei"] - 1]
            run_pending(force=True)
            P.run()


    def load_act_phase(self, name, srcT, nk):
        S = self.cfg.S
        P = Phase(self, name)
        ds = [self.dsem(f"{name}_l{i}") for i in range(4)]
        for k in range(nk):
            P.dma("sp", self.A(k, 0, S), srcT[k * 128:(k + 1) * 128, :], ds[k % 4])
        P.run()

    def proj_res_phase(self, name, wdram, nblk, nk, rhs_fn, res_dram, dst_dram, w_res=None, rstd_out=None, preload=None):
        c = self.cfg
        nc = self.nc
        S, D = c.S, c.D
        HT = S // 2
        NU = nblk * 2
        P = Phase(self, name)
        RW = 3
        with ExitStack() as es:
            if w_res is None:
                w = es.enter_context(nc.sbuf_tensor(f"{name}_w", [128, RW, nk * 128], BF16))
                wld = [self.dsem(f"{name}_w{i}") for i in range(RW)]
            rsb = es.enter_context(nc.sbuf_tensor(f"{name}_res", [128, 2, HT], F32))
            yb = es.enter_context(nc.sbuf_tensor(f"{name}_yb", [128, 2, HT], F32))
            sqb = es.enter_context(nc.sbuf_tensor(f"{name}_sqb", [128, 2, HT], BF16))
            sd = es.enter_context(nc.sbuf_tensor(f"{name}_sd", [128, S], F32))
            rs = es.enter_context(nc.sbuf_tensor(f"{name}_rs", [128, S], F32))
            ps = es.enter_context(nc.psum_tensor(f"{name}_ps", [128, 4096], F32))
            rld = [self.dsem(f"{name}_r{i}") for i in range(2)]
            std = [self.dsem(f"{name}_s{i}") for i in range(2)]
            MM, WL, RL, ADD, SQ, SS, ST = (Hist() for _ in range(7))
            pre_toks = []
            if preload is not None:
                pds = [self.dsem(f"{name}_pl{i}") for i in range(4)]
                for k in range(preload[1]):
                    P.dma("sp", self.A(k, 0, S), preload[0][k * 128:(k + 1) * 128, :], pds[k % 4])
                pre_toks = [(d_.h, d_.n) for d_ in pds]

            def emit_ss(v):
                if not (0 <= v < NU):
                    return
                ob, hh = v // 2, v % 2
                P.wait("pe", SQ[v])
                for t in range(HT // 512):
                    SS[v] = P.op("pe", "matmul", out=ps[:, 2048 + hh * HT + t * 512:2048 + hh * HT + (t + 1) * 512],
                                 lhsT=self.onesb[:, :], rhs=sqb[:, v % 2, t * 512:(t + 1) * 512],
                                 start=(ob == 0), stop=(ob == nblk - 1), signal=(t == HT // 512 - 1))

            for u in range(NU):
                ob, hh = u // 2, u % 2
                g = u % 2
                if w_res is None:
                    slot = ob % RW
                    if u == 0:
                        for o2 in range(min(RW - 1, nblk)):
                            WL[o2] = P.dma("pool", w[:, o2 % RW, :], wdram[o2], wld[o2 % RW])
                    if hh == 0 and ob + RW - 1 < nblk:
                        o2 = ob + RW - 1
                        P.wait("pool", MM[2 * (ob - 1) + 1])
                        WL[o2] = P.dma("pool", w[:, o2 % RW, :], wdram[o2], wld[o2 % RW])
                    P.wait("pe", WL[ob])
                    lhs = (lambda slot: (lambda k: w[:, slot, k * 128:(k + 1) * 128]))(slot)
                else:
                    lhs = (lambda ob: (lambda k: w_res(k, ob)))(ob)
                P.wait("sp", ADD[u - 2])
                RL[u] = P.dma("sp", rsb[:, g, :], res_dram[ob * 128:(ob + 1) * 128, hh * HT:(hh + 1) * HT], rld[g])
                P.wait("pe", ADD[u - 2])
                if u == 0:
                    for tk_ in pre_toks:
                        P.wait("pe", tk_)
                for k in range(nk):
                    for t in range(HT // 512):
                        tk = P.op("pe", "matmul", out=ps[:, g * HT + t * 512:g * HT + (t + 1) * 512],
                                  lhsT=lhs(k), rhs=rhs_fn(k, hh * HT + t * 512, hh * HT + (t + 1) * 512),
                                  start=(k == 0), stop=(k == nk - 1),
                                  signal=(k == nk - 1 and t == HT // 512 - 1))
                MM[u] = tk
                emit_ss(u - 1)
                P.wait("dve", MM[u])
                P.wait("dve", RL[u])
                P.wait("dve", ST[u - 2])
                P.wait("dve", SQ[u - 2])
                ADD[u] = P.op("dve", "tensor_tensor", out=yb[:, g, :], in0=ps[:, g * HT:(g + 1) * HT],
                              in1=rsb[:, g, :], op=ALU.add, signal=True)
                P.wait("act", ADD[u])
                P.wait("act", SS[u - 2])
                SQ[u] = P.op("act", "activation", out=sqb[:, g, :], in_=yb[:, g, :], func=AF.Square, signal=True)
                ST[u] = P.dma("act", dst_dram[ob * 128:(ob + 1) * 128, hh * HT:(hh + 1) * HT], yb[:, g, :], std[g])
            emit_ss(NU - 1)
            if rstd_out is not None:
                P.wait("act", SS[NU - 1])
                P.wait("act", SS[NU - 2])
                a1 = P.op("act", "activation", out=sd[:, :], in_=ps[:, 2048:2048 + S], func=AF.Ln, bias=EPS,
                          scale=1.0 / D, signal=True)
                P.wait("act", a1)
                a2 = P.op("act", "activation", out=rs[:, :], in_=sd[:, :], func=AF.Exp, scale=-0.5, signal=True)
                P.wait("sp", a2)
                P.dma("sp", rstd_out[:, :], rs[:, :], rld[0])
            P.run()

    def cross_phase(self):
        c = self.cfg
        nc = self.nc
        S, D, KC, M = c.S, c.D, c.KC, c.M
        scale = 128 ** -0.5
        with ExitStack() as es6:
            km = es6.enter_context(nc.sbuf_tensor("p6_km", [128, 4, M], BF16))
            vm = es6.enter_context(nc.sbuf_tensor("p6_vm", [128, M // 128, 512], BF16))
            with nc.sbuf_tensor("p6_memn", [128, KC, M], BF16) as memn:
                self.norm_phase("p6a", self.memT, M, c.c_gmem, dst_fn=lambda k: memn[:, k, :])
                mfn = lambda k, a, b: memn[:, k, a:b]
                self.qk_phase("p6b", self.wmk, 4, None, [(c.c_km, False, None, None)] * 4, S=M, rhs_fn=mfn,
                              dst_sb=lambda s, a, b: km[:, s, a:b], RW=2)
                self.v_phase("p6c", wdram=self.wmv, ncb=1, S=M, lhs_fn=mfn, dst_sb=lambda t: vm[:, t, :])
            qm = es6.enter_context(nc.sbuf_tensor("p6_qm", [128, 4, S], BF16))
            self.norm_phase("p6d", self.h1T, S, c.c_gcross, rstd_src=self.rstd1)
            self.qk_phase("p6e", self.wmq, 4, None, [(c.c_qm, False, None, None)] * 4,
                          dst_sb=lambda s, a, b: qm[:, s, a:b], RW=2)
            om = lambda k, a, b: self.act[:, k * S + a:k * S + b]
            wmo_off = 4 * S
            P = Phase(self, "p6f")
            with ExitStack() as es:
                eb = es.enter_context(nc.sbuf_tensor("p6_e", [128, 4, 512], BF16))
                rden = es.enter_context(nc.sbuf_tensor("p6_rden", [128, 2, 512], F32))
                ps = es.enter_context(nc.psum_tensor("p6_ps", [128, 4096], F32))
                wl = self.dsem("p6_wmo")
                WMO = P.dma("pool", self.act[:, wmo_off:wmo_off + 4 * D], self.wmo[:, :], wl)
                RING, EXPT, PVT, NORM = (Hist() for _ in range(4))
                si = ei = qci = 0
                MT = M // 128
                for h in range(4):
                    for qc in range(S // 512):
                        ob = qci % 2
                        pend = []
                        for mt in range(MT):
                            bank = 4 + si % 4
                            P.wait("pe", RING[si - 4])
                            tk = P.op("pe", "matmul", out=ps[:, bank * 512:(bank + 1) * 512],
                                      lhsT=km[:, h, mt * 128:(mt + 1) * 128], rhs=qm[:, h, qc * 512:(qc + 1) * 512],
                                      start=True, stop=True, signal=True)
                            P.wait("act", tk)
                            P.wait("act", PVT[ei - 4])
                            EXPT[ei] = P.op("act", "activation", out=eb[:, ei % 4, :], in_=ps[:, bank * 512:(bank + 1) * 512],
                                            func=AF.Exp, scale=scale, signal=True)
                            RING[si] = EXPT[ei]
                            pend.append((mt, ei))
                            si += 1
                            ei += 1
                        for mt, e in pend:
                            P.wait("pe", EXPT[e])
                            if mt == 0:
                                P.wait("pe", NORM[qci - 2])
                            P.op("pe", "matmul", out=ps[:, ob * 512:(ob + 1) * 512], lhsT=vm[:, mt, h * 128:(h + 1) * 128],
                                 rhs=eb[:, e % 4, :], start=(mt == 0), stop=(mt == MT - 1))
                            PVT[e] = P.op("pe", "matmul", out=ps[:, (2 + ob) * 512:(3 + ob) * 512], lhsT=self.onesb[:, :],
                                          rhs=eb[:, e % 4, :], start=(mt == 0), stop=(mt == MT - 1), signal=True)
                        P.wait("dve", PVT[pend[-1][1]])
                        a = P.op("dve", "reciprocal", out=rden[:, ob, :], in_=ps[:, (2 + ob) * 512:(3 + ob) * 512], signal=True)
                        P.wait("dve", a)
                        NORM[qci] = P.op("dve", "tensor_tensor", out=om(h, qc * 512, (qc + 1) * 512),
                                         in0=ps[:, ob * 512:(ob + 1) * 512], in1=rden[:, ob, :], op=ALU.mult, signal=True)
                        qci += 1
                P.run()
            self.proj_res_phase("p6g", None, KC, 4, om, self.h1T, self.h2T, rstd_out=self.rstd2,
                                w_res=lambda k, ob: self.act[:, wmo_off + k * D + ob * 128:wmo_off + k * D + (ob + 1) * 128])

    def ffn_up_phase(self, name):
        c = self.cfg
        nc = self.nc
        S, KC, FC = c.S, c.KC, c.FC
        P = Phase(self, name)
        RW = 4
        HT = S // 2
        with ExitStack() as es:
            w = es.enter_context(nc.sbuf_tensor(f"{name}_w", [128, RW, KC * 128], BF16))
            sg = es.enter_context(nc.sbuf_tensor(f"{name}_sg", [128, 2, HT], F32))
            ob_ = es.enter_context(nc.sbuf_tensor(f"{name}_o", [128, 2, HT], BF16))
            ps = es.enter_context(nc.psum_tensor(f"{name}_ps", [128, 4096], F32))
            wld = [self.dsem(f"{name}_w{i}") for i in range(RW)]
            std = [self.dsem(f"{name}_s{i}") for i in range(2)]
            MMG, MMU, WLG, WLU, SIL, MUL, ST = (Hist() for _ in range(7))
            n = 0
            for f in range(FC):
                sg_, su_ = (2 * f) % RW, (2 * f + 1) % RW
                P.wait("pool", MMG[f - 2])
                WLG[f] = P.dma("pool", w[:, sg_, :], self.wg[f], wld[sg_])
                P.wait("pool", MMU[f - 2])
                WLU[f] = P.dma("pool", w[:, su_, :], self.wu[f], wld[su_])
                for which, slot, WL, MMT in ((0, sg_, WLG, MMG), (1, su_, WLU, MMU)):
                    P.wait("pe", WL[f])
                    if which == 0:
                        P.wait("pe", SIL[2 * (f - 1) + 1])
                    else:
                        P.wait("pe", MUL[2 * (f - 1) + 1])
                    for k in range(KC):
                        for t in range(S // 512):
                            tk = P.op("pe", "matmul", out=ps[:, which * 2048 + t * 512:which * 2048 + (t + 1) * 512],
                                      lhsT=w[:, slot, k * 128:(k + 1) * 128], rhs=self.A(k, t * 512, (t + 1) * 512),
                                      start=(k == 0), stop=(k == KC - 1),
                                      signal=(k == KC - 1 and t == S // 512 - 1))
                    MMT[f] = tk
                for hh in range(2):
                    P.wait("act", MMG[f])
                    P.wait("act", MUL[n - 2])
                    SIL[n] = P.op("act", "activation", out=sg[:, n % 2, :], in_=ps[:, hh * HT:(hh + 1) * HT],
                                  func=AF.Silu, signal=True)
                    P.wait("dve", SIL[n])
                    P.wait("dve", MMU[f])
                    P.wait("dve", ST[n - 2])
                    MUL[n] = P.op("dve", "tensor_tensor", out=ob_[:, n % 2, :], in0=ps[:, 2048 + hh * HT:2048 + (hh + 1) * HT],
                                  in1=sg[:, n % 2, :], op=ALU.mult, signal=True)
                    P.wait("sp", MUL[n])
                    ST[n] = P.dma("sp", self.actT[f * 128:(f + 1) * 128, hh * HT:(hh + 1) * HT], ob_[:, n % 2, :], std[n % 2])
                    n += 1
            P.run()

    def ffn_down_phase(self, name):
        c = self.cfg
        nc = self.nc
        S, D, FC = c.S, c.D, c.FC
        P = Phase(self, name)
        NP = D // 512
        HT = S // 2
        RC = 4
        G = 8 if FC >= 16 else (4 if FC >= 8 else 1)
        kb = [int(round(i * FC / G)) for i in range(G + 1)]
        grp_of_end = {kb[g + 1] - 1: g for g in range(G)}
        grp_of_start = {kb[g]: g for g in range(G)}
        coff = FC * 512
        with ExitStack() as es:
            rsb = es.enter_context(nc.sbuf_tensor(f"{name}_res", [128, 4, HT], F32))
            yb = es.enter_context(nc.sbuf_tensor(f"{name}_yb", [128, 4, HT], F32))
            tmp = es.enter_context(nc.sbuf_tensor(f"{name}_tmp", [128, 2, HT], F32))
            ps = es.enter_context(nc.psum_tensor(f"{name}_ps", [128, 4096], F32))
            wt = lambda k, ob: self.act[:, k * 512 + ob * 128:k * 512 + (ob + 1) * 128]
            ch = lambda s, a, b_: self.act[:, coff + s * 4 * HT + a:coff + s * 4 * HT + b_]
            wld = [self.dsem(f"{name}_w{i}") for i in range(G)]
            cld = [self.dsem(f"{name}_c{i}") for i in range(RC)]
            rld = [self.dsem(f"{name}_r{i}") for i in range(4)]
            std = [self.dsem(f"{name}_s{i}") for i in range(4)]
            MM, WL, CL, RL, ADD, ST, FREE = (Hist() for _ in range(7))
            n = 0
            ei = 0
            for g in range(G):
                WL[(0, g)] = P.dma("pool", self.act[:, kb[g] * 512:kb[g + 1] * 512],
                                   self.wd[0][:, kb[g] * 512:kb[g + 1] * 512], wld[g])
            CK = 4
            NG = (FC + CK - 1) // CK
            for pb in range(NP):
                for th in range(2):
                    for kg in range(NG):
                        k0 = kg * CK
                        nk = min(CK, FC - k0)
                        cs = n % RC
                        if kg == max(0, NG - 6):
                            for j in range(4):
                                e = ei + j
                                P.wait("act", ADD[e - 4])
                                RL[e] = P.dma("act", rsb[:, e % 4, :],
                                              self.h2T[(pb * 4 + j) * 128:(pb * 4 + j + 1) * 128, th * HT:(th + 1) * HT], rld[e % 4])
                        P.wait("sp", MM[n - RC])
                        CL[n] = P.dma("sp", ch(cs, 0, nk * HT).rearrange("p (a t) -> p a t", a=nk),
                                      self.actT[k0 * 128:(k0 + nk) * 128, th * HT:(th + 1) * HT].rearrange("(a p) t -> p a t", p=128),
                                      cld[cs])
                        P.wait("pe", CL[n])
                        for kk in range(nk):
                            k = k0 + kk
                            if th == 0 and k in grp_of_start:
                                P.wait("pe", WL[(pb, grp_of_start[k])])
                            for ob in range(4):
                                if k == 0:
                                    P.wait("pe", FREE[ei - 4 + ob])
                                for t in range(HT // 512):
                                    endg = (th == 1 and k in grp_of_end and pb + 1 < NP)
                                    tk = P.op("pe", "matmul", out=ps[:, ob * HT + t * 512:ob * HT + (t + 1) * 512],
                                              lhsT=wt(k, ob), rhs=ch(cs, kk * HT + t * 512, kk * HT + (t + 1) * 512),
                                              start=(k == 0), stop=(k == FC - 1),
                                              signal=(ob == 3 and t == HT // 512 - 1 and (kk == nk - 1 or endg)))
                            if th == 1 and k in grp_of_end and pb + 1 < NP:
                                g = grp_of_end[k]
                                P.wait("pool", tk)
                                WL[(pb + 1, g)] = P.dma("pool", self.act[:, kb[g] * 512:kb[g + 1] * 512],
                                                        self.wd[pb + 1][:, kb[g] * 512:kb[g + 1] * 512], wld[g])
                        MM[n] = tk
                        n += 1
                    last = tk

                    for ob in (2, 3):
                        e = ei + ob
                        P.wait("act", last)
                        P.wait("act", ADD[e - 4])
                        FREE[e] = P.op("act", "activation", out=tmp[:, ob - 2, :], in_=ps[:, ob * HT:(ob + 1) * HT],
                                       func=AF.Copy, signal=True)
                    for ob in range(4):
                        r0 = (pb * 4 + ob) * 128
                        e = ei + ob
                        P.wait("dve", last)
                        P.wait("dve", RL[e])
                        P.wait("dve", ST[e - 4])
                        if ob < 2:
                            src_ap = ps[:, ob * HT:(ob + 1) * HT]
                        else:
                            P.wait("dve", FREE[e])
                            src_ap = tmp[:, ob - 2, :]
                        ADD[e] = P.op("dve", "tensor_tensor", out=yb[:, e % 4, :], in0=src_ap,
                                      in1=rsb[:, e % 4, :], op=ALU.add, signal=True)
                        if ob < 2:
                            FREE[e] = ADD[e]
                        P.wait("act", ADD[e])
                        ST[e] = P.dma("act", self.yT[r0:r0 + 128, th * HT:(th + 1) * HT], yb[:, e % 4, :], std[e % 4])
                    ei += 4
            P.run()

    def final_phase(self):
        if self.upto >= 8:
            return
        nc = self.nc
        P = Phase(self, "pfin")
        with nc.sbuf_tensor("fin_x", [128, self.cfg.S], F32) as fx:
            d1 = self.dsem("fin1")
            t = P.dma("sp", fx[:, :], self.xT[0:128, :], d1)
            P.wait("sp", t)
            P.dma("sp", self.yT[0:128, :], fx[:, :], d1)
            P.run()


def tile_w(w, C):
    K, N = w.shape
    return np.ascontiguousarray(
        w.reshape(K // 128, 128, N // C, C).transpose(2, 1, 0, 3).reshape(N // C, 128, (K // 128) * C))


def host_consts(c):
    S = c.S
    pos = np.arange(S, dtype=np.float32)
    inv = (np.float32(500000.0) ** (-np.arange(0, 32, 2, dtype=np.float32) / np.float32(32))).astype(np.float32)
    ang = (pos[:, None] * inv[None, :]).astype(np.float32)
    cos = np.cos(ang).astype(np.float32).T
    sin = np.sin(ang).astype(np.float32).T
    rope = np.zeros((128, 2 * S), np.float32)
    rope[0:16, 0:S] = cos
    rope[16:32, 0:S] = cos
    rope[0:16, S:] = sin
    rope[16:32, S:] = sin
    cst = np.zeros((128, 128 * 4 + 64), np.float32)
    cst[:, 0:128] = np.eye(128, dtype=np.float32)
    RT = np.zeros((128, 128), np.float32)
    for p in range(16):
        RT[p + 16, p] = -1.0
        RT[p, p + 16] = 1.0
    cst[:, 128:256] = RT
    tri = np.zeros((128, 128), np.float32)
    kk, qq = np.meshgrid(np.arange(128), np.arange(128), indexing="ij")
    tri[kk > qq] = NEG
    cst[:, 256:384] = tri
    cm = np.zeros((8, 8), np.float32)
    for qt in range(8, 16):
        i = qt // 2
        for j in range(8):
            cm[qt - 8, j] = 0.0 if j < i else (1e30 if j == i else -1e30)
    cst[:, 512:576] = cm.reshape(1, 64)
    return rope, cst


def host_prep(c, inp):
    D, KC = c.D, c.KC
    g = lambda k: np.asarray(inp[k], dtype=np.float32)[0]
    w_in = g("w_in")
    WA = c.WA
    WB = c.HB * 256
    o_qa, o_ka, o_va = 0, WA, 2 * WA
    o_qb, o_kb, o_vb = 3 * WA, 3 * WA + WB, 3 * WA + 2 * WB
    wqk_cols = np.concatenate([w_in[:, o_qa:o_qa + WA], w_in[:, o_ka:o_ka + WA],
                               w_in[:, o_qb:o_qb + WB], w_in[:, o_kb:o_kb + WB]], axis=1)
    wv_cols = np.concatenate([w_in[:, o_va:o_va + WA], w_in[:, o_vb:o_vb + WB]], axis=1)
    shared = {
        "wqk": tile_w(wqk_cols, 128),
        "wv": tile_w(wv_cols, 512),
        "wo": tile_w(g("w_out"), 128),
        "wmq": tile_w(g("w_mq"), 128),
        "wmk": tile_w(g("w_mk"), 128),
        "wmv": tile_w(g("w_mv"), 512),
        "wmo": np.ascontiguousarray(g("w_mo").reshape(4, 128, D).transpose(1, 0, 2).reshape(128, 4 * D)),
        "wg": tile_w(g("w_gate"), 128),
        "wu": tile_w(g("w_up"), 128),
        "wd": tile_w(g("w_down"), 512),
    }
    par = np.zeros((128, c.NPAR), np.float32)
    col = lambda v: v.reshape(-1, 128).T
    par[:, c.c_gmix:c.c_gmix + KC] = col(g("norm_mix_g"))
    par[:, c.c_gcross:c.c_gcross + KC] = col(g("norm_cross_g"))
    par[:, c.c_gmem:c.c_gmem + KC] = col(g("norm_mem_g"))
    par[:, c.c_gffn:c.c_gffn + KC] = col(g("norm_ffn_g"))
    par[:, c.c_qa] = g("q_norm_a")
    par[:, c.c_ka] = g("k_norm_a")
    par[:, c.c_qb] = g("q_norm_b")
    par[:, c.c_kb] = g("k_norm_b")
    par[:, c.c_qm] = g("q_norm_m")
    par[:, c.c_km] = g("k_norm_m")
    par[:, c.c_sub:c.c_sub + 2] = col(g("diff_subln_g"))
    par[:, c.c_lam + 0] = g("lam_q1")
    par[:, c.c_lam + 1] = g("lam_k1")
    par[:, c.c_lam + 2] = g("lam_q2")
    par[:, c.c_lam + 3] = g("lam_k2")
    rope, cst = host_consts(c)
    shared.update(par=par, rope=rope, cst=cst)
    x = np.asarray(inp["x"], dtype=np.float32)
    mem = np.asarray(inp["mem"], dtype=np.float32)
    in_maps = []
    for b in range(x.shape[0]):
        m = dict(shared)
        m["xT"] = np.ascontiguousarray(x[b].T)
        m["memT"] = np.ascontiguousarray(mem[b].T)
        in_maps.append(m)
    return in_maps


def run(cfg, inp, debug=False, upto=99, trace=False):
    K = Kern(cfg, debug=debug, upto=upto)
    nc = K.build()
    in_maps = host_prep(cfg, inp)
    res = run_bass_kernel_spmd(nc, in_maps, core_ids=list(range(len(in_maps))), trace=trace)
    return res


def kernel(**inputs):
    cfg = Cfg()
    res = run(cfg, inputs)
    out = np.stack([np.ascontiguousarray(r["yT"].T) for r in res.results], axis=0)
    return out.astype(np.float32)
```

```python
import math
from contextlib import ExitStack

import numpy as np

import concourse.bass as bass
import concourse.mybir as mybir
from concourse.bass_utils import run_bass_kernel_spmd

F32 = mybir.dt.float32
BF16 = mybir.dt.bfloat16
AF = mybir.ActivationFunctionType
ALU = mybir.AluOpType
AX = mybir.AxisListType

EPS = 1e-6
NEG = -30000.0
ENGS = ("pe", "act", "dve", "pool", "sp")


class Cfg:
    def __init__(self, D=4096, S=2048, DFF=11008, M=256):
        self.D, self.S, self.DFF, self.M = D, S, DFF, M
        self.KC = D // 128
        self.HA = D // 256
        self.HB = D // 512
        self.WA = self.HA * 128
        self.NQK = 2 * self.HA + 4 * self.HB
        self.NVB = (self.WA + self.HB * 256) // 512
        self.FC = DFF // 128
        self.MH = 4
        self.MW = 512
        self.NB = S // 256
        c = 0
        self.c_gmix = c; c += self.KC
        self.c_gcross = c; c += self.KC
        self.c_gmem = c; c += self.KC
        self.c_gffn = c; c += self.KC
        self.c_qa = c; c += 1
        self.c_ka = c; c += 1
        self.c_qb = c; c += 1
        self.c_kb = c; c += 1
        self.c_qm = c; c += 1
        self.c_km = c; c += 1
        self.c_sub = c; c += 2
        self.c_lam = c; c += 4
        self.NPAR = c


class DSem:
    def __init__(self, h):
        self.h = h
        self.n = 0


class Phase:
    def __init__(self, K, name):
        self.K = K
        self.nc = K.nc
        self.name = name
        self.q = {e: [] for e in ENGS}
        self.sem = K.eng_sem()
        self.cnt = K.eng_cnt
        self.waited = {e: {} for e in ENGS}
        self.dsems = {e: {} for e in ENGS}

    def op(self, eng, method, signal=False, **kw):
        if signal:
            self.cnt[eng] += 1
            self.q[eng].append(("i", method, kw, self.sem[eng], 1))
            return (self.sem[eng], self.cnt[eng])
        self.q[eng].append(("i", method, kw, None, 0))
        return None

    def wait(self, eng, tok):
        if tok is None:
            return
        sem, val = tok
        if self.waited[eng].get(sem.name, 0) >= val:
            return
        self.waited[eng][sem.name] = val
        self.q[eng].append(("w", sem, val))

    def dma(self, eng, out, in_, ds):
        ds.n += 16
        self.q[eng].append(("i", "dma_start", dict(out=out, in_=in_), ds.h, 16))
        self.dsems[eng][ds.h.name] = ds
        return (ds.h, ds.n)

    def run(self):
        nc = self.nc
        for e in ENGS:
            for ds in self.dsems[e].values():
                self.wait(e, (ds.h, ds.n))

        def replay(E, items):
            for it in items:
                if it[0] == "w":
                    E.wait_ge(it[1], it[2])
                else:
                    ins = getattr(E, it[1])(**it[2])
                    if it[3] is not None:
                        ins.then_inc(it[3], it[4])

        self.K.release_dsems()
        q = self.q
        with nc.Block() as block:
            @block.tensor
            def _(E):
                replay(E, q["pe"])

            @block.scalar
            def _(E):
                replay(E, q["act"])

            @block.vector
            def _(E):
                replay(E, q["dve"])

            @block.gpsimd
            def _(E):
                replay(E, q["pool"])

            @block.sync
            def _(E):
                replay(E, q["sp"])


class Hist:
    def __init__(self):
        self.d = {}

    def __setitem__(self, k, v):
        self.d[k] = v

    def __getitem__(self, k):
        return self.d.get(k)


class Kern:
    def __init__(self, cfg, debug=False, upto=99):
        self.cfg = cfg
        self.debug = debug
        self.upto = upto
        self.nc = bass.Bass("TRN2", target_bir_lowering=False)
        self.es = ExitStack()
        self.nsem = 0

    def new_sem(self, name):
        self.nsem += 1
        return self.es.enter_context(self.nc.semaphore(f"s{self.nsem}_{name}"))

    def eng_sem(self):
        if not hasattr(self, "_eng_sem"):
            self._eng_sem = {e: self.new_sem(f"eng_{e}") for e in ENGS}
            self.eng_cnt = {e: 0 for e in ENGS}
        return self._eng_sem

    def dsem(self, name):
        if not hasattr(self, "_free_ds"):
            self._free_ds, self._used_ds = [], []
        d = self._free_ds.pop() if self._free_ds else DSem(self.new_sem("dma"))
        self._used_ds.append(d)
        return d

    def release_dsems(self):
        if hasattr(self, "_free_ds"):
            self._free_ds.extend(self._used_ds)
            self._used_ds = []

    def dram_in(self, name, shape, dt=F32):
        return self.nc.dram_tensor(name, list(shape), dt, kind="ExternalInput").ap()

    def dram_scratch(self, name, shape, dt):
        kind = "ExternalOutput" if self.debug else "Internal"
        return self.nc.dram_tensor(name, list(shape), dt, kind=kind).ap()

    def sb(self, name, shape, dt):
        return self.es.enter_context(self.nc.sbuf_tensor(name, list(shape), dt))

    def A(self, k, a, b):
        S = self.cfg.S
        return self.act[:, k * S + a:k * S + b]

    def build(self):
        c = self.cfg
        nc = self.nc
        D, S, KC = c.D, c.S, c.KC
        with self.es:
            self.xT = self.dram_in("xT", [D, S])
            self.memT = self.dram_in("memT", [D, c.M])
            self.par_d = self.dram_in("par", [128, c.NPAR])
            self.rope_d = self.dram_in("rope", [128, 2 * S])
            self.cst_d = self.dram_in("cst", [128, 128 * 4 + 64])
            self.wqk = self.dram_in("wqk", [c.NQK, 128, KC * 128])
            self.wv = self.dram_in("wv", [c.NVB, 128, KC * 512])
            self.wo = self.dram_in("wo", [KC, 128, KC * 128])
            self.wmq = self.dram_in("wmq", [4, 128, KC * 128])
            self.wmk = self.dram_in("wmk", [4, 128, KC * 128])
            self.wmv = self.dram_in("wmv", [1, 128, KC * 512])
            self.wmo = self.dram_in("wmo", [128, 4 * D])
            self.wg = self.dram_in("wg", [c.FC, 128, KC * 128])
            self.wu = self.dram_in("wu", [c.FC, 128, KC * 128])
            self.wd = self.dram_in("wd", [D // 512, 128, c.FC * 512])
            self.yT = nc.dram_tensor("yT", [D, S], F32, kind="ExternalOutput").ap()

            self.qkT = self.dram_scratch("qkT", [c.NQK, 128, S], BF16)
            self.vtok = self.dram_scratch("vtok", [S, c.NVB * 512], BF16)
            self.attnT = self.dram_scratch("attnT", [D, S], BF16)
            self.h1T = self.dram_scratch("h1T", [D, S], F32)
            self.h2T = self.dram_scratch("h2T", [D, S], F32)
            self.actT = self.dram_scratch("actT", [c.DFF, S], BF16)
            self.qaf = self.dram_scratch("qaf", [c.HA, 128, S // 2], F32)
            self.rstd1 = self.dram_scratch("rstd1", [128, S], F32)
            self.rstd2 = self.dram_scratch("rstd2", [128, S], F32)

            self.act_elems = max(KC * S, c.FC * 512 + 16 * (S // 2), 4 * S + 4 * D, 32768)
            self.act = self.sb("act", [128, self.act_elems], BF16)
            self.par = self.sb("par_sb", [128, c.NPAR], F32)
            self.identf = self.sb("identf", [128, 128], F32)
            self.RT = self.sb("RT", [128, 128], F32)
            self.trif = self.sb("trif", [128, 128], F32)
            self.cmask = self.sb("cmask", [128, 64], F32)
            self.identb = self.sb("identb", [128, 128], BF16)
            self.trib = self.sb("trib", [128, 128], BF16)
            self.onesb = self.sb("onesb", [128, 128], BF16)
            self.onesf = self.sb("onesf", [128, 128], F32)
            self.selb = self.sb("selb", [128, 8 * 128], BF16)
            self.kmt = self.sb("kmt", [128, c.HA * 8], F32)
            self.kmb = self.sb("kmb", [128, c.HA * 8], BF16)
            self.lamc = self.sb("lamc", [128, 8], F32)
            self.es_early = ExitStack()
            self.rope = self.es_early.enter_context(nc.sbuf_tensor("rope_sb", [128, 2 * S], F32))
            self.RTb = self.es_early.enter_context(nc.sbuf_tensor("RTb", [128, 128], BF16))

            self.phase0()
            if self.upto >= 1:
                self.norm_phase("p1", self.xT, S, c.c_gmix)
            if self.upto >= 2:
                self.qk_phase("p2a", self.wqk, c.NQK, self.qkT, self.slot_kinds_main())
                self.es_early.close()
                self.v_phase("p2b", RW=2)
            if self.upto >= 3:
                self.moba_phase("p3")
            if self.upto >= 4:
                self.diff_phase("p4")
            if self.upto < 2:
                self.es_early.close()
            if self.upto >= 5:
                self.proj_res_phase("p5", self.wo, KC, KC, lambda k, a, b: self.A(k, a, b), self.xT, self.h1T,
                                    rstd_out=self.rstd1, preload=(self.attnT, KC))
            if self.upto >= 6:
                self.cross_phase()
            if self.upto >= 7:
                self.norm_phase("p7a", self.h2T, S, c.c_gffn, rstd_src=self.rstd2)
                self.ffn_up_phase("p7")
            if self.upto >= 8:
                self.ffn_down_phase("p8")
            self.final_phase()
        return nc

    def phase0(self):
        c = self.cfg
        P = Phase(self, "p0")
        d = self.dsem("p0ld")
        P.dma("sp", self.par[:, :], self.par_d[:, :], d)
        P.dma("sp", self.rope[:, :], self.rope_d[:, :], d)
        P.dma("sp", self.identf[:, :], self.cst_d[:, 0:128], d)
        P.dma("sp", self.RT[:, :], self.cst_d[:, 128:256], d)
        P.dma("sp", self.trif[:, :], self.cst_d[:, 256:384], d)
        tk = P.dma("sp", self.cmask[:, :], self.cst_d[:, 512:576], d)
        P.wait("act", tk)
        P.op("act", "activation", out=self.RTb[:, :], in_=self.RT[:, :], func=AF.Copy)
        P.wait("dve", tk)
        P.op("dve", "memset", ap=self.onesb[:, :], constant=1.0)
        P.op("dve", "memset", ap=self.onesf[:, :], constant=1.0)
        P.op("dve", "memset", ap=self.selb[:, :], constant=0.0)
        P.op("dve", "memset", ap=self.kmt[:, :], constant=0.0)
        P.op("dve", "tensor_copy", out=self.identb[:, :], in_=self.identf[:, :])
        t0 = P.op("dve", "tensor_copy", out=self.trib[:, :], in_=self.trif[:, :], signal=True)
        P.wait("dve", t0)
        for j in range(8):
            P.op("dve", "tensor_scalar", out=self.selb[:, j * 128:(j + 1) * 128],
                 in0=self.onesb[:, :], scalar1=self.identf[:, j:j + 1], scalar2=None,
                 op0=ALU.mult)
        cl = c.c_lam
        P.op("dve", "tensor_tensor", out=self.lamc[:, 0:1], in0=self.par[:, cl:cl + 1],
             in1=self.par[:, cl + 1:cl + 2], op=ALU.mult)
        t1 = P.op("dve", "tensor_tensor", out=self.lamc[:, 1:2], in0=self.par[:, cl + 2:cl + 3],
                  in1=self.par[:, cl + 3:cl + 4], op=ALU.mult, signal=True)
        with self.nc.psum_tensor("ps0", [128, 512], F32) as ps0:
            P.wait("pe", t1)
            t2 = P.op("pe", "matmul", out=ps0[:, 0:2], lhsT=self.onesf[:, :], rhs=self.lamc[:, 0:2],
                      start=True, stop=True, signal=True)
            P.wait("act", t2)
            t3 = P.op("act", "activation", out=self.lamc[:, 2:4], in_=ps0[:, 0:2], func=AF.Exp,
                      signal=True)
            P.wait("dve", t3)
            t4 = P.op("dve", "tensor_tensor", out=self.lamc[:, 4:5], in0=self.lamc[:, 3:4],
                      in1=self.lamc[:, 2:3], op=ALU.subtract, signal=True)
            P.wait("dve", t4)
            t5 = P.op("dve", "tensor_scalar", out=self.lamc[:, 5:6], in0=self.lamc[:, 4:5],
                      scalar1=-self.lam_init(), scalar2=None, op0=ALU.add, signal=True)
            P.op("dve", "tensor_scalar", out=self.par[:, c.c_sub:c.c_sub + 2],
                 in0=self.par[:, c.c_sub:c.c_sub + 2], scalar1=1.0 - self.lam_init(), scalar2=None,
                 op0=ALU.mult)
            P.run()

    @staticmethod
    def lam_init():
        return 0.8 - 0.6 * math.exp(-0.3 * 0)

    def norm_phase(self, name, src, S, gcol, dst_fn=None, rstd_src=None):
        c = self.cfg
        KC, D = c.KC, c.D
        nc = self.nc
        dst_fn = (lambda k: self.A(k, 0, S)) if dst_fn is None else dst_fn
        P = Phase(self, name)
        R = 4
        NW = 512 if S >= 512 else S
        NT = S // NW
        with ExitStack() as es:
            xs = es.enter_context(nc.sbuf_tensor(f"{name}_xs", [128, R, S], F32))
            if rstd_src is None:
                sq = es.enter_context(nc.sbuf_tensor(f"{name}_sq", [128, 2, S], BF16))
                sd = es.enter_context(nc.sbuf_tensor(f"{name}_sd", [128, S], F32))
                ps = es.enter_context(nc.psum_tensor(f"{name}_ps", [128, max(S, 512)], F32))
            rstd = es.enter_context(nc.sbuf_tensor(f"{name}_rstd", [128, S], F32))
            lds = [self.dsem(f"{name}_ld{i}") for i in range(R)]
            sqt, mmt, use = Hist(), Hist(), Hist()
            n = 0
            if rstd_src is not None:
                t2 = P.dma("sp", rstd[:, :], rstd_src[:, :], self.dsem(f"{name}_rl"))
            else:
                for k in range(KC):
                    slot = n % R
                    P.wait("sp", use[n - R])
                    ld = P.dma("sp", xs[:, slot, :], src[k * 128:(k + 1) * 128, :], lds[slot])
                    P.wait("act", ld)
                    P.wait("act", mmt[k - 2])
                    sqt[k] = P.op("act", "activation", out=sq[:, k % 2, :], in_=xs[:, slot, :],
                                  func=AF.Square, signal=True)
                    use[n] = sqt[k]
                    P.wait("pe", sqt[k])
                    for t in range(NT):
                        tk = P.op("pe", "matmul", out=ps[:, t * NW:(t + 1) * NW], lhsT=self.onesb[:, :],
                                  rhs=sq[:, k % 2, t * NW:(t + 1) * NW], start=(k == 0),
                                  stop=(k == KC - 1), signal=(t == NT - 1))
                    mmt[k] = tk
                    n += 1
                P.wait("act", mmt[KC - 1])
                t1 = P.op("act", "activation", out=sd[:, :], in_=ps[:, 0:S], func=AF.Sqrt, bias=EPS,
                          scale=1.0 / D, signal=True)
                P.wait("dve", t1)
                t2 = P.op("dve", "reciprocal", out=rstd[:, :], in_=sd[:, :], signal=True)
            for k in range(KC):
                slot = n % R
                P.wait("sp", use[n - R])
                ld = P.dma("sp", xs[:, slot, :], src[k * 128:(k + 1) * 128, :], lds[slot])
                P.wait("dve", ld)
                P.wait("dve", t2)
                use[n] = P.op("dve", "scalar_tensor_tensor", out=dst_fn(k), in0=xs[:, slot, :],
                              scalar=self.par[:, gcol + k:gcol + k + 1], in1=rstd[:, :],
                              op0=ALU.mult, op1=ALU.mult, signal=True)
                n += 1
            P.run()

    def slot_kinds_main(self):
        c = self.cfg
        kinds = []
        for h in range(c.HA):
            kinds.append((c.c_qa, True, None, h))
        for h in range(c.HA):
            kinds.append((c.c_ka, True, h, None))
        for h in range(2 * c.HB):
            kinds.append((c.c_qb, True, None, None))
        for h in range(2 * c.HB):
            kinds.append((c.c_kb, True, None, None))
        return kinds

    def qk_phase(self, name, wdram, nslots, dst, kinds, S=None, rhs_fn=None, dst_sb=None, RW=3):
        c = self.cfg
        nc = self.nc
        KC = c.KC
        S = c.S if S is None else S
        rhs_fn = self.A if rhs_fn is None else rhs_fn
        TW = min(1024, S)
        NH = S // TW
        NW = min(512, TW)
        NTT = TW // NW
        NU = nslots * NH
        P = Phase(self, name)
        with ExitStack() as es:
            w = es.enter_context(nc.sbuf_tensor(f"{name}_w", [128, RW, KC * 128], BF16))
            sqb = es.enter_context(nc.sbuf_tensor(f"{name}_sqb", [128, 2, TW], BF16))
            sd = es.enter_context(nc.sbuf_tensor(f"{name}_sd", [128, TW], F32))
            rs = es.enter_context(nc.sbuf_tensor(f"{name}_rs", [128, TW], F32))
            qn = es.enter_context(nc.sbuf_tensor(f"{name}_qn", [128, 2, TW], F32))
            t2b = es.enter_context(nc.sbuf_tensor(f"{name}_t2", [128, TW], F32))
            qkb = es.enter_context(nc.sbuf_tensor(f"{name}_qkb", [128, 2, TW], BF16))
            any_rope = any(k_[1] for k_ in kinds)
            if any_rope:
                hib = es.enter_context(nc.sbuf_tensor(f"{name}_hib", [128, TW], BF16))
                lob = es.enter_context(nc.sbuf_tensor(f"{name}_lob", [128, TW], BF16))
            ps = es.enter_context(nc.psum_tensor(f"{name}_ps", [128, 4096], F32))
            psP = lambda g: ps[:, g * 1024:g * 1024 + TW]
            psS = ps[:, 2048:2048 + TW]
            psR = ps[:, 3072:3072 + TW]
            wld = [self.dsem(f"{name}_w{i}") for i in range(RW)]
            std = [self.dsem(f"{name}_st{i}") for i in range(2)]
            A, B1, C, D1, D2, E, Fm, G, H, ST, WL, GL, STF, HI, LO = (Hist() for _ in range(15))
            ZT = None
            if any_rope:
                P.op("dve", "memset", ap=hib[:, :], constant=0.0)
                ZT = P.op("dve", "memset", ap=lob[:, :], constant=0.0, signal=True)
            stf = [self.dsem(f"{name}_stf{i}") for i in range(2)]
            KH = KC // 2 if KC >= 2 else KC

            def emitA(u, k0, k1, last):
                hs = u // NH
                tb = (u % NH) * TW
                slot = hs % RW
                g = u % 2
                tk = None
                for k in range(k0, k1):
                    for t in range(NTT):
                        tk = P.op("pe", "matmul", out=ps[:, g * 1024 + t * NW:g * 1024 + (t + 1) * NW],
                                  lhsT=w[:, slot, k * 128:(k + 1) * 128],
                                  rhs=rhs_fn(k, tb + t * NW, tb + (t + 1) * NW),
                                  start=(k == 0), stop=(k == KC - 1),
                                  signal=(last and k == k1 - 1 and t == NTT - 1))
                return tk

            for step in range(NU + 3):
                u = step
                if u < NU and u % NH == 0:
                    hs = u // NH
                    slot = hs % RW
                    P.wait("pool", A[(hs - RW) * NH + NH - 1])
                    WL[hs] = P.dma("pool", w[:, slot, :], wdram[hs], wld[slot])
                v = u - 2
                if 0 <= v < NU and kinds[v // NH][1]:
                    P.wait("pe", LO[v])
                    P.wait("pe", ZT)
                    P.wait("pe", G[v - 1])
                    for t in range(NTT):
                        P.op("pe", "matmul", out=ps[:, 3072 + t * NW:3072 + (t + 1) * NW],
                             lhsT=self.RTb[:, :], rhs=hib[:, t * NW:(t + 1) * NW], start=True, stop=False)
                        Fm[v] = P.op("pe", "matmul", out=ps[:, 3072 + t * NW:3072 + (t + 1) * NW],
                                     lhsT=self.RTb[:, :], rhs=lob[:, t * NW:(t + 1) * NW],
                                     start=False, stop=True, signal=(t == NTT - 1))
                if u < NU:
                    P.wait("pe", WL[u // NH])
                    P.wait("pe", E[u - 2])
                    emitA(u, 0, KH, KH == KC)
                    if KH == KC:
                        A[u] = (P.sem["pe"], P.cnt["pe"])
                v = u - 1
                if 0 <= v < NU:
                    P.wait("pe", B1[v])
                    P.wait("pe", D1[v - 1])
                    for t in range(NTT):
                        C[v] = P.op("pe", "matmul", out=ps[:, 2048 + t * NW:2048 + (t + 1) * NW],
                                    lhsT=self.onesb[:, :], rhs=sqb[:, v % 2, t * NW:(t + 1) * NW],
                                    start=True, stop=True, signal=(t == NTT - 1))
                if u < NU and KH < KC:
                    A[u] = emitA(u, KH, KC, True)
                v = u - 2
                if 0 <= v < NU:
                    gc, rope, kmh, qfh = kinds[v // NH]
                    tb = (v % NH) * TW
                    last = E[v]
                    if rope:
                        P.wait("dve", Fm[v])
                        P.wait("dve", E[v])
                        g1 = P.op("dve", "tensor_tensor", out=qn[0:32, v % 2, :], in0=qn[0:32, v % 2, :],
                                  in1=self.rope[0:32, tb:tb + TW], op=ALU.mult, signal=True)
                        g2 = P.op("dve", "tensor_tensor", out=t2b[0:32, :], in0=psR[0:32, :],
                                  in1=self.rope[0:32, c.S + tb:c.S + tb + TW], op=ALU.mult,
                                  signal=True)
                        P.wait("dve", g2)
                        last = P.op("dve", "tensor_tensor", out=qn[0:32, v % 2, :],
                                    in0=qn[0:32, v % 2, :], in1=t2b[0:32, :], op=ALU.add, signal=True)
                        G[v] = last
                    if kmh is not None:
                        P.wait("dve", last)
                        nb = TW // 256
                        b0 = kmh * 8 + tb // 256
                        last = P.op("dve", "tensor_reduce", out=self.kmt[:, b0:b0 + nb],
                                    in_=qn[:, v % 2, :].rearrange("p (b n) -> p b n", n=256),
                                    axis=AX.X, op=ALU.add, signal=True)
                        G[v] = last
                    GL[v] = last
                    if qfh is not None and tb >= c.S // 2 and c.NB > 4:
                        P.wait("sp", last)
                        STF[v] = P.dma("sp", self.qaf[qfh][:, tb - c.S // 2:tb - c.S // 2 + TW], qn[:, v % 2, :], stf[v % 2])
                v = u - 1
                if 0 <= v < NU:
                    P.wait("act", C[v])
                    P.wait("act", E[v - 1])
                    D1[v] = P.op("act", "activation", out=sd[:, :], in_=psS, func=AF.Ln, bias=EPS,
                                 scale=1.0 / 128, signal=True)
                    P.wait("act", D1[v])
                    D2[v] = P.op("act", "activation", out=rs[:, :], in_=sd[:, :], func=AF.Exp, scale=-0.5,
                                 signal=True)
                    P.wait("dve", D2[v])
                    P.wait("dve", A[v])
                    P.wait("dve", H[v - 2])
                    P.wait("dve", Fm[v - 2])
                    P.wait("dve", STF[v - 2])
                    gc = kinds[v // NH][0]
                    E[v] = P.op("dve", "scalar_tensor_tensor", out=qn[:, v % 2, :], in0=psP(v % 2),
                                scalar=self.par[:, gc:gc + 1], in1=rs[:, :], op0=ALU.mult,
                                op1=ALU.mult, signal=True)
                    if kinds[v // NH][1]:
                        P.wait("act", E[v])
                        P.wait("act", Fm[v - 1])
                        HI[v] = P.op("act", "activation", out=hib[0:32, :], in_=qn[0:32, v % 2, :],
                                     func=AF.Copy, signal=True)
                        P.wait("dve", HI[v])
                        P.wait("dve", Fm[v - 1])
                        LO[v] = P.op("dve", "tensor_tensor", out=lob[0:32, :], in0=qn[0:32, v % 2, :],
                                     in1=hib[0:32, :], op=ALU.subtract, signal=True)
                v = u - 2
                if 0 <= v < NU:
                    tb = (v % NH) * TW
                    P.wait("act", GL[v])
                    P.wait("act", ST[v - 2])
                    if dst_sb is not None:
                        H[v] = P.op("act", "activation", out=dst_sb(v // NH, tb, tb + TW), in_=qn[:, v % 2, :],
                                    func=AF.Copy, signal=True)
                    else:
                        H[v] = P.op("act", "activation", out=qkb[:, v % 2, :], in_=qn[:, v % 2, :],
                                    func=AF.Copy, signal=True)
                        P.wait("sp", H[v])
                        ST[v] = P.dma("sp", dst[v // NH][:, tb:tb + TW], qkb[:, v % 2, :], std[v % 2])
                if u < NU:
                    P.wait("act", A[u])
                    P.wait("act", C[u - 2])
                    B1[u] = P.op("act", "activation", out=sqb[:, u % 2, :], in_=psP(u % 2),
                                 func=AF.Square, signal=True)
            P.run()

    def v_phase(self, name, wdram=None, ncb=None, S=None, lhs_fn=None, dst_sb=None, RW=1):
        c = self.cfg
        nc = self.nc
        KC = c.KC
        S = c.S if S is None else S
        wdram = self.wv if wdram is None else wdram
        ncb = c.NVB if ncb is None else ncb
        lhs_fn = self.A if lhs_fn is None else lhs_fn
        P = Phase(self, name)
        NTK = S // 128
        with ExitStack() as es:
            w = es.enter_context(nc.sbuf_tensor(f"{name}_w", [128, RW, KC * 512], BF16))
            vsb = es.enter_context(nc.sbuf_tensor(f"{name}_vsb", [128, 4, 512], BF16))
            ps = es.enter_context(nc.psum_tensor(f"{name}_ps", [128, 4096], F32))
            wld = [self.dsem(f"{name}_w{i}") for i in range(RW)]
            std = [self.dsem(f"{name}_st{i}") for i in range(4)]
            MM, EV, ST, CBEND, WLD = Hist(), Hist(), Hist(), Hist(), Hist()
            n = 0
            for cb in range(min(RW, ncb)):
                WLD[cb] = P.dma("pool", w[:, cb % RW, :], wdram[cb], wld[cb % RW])
            for cb in range(ncb):
                ws = cb % RW
                P.wait("pe", WLD[cb])
                for t in range(NTK):
                    b = n % 8
                    P.wait("pe", EV[n - 8])
                    for k in range(KC):
                        tk = P.op("pe", "matmul", out=ps[:, b * 512:(b + 1) * 512],
                                  lhsT=lhs_fn(k, t * 128, (t + 1) * 128),
                                  rhs=w[:, ws, k * 512:(k + 1) * 512], start=(k == 0), stop=(k == KC - 1),
                                  signal=(k == KC - 1))
                    MM[n] = tk
                    eng = "act" if n % 2 == 0 else "dve"
                    P.wait(eng, MM[n])
                    P.wait(eng, ST[n - 4])
                    o = vsb[:, n % 4, :] if dst_sb is None else dst_sb(t)
                    if eng == "act":
                        EV[n] = P.op("act", "activation", out=o, in_=ps[:, b * 512:(b + 1) * 512],
                                     func=AF.Copy, signal=True)
                    else:
                        EV[n] = P.op("dve", "tensor_copy", out=o, in_=ps[:, b * 512:(b + 1) * 512],
                                     signal=True)
                    if dst_sb is None:
                        P.wait("sp", EV[n])
                        ST[n] = P.dma("sp", self.vtok[t * 128:(t + 1) * 128, cb * 512:(cb + 1) * 512],
                                      vsb[:, n % 4, :], std[n % 4])
                    n += 1
                CBEND[cb] = MM[n - 1]
                if cb + RW < ncb:
                    P.wait("pool", CBEND[cb])
                    WLD[cb + RW] = P.dma("pool", w[:, ws, :], wdram[cb + RW], wld[ws])
            P.run()


    def attn_tiles(self):
        S = self.cfg.S
        out = []
        for qc in range(S // 512):
            for kt in range(4 * qc + 4):
                q0 = max(512 * qc, 128 * kt)
                out.append((qc, kt, q0, 512 * (qc + 1) - q0))
        return out

    def moba_phase(self, name):
        c = self.cfg
        nc = self.nc
        S, HA = c.S, c.HA
        P = Phase(self, name)
        scale = 128 ** -0.5
        vt = self.vtok.rearrange("(t p) c -> p t c", p=128)
        NT16 = S // 128
        SEL = (c.NB > 4)
        with ExitStack() as es:
            A_ = self.act
            qs = A_[:, 0:3 * S].rearrange("p (a s) -> p a s", a=3)
            ks = A_[:, 3 * S:6 * S].rearrange("p (a s) -> p a s", a=3)
            v4 = A_[:, 6 * S:6 * S + 2 * NT16 * 512].rearrange("p (a t c) -> p a t c", a=2, t=NT16)
            e0 = 6 * S + 2 * NT16 * 512
            eb = A_[:, e0:e0 + 2048].rearrange("p (a s) -> p a s", a=4)
            gm = es.enter_context(nc.sbuf_tensor(f"{name}_gm", [128, 2, 64], F32))
            top = es.enter_context(nc.sbuf_tensor(f"{name}_top", [128, 2, 64], F32))
            negm = es.enter_context(nc.sbuf_tensor(f"{name}_negm", [128, 2, 8, 128], F32))
            negmT = es.enter_context(nc.sbuf_tensor(f"{name}_negmT", [128, 2, 1024], BF16))
            rden = es.enter_context(nc.sbuf_tensor(f"{name}_rden", [128, 2, 512], F32))
            osb = es.enter_context(nc.sbuf_tensor(f"{name}_osb", [128, 2, 512], BF16))
            qf = es.enter_context(nc.sbuf_tensor(f"{name}_qf", [128, 2, S // 2], F32))
            ps = es.enter_context(nc.psum_tensor(f"{name}_ps", [128, 4096], F32))
            qfl = [self.dsem(f"{name}_qf{i}") for i in range(2)]
            QFL, GT = Hist(), Hist()
            qld = [self.dsem(f"{name}_q{i}") for i in range(3)]
            vld = [self.dsem(f"{name}_v{i}") for i in range(2)]
            std = [self.dsem(f"{name}_st{i}") for i in range(2)]
            t0 = P.op("dve", "memset", ap=negm[:, :, :, :], constant=0.0)
            t0 = P.op("dve", "tensor_copy", out=self.kmb[:, :], in_=self.kmt[:, :], signal=True)
            LD, VL, HEADEND, SEL1, SEL2, NORM, ST = (Hist() for _ in range(7))
            RING, EXPT, PVT = Hist(), Hist(), Hist()
            st = dict(si=0, ei=0, qci=0, vg=-1, vn=0)
            tiles = self.attn_tiles()

            def loads(h):
                if h >= HA:
                    return
                P.wait("sp", HEADEND[h - 3])
                P.dma("sp", qs[:, h % 3, :], self.qkT[h], qld[h % 3])
                LD[h] = P.dma("sp", ks[:, h % 3, :], self.qkT[HA + h], qld[h % 3])
                if SEL:
                    P.wait("sp", GT[h - 2])
                    QFL[h] = P.dma("sp", qf[:, h % 2, :], self.qaf[h], qfl[h % 2])
                g = (h * 128) // 512
                if g != st["vg"]:
                    st["vg"] = g
                    st["vn"] += 1
                    vs = st["vn"] % 2
                    P.wait("sp", HEADEND[(g - 1) * 4 - 1] if g >= 2 else None)
                    VL[g] = P.dma("sp", v4[:, vs, :, :], vt[:, :, g * 512:(g + 1) * 512], vld[vs])
                    st[("vs", g)] = vs

            def sel1(h):
                if h >= HA or not SEL:
                    return
                P.wait("pe", QFL[h])
                P.wait("pe", SEL2[h - 1])
                for i in range(8):
                    tk = P.op("pe", "matmul", out=ps[:, 3072 + i * 8:3072 + (i + 1) * 8],
                              lhsT=qf[:, h % 2, i * 128:(i + 1) * 128], rhs=self.kmt[:, h * 8:(h + 1) * 8],
                              start=True, stop=True, signal=(i == 7))
                GT[h] = tk
                P.wait("dve", tk)
                P.wait("dve", SEL2[h - 2])
                a = P.op("dve", "tensor_tensor", out=gm[:, h % 2, :], in0=ps[:, 3072:3072 + 64],
                         in1=self.cmask[:, :], op=ALU.add, signal=True)
                P.wait("dve", a)
                for i in range(8):
                    b = P.op("dve", "max", out=top[:, h % 2, i * 8:(i + 1) * 8], in_=gm[:, h % 2, i * 8:(i + 1) * 8],
                             signal=(i == 7))
                P.wait("dve", b)
                for i in range(8):
                    d = P.op("dve", "tensor_scalar", out=negm[:, h % 2, i, 0:8], in0=gm[:, h % 2, i * 8:(i + 1) * 8],
                             scalar1=top[:, h % 2, i * 8 + 3:i * 8 + 4], scalar2=NEG, op0=ALU.is_lt,
                             op1=ALU.mult, signal=(i == 7))
                SEL1[h] = d

            def sel2(h):
                if h >= HA or not SEL:
                    return
                P.wait("pe", SEL1[h])
                for i in range(8):
                    tk = P.op("pe", "transpose", out=ps[:, 3072 + i * 128:3072 + (i + 1) * 128],
                              in_=negm[:, h % 2, i, :], identity=self.identf[:, :], signal=(i == 7))
                P.wait("act", tk)
                P.wait("act", HEADEND[h - 2])
                SEL2[h] = P.op("act", "activation", out=negmT[:, h % 2, :], in_=ps[:, 3072:4096],
                               func=AF.Copy, signal=True)

            def emitS(h, tile):
                qc, kt, q0, wq = tile
                si = st["si"]
                st["si"] += 1
                bank = 4 + si % 2
                P.wait("pe", RING[si - 2])
                P.wait("pe", LD[h])
                need_sel = SEL and qc >= 2 and kt < 4 * qc + 2
                diag = kt >= 4 * qc
                if need_sel:
                    P.wait("pe", SEL2[h])
                o = ps[:, bank * 512:bank * 512 + wq]
                tk = P.op("pe", "matmul", out=o, lhsT=ks[:, h % 3, kt * 128:(kt + 1) * 128],
                          rhs=qs[:, h % 3, q0:q0 + wq], start=True, stop=not (need_sel or diag),
                          signal=not (need_sel or diag))
                if need_sel:
                    j = kt // 2
                    tk = P.op("pe", "matmul", out=o, lhsT=self.selb[:, j * 128:(j + 1) * 128],
                              rhs=negmT[:, h % 2, q0 - 1024:q0 - 1024 + wq], start=False, stop=not diag,
                              signal=not diag)
                if diag:
                    tk = P.op("pe", "matmul", out=ps[:, bank * 512:bank * 512 + 128], lhsT=self.identb[:, :],
                              rhs=self.trib[:, :], start=False, stop=True, signal=True)
                return si, tk

            def emitExp(si, stok, wq):
                ei = st["ei"]
                st["ei"] += 1
                bank = 4 + si % 2
                P.wait("act", stok)
                P.wait("act", PVT[ei - 4])
                EXPT[ei] = P.op("act", "activation", out=eb[:, ei % 4, 0:wq], in_=ps[:, bank * 512:bank * 512 + wq],
                                func=AF.Exp, scale=scale, signal=True)
                RING[si] = EXPT[ei]
                return ei

            def emitPV(h, tile, ei):
                qc, kt, q0, wq = tile
                qci = st["qci"]
                ob = qci % 2
                g = (h * 128) // 512
                vs = st[("vs", g)]
                vc = (h * 128) % 512
                P.wait("pe", EXPT[ei])
                P.wait("pe", VL[g])
                if kt == 0:
                    P.wait("pe", NORM[qci - 2])
                c0 = q0 - 512 * qc
                last = (kt == 4 * qc + 3)
                P.op("pe", "matmul", out=ps[:, ob * 512 + c0:ob * 512 + c0 + wq],
                     lhsT=v4[:, vs, kt, vc:vc + 128], rhs=eb[:, ei % 4, 0:wq], start=(kt == 0), stop=last)
                PVT[ei] = P.op("pe", "matmul", out=ps[:, (2 + ob) * 512 + c0:(2 + ob) * 512 + c0 + wq],
                               lhsT=self.onesb[:, :], rhs=eb[:, ei % 4, 0:wq], start=(kt == 0), stop=last,
                               signal=True)
                if last:
                    P.wait("dve", PVT[ei])
                    P.wait("dve", NORM[qci - 2])
                    a = P.op("dve", "reciprocal", out=rden[:, ob, :], in_=ps[:, (2 + ob) * 512:(3 + ob) * 512],
                             signal=True)
                    P.wait("dve", a)
                    P.wait("dve", ST[qci - 2])
                    NORM[qci] = P.op("dve", "tensor_tensor", out=osb[:, ob, :], in0=ps[:, ob * 512:(ob + 1) * 512],
                                     in1=rden[:, ob, :], op=ALU.mult, signal=True)
                    P.wait("sp", NORM[qci])
                    ST[qci] = P.dma("sp", self.attnT[h * 128:(h + 1) * 128, qc * 512:(qc + 1) * 512],
                                    osb[:, ob, :], std[ob])
                    st["qci"] += 1

            loads(0)
            loads(1)
            sel1(0)
            sel2(0)
            for h in range(HA):
                loads(h + 2)
                sel1(h + 1)
                nt = len(tiles)
                pend = emitS(h, tiles[0])
                for i in range(nt):
                    si, stok = pend
                    ei = emitExp(si, stok, tiles[i][3])
                    if i + 1 < nt:
                        pend = emitS(h, tiles[i + 1])
                    emitPV(h, tiles[i], ei)
                    if i == 11:
                        sel2(h + 1)
                HEADEND[h] = PVT[st["ei"] - 1]
            P.run()

    def diff_phase(self, name):
        c = self.cfg
        nc = self.nc
        S, HA, HB, WA = c.S, c.HA, c.HB, c.WA
        P = Phase(self, name)
        scale = 128 ** -0.5
        vt = self.vtok.rearrange("(t p) c -> p t c", p=128)
        NT16 = S // 128
        q_base = 2 * HA
        k_base = 2 * HA + 2 * HB
        with ExitStack() as es:
            A_ = self.act
            qs = A_[:, 0:3 * S].rearrange("p (a s) -> p a s", a=3)
            ks = A_[:, 3 * S:6 * S].rearrange("p (a s) -> p a s", a=3)
            v2 = A_[:, 6 * S:6 * S + 2 * NT16 * 256].rearrange("p (a t c) -> p a t c", a=2, t=NT16)
            e0 = 6 * S + 2 * NT16 * 256
            eb = A_[:, e0:e0 + 2048].rearrange("p (a s) -> p a s", a=4)
            on0 = es.enter_context(nc.sbuf_tensor(f"{name}_on0", [128, 2, S], F32))
            rden = es.enter_context(nc.sbuf_tensor(f"{name}_rden", [128, 2, 512], F32))
            o1 = es.enter_context(nc.sbuf_tensor(f"{name}_o1", [128, 2, 2, 512], F32))
            comb = es.enter_context(nc.sbuf_tensor(f"{name}_comb", [128, 2, 2, 512], F32))
            sqb = es.enter_context(nc.sbuf_tensor(f"{name}_sqb", [128, 2, 2, 512], BF16))
            sd = es.enter_context(nc.sbuf_tensor(f"{name}_sd", [128, 512], F32))
            rs = es.enter_context(nc.sbuf_tensor(f"{name}_rs", [128, 512], F32))
            osb = es.enter_context(nc.sbuf_tensor(f"{name}_osb", [128, 2, 2, 512], BF16))
            ps = es.enter_context(nc.psum_tensor(f"{name}_ps", [128, 4096], F32))
            qld = [self.dsem(f"{name}_q{i}") for i in range(3)]
            vld = [self.dsem(f"{name}_v{i}") for i in range(2)]
            std = [self.dsem(f"{name}_st{i}") for i in range(2)]
            LD, VL, UEND, NORM, ST, FIN = (Hist() for _ in range(6))
            RING, EXPT, PVT, RD, SSQREAD = Hist(), Hist(), Hist(), Hist(), Hist()
            st = dict(si=0, ei=0, qci=0, tcount=0, fin=0, rem=4, act_pending=False, pv_started=set())
            tiles = self.attn_tiles()
            pending = []
            NU = 2 * HB

            def loads(u):
                if u >= NU:
                    return
                h, m = u // 2, u % 2
                P.wait("sp", UEND[u - 3])
                P.dma("sp", qs[:, u % 3, :], self.qkT[q_base + u], qld[u % 3])
                LD[u] = P.dma("sp", ks[:, u % 3, :], self.qkT[k_base + u], qld[u % 3])
                if m == 0:
                    P.wait("sp", UEND[u - 3])
                    VL[h] = P.dma("sp", v2[:, h % 2, :, :], vt[:, :, WA + h * 256:WA + (h + 1) * 256], vld[h % 2])

            def emitS(u, tile):
                qc, kt, q0, wq = tile
                si = st["si"]
                st["si"] += 1
                bank = 6 + si % 2
                P.wait("pe", RING[si - 2])
                P.wait("pe", LD[u])
                diag = kt >= 4 * qc
                tk = P.op("pe", "matmul", out=ps[:, bank * 512:bank * 512 + wq],
                          lhsT=ks[:, u % 3, kt * 128:(kt + 1) * 128], rhs=qs[:, u % 3, q0:q0 + wq],
                          start=True, stop=not diag, signal=not diag)
                if diag:
                    tk = P.op("pe", "matmul", out=ps[:, bank * 512:bank * 512 + 128], lhsT=self.identb[:, :],
                              rhs=self.trib[:, :], start=False, stop=True, signal=True)
                return si, tk

            def emitExp(si, stok, wq):
                ei = st["ei"]
                st["ei"] += 1
                bank = 6 + si % 2
                P.wait("act", stok)
                P.wait("act", PVT[ei - 4])
                EXPT[ei] = P.op("act", "activation", out=eb[:, ei % 4, 0:wq], in_=ps[:, bank * 512:bank * 512 + wq],
                                func=AF.Exp, scale=scale, signal=True)
                RING[si] = EXPT[ei]
                return ei

            def fin_stage1(h, qc, fi):
                def f():
                    P.wait("act", NORM[("comb", fi)])
                    P.wait("act", FIN[("ssq", fi - 2)])
                    FIN[("sq", fi)] = P.op("act", "activation", out=sqb[:, fi % 2, :, :], in_=comb[:, fi % 2, :, :],
                                           func=AF.Square, signal=True)
                return f

            def fin_stage2(h, qc, fi):
                def act_part(bank, cur):
                    def g():
                        assert (cur + 1) not in st["pv_started"], "ssq bank would be overwritten"
                        tk = FIN[("ssq", fi)]
                        P.wait("act", tk)
                        P.wait("act", FIN[("out", fi - 1)])
                        a = P.op("act", "activation", out=sd[:, :], in_=ps[:, bank * 512:(bank + 1) * 512],
                                 func=AF.Ln, bias=EPS, scale=1.0 / 256, signal=True)
                        SSQREAD[cur + 1] = a
                        st["lastln"] = a
                        P.wait("act", a)
                        b = P.op("act", "activation", out=rs[:, :], in_=sd[:, :], func=AF.Exp, scale=-0.5,
                                 signal=True)
                        FIN[("rs", fi)] = b
                        P.wait("dve", b)
                        P.wait("dve", ST[fi - 2])
                        for half in range(2):
                            d = P.op("dve", "scalar_tensor_tensor", out=osb[:, fi % 2, half, :],
                                     in0=comb[:, fi % 2, half, :],
                                     scalar=self.par[:, c.c_sub + half:c.c_sub + half + 1], in1=rs[:, :],
                                     op0=ALU.mult, op1=ALU.mult, signal=(half == 1))
                        FIN[("out", fi)] = d
                        P.wait("sp", d)
                        for half in range(2):
                            r0 = WA + h * 256 + half * 128
                            ST[fi] = P.dma("sp", self.attnT[r0:r0 + 128, qc * 512:(qc + 1) * 512],
                                           osb[:, fi % 2, half, :], std[fi % 2])
                    return g

                def f():
                    cur = st["qci"]
                    bank = 4 + (cur + 1) % 2
                    P.wait("pe", RD[cur - 1])
                    P.wait("pe", st.get("lastln"))
                    P.wait("pe", st.get("pend_ln"))
                    P.wait("pe", FIN[("sq", fi)])
                    P.op("pe", "matmul", out=ps[:, bank * 512:(bank + 1) * 512], lhsT=self.onesb[:, :],
                         rhs=sqb[:, fi % 2, 0, :], start=True, stop=False)
                    tk = P.op("pe", "matmul", out=ps[:, bank * 512:(bank + 1) * 512], lhsT=self.onesb[:, :],
                              rhs=sqb[:, fi % 2, 1, :], start=False, stop=True, signal=True)
                    FIN[("ssq", fi)] = tk
                    g = act_part(bank, cur)
                    if st["rem"] >= 3 and not st["act_pending"]:
                        st["act_pending"] = True

                        def g2():
                            g()
                            st["act_pending"] = False
                        pending.append((st["tcount"] + 2, g2))
                    else:
                        g()
                return f

            def emitPV(u, tile, ei):
                h, m = u // 2, u % 2
                qc, kt, q0, wq = tile
                qci = st["qci"]
                ob = qci % 2
                P.wait("pe", EXPT[ei])
                P.wait("pe", VL[h])
                if kt == 0:
                    P.wait("pe", NORM[qci - 2])
                    P.wait("pe", SSQREAD[qci])
                    st["pv_started"].add(qci)
                c0 = q0 - 512 * qc
                last = (kt == 4 * qc + 3)
                for half in range(2):
                    P.op("pe", "matmul", out=ps[:, (2 * ob + half) * 512 + c0:(2 * ob + half) * 512 + c0 + wq],
                         lhsT=v2[:, h % 2, kt, half * 128:(half + 1) * 128], rhs=eb[:, ei % 4, 0:wq],
                         start=(kt == 0), stop=last)
                PVT[ei] = P.op("pe", "matmul", out=ps[:, (4 + ob) * 512 + c0:(4 + ob) * 512 + c0 + wq],
                               lhsT=self.onesb[:, :], rhs=eb[:, ei % 4, 0:wq], start=(kt == 0), stop=last,
                               signal=True)
                if last:
                    P.wait("dve", PVT[ei])
                    a = P.op("dve", "reciprocal", out=rden[:, ob, :], in_=ps[:, (4 + ob) * 512:(5 + ob) * 512],
                             signal=True)
                    RD[qci] = a
                    P.wait("dve", a)
                    if m == 0:
                        for half in range(2):
                            d = P.op("dve", "tensor_tensor", out=on0[:, half, qc * 512:(qc + 1) * 512],
                                     in0=ps[:, (2 * ob + half) * 512:(2 * ob + half + 1) * 512],
                                     in1=rden[:, ob, :], op=ALU.mult, signal=(half == 1))
                        NORM[qci] = d
                    else:
                        fi = st["fin"]
                        st["fin"] += 1
                        P.wait("dve", FIN[("out", fi - 2)])
                        P.wait("dve", FIN[("sq", fi - 2)])
                        for half in range(2):
                            d = P.op("dve", "tensor_tensor", out=o1[:, fi % 2, half, :],
                                     in0=ps[:, (2 * ob + half) * 512:(2 * ob + half + 1) * 512],
                                     in1=rden[:, ob, :], op=ALU.mult, signal=(half == 1))
                        NORM[qci] = d
                        P.wait("dve", d)
                        for half in range(2):
                            e = P.op("dve", "scalar_tensor_tensor", out=comb[:, fi % 2, half, :],
                                     in0=o1[:, fi % 2, half, :], scalar=self.lamc[:, 5:6],
                                     in1=on0[:, half, qc * 512:(qc + 1) * 512], op0=ALU.mult, op1=ALU.add,
                                     signal=(half == 1))
                        NORM[("comb", fi)] = e
                        pending.append((st["tcount"] + 7, fin_stage1(h, qc, fi)))
                        pending.append((st["tcount"] + 10, fin_stage2(h, qc, fi)))
                    st["qci"] += 1

            def run_pending(force=False):
                keep = []
                for due, f in pending:
                    if force or due <= st["tcount"]:
                        f()
                    else:
                        keep.append((due, f))
                pending[:] = keep

            loads(0)
            loads(1)
            for u in range(NU):
                loads(u + 2)
                nt = len(tiles)
                pend = emitS(u, tiles[0])
                for i in range(nt):
                    si, stok = pend
                    ei = emitExp(si, stok, tiles[i][3])
                    if i + 1 < nt:
                        pend = emitS(u, tiles[i + 1])
                    emitPV(u, tiles[i], ei)
                    st["rem"] = (4 * tiles[i][0] + 3) - tiles[i][1]
                    if st["rem"] == 0:
                        st["rem"] = 4
                    st["tcount"] += 1
                    run_pending()
                UEND[u] = PVT[st["ei"] - 1]
            run_pending(force=True)
            P.run()


    def load_act_phase(self, name, srcT, nk):
        S = self.cfg.S
        P = Phase(self, name)
        ds = [self.dsem(f"{name}_l{i}") for i in range(4)]
        for k in range(nk):
            P.dma("sp", self.A(k, 0, S), srcT[k * 128:(k + 1) * 128, :], ds[k % 4])
        P.run()

    def proj_res_phase(self, name, wdram, nblk, nk, rhs_fn, res_dram, dst_dram, w_res=None, rstd_out=None, preload=None):
        c = self.cfg
        nc = self.nc
        S, D = c.S, c.D
        HT = S // 2
        NU = nblk * 2
        P = Phase(self, name)
        RW = 3
        with ExitStack() as es:
            if w_res is None:
                w = es.enter_context(nc.sbuf_tensor(f"{name}_w", [128, RW, nk * 128], BF16))
                wld = [self.dsem(f"{name}_w{i}") for i in range(RW)]
            rsb = es.enter_context(nc.sbuf_tensor(f"{name}_res", [128, 2, HT], F32))
            yb = es.enter_context(nc.sbuf_tensor(f"{name}_yb", [128, 2, HT], F32))
            sqb = es.enter_context(nc.sbuf_tensor(f"{name}_sqb", [128, 2, HT], BF16))
            sd = es.enter_context(nc.sbuf_tensor(f"{name}_sd", [128, S], F32))
            rs = es.enter_context(nc.sbuf_tensor(f"{name}_rs", [128, S], F32))
            ps = es.enter_context(nc.psum_tensor(f"{name}_ps", [128, 4096], F32))
            rld = [self.dsem(f"{name}_r{i}") for i in range(2)]
            std = [self.dsem(f"{name}_s{i}") for i in range(2)]
            MM, WL, RL, ADD, SQ, SS, ST = (Hist() for _ in range(7))
            pre_toks = []
            if preload is not None:
                pds = [self.dsem(f"{name}_pl{i}") for i in range(4)]
                for k in range(preload[1]):
                    P.dma("sp", self.A(k, 0, S), preload[0][k * 128:(k + 1) * 128, :], pds[k % 4])
                pre_toks = [(d_.h, d_.n) for d_ in pds]

            def emit_ss(v):
                if not (0 <= v < NU):
                    return
                ob, hh = v // 2, v % 2
                P.wait("pe", SQ[v])
                for t in range(HT // 512):
                    SS[v] = P.op("pe", "matmul", out=ps[:, 2048 + hh * HT + t * 512:2048 + hh * HT + (t + 1) * 512],
                                 lhsT=self.onesb[:, :], rhs=sqb[:, v % 2, t * 512:(t + 1) * 512],
                                 start=(ob == 0), stop=(ob == nblk - 1), signal=(t == HT // 512 - 1))

            for u in range(NU):
                ob, hh = u // 2, u % 2
                g = u % 2
                if w_res is None:
                    slot = ob % RW
                    if u == 0:
                        for o2 in range(min(RW - 1, nblk)):
                            WL[o2] = P.dma("pool", w[:, o2 % RW, :], wdram[o2], wld[o2 % RW])
                    if hh == 0 and ob + RW - 1 < nblk:
                        o2 = ob + RW - 1
                        P.wait("pool", MM[2 * (ob - 1) + 1])
                        WL[o2] = P.dma("pool", w[:, o2 % RW, :], wdram[o2], wld[o2 % RW])
                    P.wait("pe", WL[ob])
                    lhs = (lambda slot: (lambda k: w[:, slot, k * 128:(k + 1) * 128]))(slot)
                else:
                    lhs = (lambda ob: (lambda k: w_res(k, ob)))(ob)
                P.wait("sp", ADD[u - 2])
                RL[u] = P.dma("sp", rsb[:, g, :], res_dram[ob * 128:(ob + 1) * 128, hh * HT:(hh + 1) * HT], rld[g])
                P.wait("pe", ADD[u - 2])
                if u == 0:
                    for tk_ in pre_toks:
                        P.wait("pe", tk_)
                for k in range(nk):
                    for t in range(HT // 512):
                        tk = P.op("pe", "matmul", out=ps[:, g * HT + t * 512:g * HT + (t + 1) * 512],
                                  lhsT=lhs(k), rhs=rhs_fn(k, hh * HT + t * 512, hh * HT + (t + 1) * 512),
                                  start=(k == 0), stop=(k == nk - 1),
                                  signal=(k == nk - 1 and t == HT // 512 - 1))
                MM[u] = tk
                emit_ss(u - 1)
                P.wait("dve", MM[u])
                P.wait("dve", RL[u])
                P.wait("dve", ST[u - 2])
                P.wait("dve", SQ[u - 2])
                ADD[u] = P.op("dve", "tensor_tensor", out=yb[:, g, :], in0=ps[:, g * HT:(g + 1) * HT],
                              in1=rsb[:, g, :], op=ALU.add, signal=True)
                P.wait("act", ADD[u])
                P.wait("act", SS[u - 2])
                SQ[u] = P.op("act", "activation", out=sqb[:, g, :], in_=yb[:, g, :], func=AF.Square, signal=True)
                ST[u] = P.dma("act", dst_dram[ob * 128:(ob + 1) * 128, hh * HT:(hh + 1) * HT], yb[:, g, :], std[g])
            emit_ss(NU - 1)
            if rstd_out is not None:
                P.wait("act", SS[NU - 1])
                P.wait("act", SS[NU - 2])
                a1 = P.op("act", "activation", out=sd[:, :], in_=ps[:, 2048:2048 + S], func=AF.Ln, bias=EPS,
                          scale=1.0 / D, signal=True)
                P.wait("act", a1)
                a2 = P.op("act", "activation", out=rs[:, :], in_=sd[:, :], func=AF.Exp, scale=-0.5, signal=True)
                P.wait("sp", a2)
                P.dma("sp", rstd_out[:, :], rs[:, :], rld[0])
            P.run()

    def cross_phase(self):
        c = self.cfg
        nc = self.nc
        S, D, KC, M = c.S, c.D, c.KC, c.M
        scale = 128 ** -0.5
        with ExitStack() as es6:
            km = es6.enter_context(nc.sbuf_tensor("p6_km", [128, 4, M], BF16))
            vm = es6.enter_context(nc.sbuf_tensor("p6_vm", [128, M // 128, 512], BF16))
            with nc.sbuf_tensor("p6_memn", [128, KC, M], BF16) as memn:
                self.norm_phase("p6a", self.memT, M, c.c_gmem, dst_fn=lambda k: memn[:, k, :])
                mfn = lambda k, a, b: memn[:, k, a:b]
                self.qk_phase("p6b", self.wmk, 4, None, [(c.c_km, False, None, None)] * 4, S=M, rhs_fn=mfn,
                              dst_sb=lambda s, a, b: km[:, s, a:b], RW=2)
                self.v_phase("p6c", wdram=self.wmv, ncb=1, S=M, lhs_fn=mfn, dst_sb=lambda t: vm[:, t, :])
            qm = es6.enter_context(nc.sbuf_tensor("p6_qm", [128, 4, S], BF16))
            self.norm_phase("p6d", self.h1T, S, c.c_gcross, rstd_src=self.rstd1)
            self.qk_phase("p6e", self.wmq, 4, None, [(c.c_qm, False, None, None)] * 4,
                          dst_sb=lambda s, a, b: qm[:, s, a:b], RW=2)
            om = lambda k, a, b: self.act[:, k * S + a:k * S + b]
            wmo_off = 4 * S
            P = Phase(self, "p6f")
            with ExitStack() as es:
                eb = es.enter_context(nc.sbuf_tensor("p6_e", [128, 4, 512], BF16))
                rden = es.enter_context(nc.sbuf_tensor("p6_rden", [128, 2, 512], F32))
                ps = es.enter_context(nc.psum_tensor("p6_ps", [128, 4096], F32))
                wl = self.dsem("p6_wmo")
                WMO = P.dma("pool", self.act[:, wmo_off:wmo_off + 4 * D], self.wmo[:, :], wl)
                RING, EXPT, PVT, NORM = (Hist() for _ in range(4))
                si = ei = qci = 0
                MT = M // 128
                for h in range(4):
                    for qc in range(S // 512):
                        ob = qci % 2
                        pend = []
                        for mt in range(MT):
                            bank = 4 + si % 4
                            P.wait("pe", RING[si - 4])
                            tk = P.op("pe", "matmul", out=ps[:, bank * 512:(bank + 1) * 512],
                                      lhsT=km[:, h, mt * 128:(mt + 1) * 128], rhs=qm[:, h, qc * 512:(qc + 1) * 512],
                                      start=True, stop=True, signal=True)
                            P.wait("act", tk)
                            P.wait("act", PVT[ei - 4])
                            EXPT[ei] = P.op("act", "activation", out=eb[:, ei % 4, :], in_=ps[:, bank * 512:(bank + 1) * 512],
                                            func=AF.Exp, scale=scale, signal=True)
                            RING[si] = EXPT[ei]
                            pend.append((mt, ei))
                            si += 1
                            ei += 1
                        for mt, e in pend:
                            P.wait("pe", EXPT[e])
                            if mt == 0:
                                P.wait("pe", NORM[qci - 2])
                            P.op("pe", "matmul", out=ps[:, ob * 512:(ob + 1) * 512], lhsT=vm[:, mt, h * 128:(h + 1) * 128],
                                 rhs=eb[:, e % 4, :], start=(mt == 0), stop=(mt == MT - 1))
                            PVT[e] = P.op("pe", "matmul", out=ps[:, (2 + ob) * 512:(3 + ob) * 512], lhsT=self.onesb[:, :],
                                          rhs=eb[:, e % 4, :], start=(mt == 0), stop=(mt == MT - 1), signal=True)
                        P.wait("dve", PVT[pend[-1][1]])
                        a = P.op("dve", "reciprocal", out=rden[:, ob, :], in_=ps[:, (2 + ob) * 512:(3 + ob) * 512], signal=True)
                        P.wait("dve", a)
                        NORM[qci] = P.op("dve", "tensor_tensor", out=om(h, qc * 512, (qc + 1) * 512),
                                         in0=ps[:, ob * 512:(ob + 1) * 512], in1=rden[:, ob, :], op=ALU.mult, signal=True)
                        qci += 1
                P.run()
            self.proj_res_phase("p6g", None, KC, 4, om, self.h1T, self.h2T, rstd_out=self.rstd2,
                                w_res=lambda k, ob: self.act[:, wmo_off + k * D + ob * 128:wmo_off + k * D + (ob + 1) * 128])

    def ffn_up_phase(self, name):
        c = self.cfg
        nc = self.nc
        S, KC, FC = c.S, c.KC, c.FC
        P = Phase(self, name)
        RW = 4
        HT = S // 2
        with ExitStack() as es:
            w = es.enter_context(nc.sbuf_tensor(f"{name}_w", [128, RW, KC * 128], BF16))
            sg = es.enter_context(nc.sbuf_tensor(f"{name}_sg", [128, 2, HT], F32))
            ob_ = es.enter_context(nc.sbuf_tensor(f"{name}_o", [128, 2, HT], BF16))
            ps = es.enter_context(nc.psum_tensor(f"{name}_ps", [128, 4096], F32))
            wld = [self.dsem(f"{name}_w{i}") for i in range(RW)]
            std = [self.dsem(f"{name}_s{i}") for i in range(2)]
            MMG, MMU, WLG, WLU, SIL, MUL, ST = (Hist() for _ in range(7))
            n = 0
            for f in range(FC):
                sg_, su_ = (2 * f) % RW, (2 * f + 1) % RW
                P.wait("pool", MMG[f - 2])
                WLG[f] = P.dma("pool", w[:, sg_, :], self.wg[f], wld[sg_])
                P.wait("pool", MMU[f - 2])
                WLU[f] = P.dma("pool", w[:, su_, :], self.wu[f], wld[su_])
                for which, slot, WL, MMT in ((0, sg_, WLG, MMG), (1, su_, WLU, MMU)):
                    P.wait("pe", WL[f])
                    if which == 0:
                        P.wait("pe", SIL[2 * (f - 1) + 1])
                    else:
                        P.wait("pe", MUL[2 * (f - 1) + 1])
                    for k in range(KC):
                        for t in range(S // 512):
                            tk = P.op("pe", "matmul", out=ps[:, which * 2048 + t * 512:which * 2048 + (t + 1) * 512],
                                      lhsT=w[:, slot, k * 128:(k + 1) * 128], rhs=self.A(k, t * 512, (t + 1) * 512),
                                      start=(k == 0), stop=(k == KC - 1),
                                      signal=(k == KC - 1 and t == S // 512 - 1))
                    MMT[f] = tk
                for hh in range(2):
                    P.wait("act", MMG[f])
                    P.wait("act", MUL[n - 2])
                    SIL[n] = P.op("act", "activation", out=sg[:, n % 2, :], in_=ps[:, hh * HT:(hh + 1) * HT],
                                  func=AF.Silu, signal=True)
                    P.wait("dve", SIL[n])
                    P.wait("dve", MMU[f])
                    P.wait("dve", ST[n - 2])
                    MUL[n] = P.op("dve", "tensor_tensor", out=ob_[:, n % 2, :], in0=ps[:, 2048 + hh * HT:2048 + (hh + 1) * HT],
                                  in1=sg[:, n % 2, :], op=ALU.mult, signal=True)
                    P.wait("sp", MUL[n])
                    ST[n] = P.dma("sp", self.actT[f * 128:(f + 1) * 128, hh * HT:(hh + 1) * HT], ob_[:, n % 2, :], std[n % 2])
                    n += 1
            P.run()

    def ffn_down_phase(self, name):
        c = self.cfg
        nc = self.nc
        S, D, FC = c.S, c.D, c.FC
        P = Phase(self, name)
        NP = D // 512
        HT = S // 2
        RC = 4
        G = 8 if FC >= 16 else (4 if FC >= 8 else 1)
        kb = [int(round(i * FC / G)) for i in range(G + 1)]
        grp_of_end = {kb[g + 1] - 1: g for g in range(G)}
        grp_of_start = {kb[g]: g for g in range(G)}
        coff = FC * 512
        with ExitStack() as es:
            rsb = es.enter_context(nc.sbuf_tensor(f"{name}_res", [128, 4, HT], F32))
            yb = es.enter_context(nc.sbuf_tensor(f"{name}_yb", [128, 4, HT], F32))
            tmp = es.enter_context(nc.sbuf_tensor(f"{name}_tmp", [128, 2, HT], F32))
            ps = es.enter_context(nc.psum_tensor(f"{name}_ps", [128, 4096], F32))
            wt = lambda k, ob: self.act[:, k * 512 + ob * 128:k * 512 + (ob + 1) * 128]
            ch = lambda s, a, b_: self.act[:, coff + s * 4 * HT + a:coff + s * 4 * HT + b_]
            wld = [self.dsem(f"{name}_w{i}") for i in range(G)]
            cld = [self.dsem(f"{name}_c{i}") for i in range(RC)]
            rld = [self.dsem(f"{name}_r{i}") for i in range(4)]
            std = [self.dsem(f"{name}_s{i}") for i in range(4)]
            MM, WL, CL, RL, ADD, ST, FREE = (Hist() for _ in range(7))
            n = 0
            ei = 0
            for g in range(G):
                WL[(0, g)] = P.dma("pool", self.act[:, kb[g] * 512:kb[g + 1] * 512],
                                   self.wd[0][:, kb[g] * 512:kb[g + 1] * 512], wld[g])
            CK = 4
            NG = (FC + CK - 1) // CK
            for pb in range(NP):
                for th in range(2):
                    for kg in range(NG):
                        k0 = kg * CK
                        nk = min(CK, FC - k0)
                        cs = n % RC
                        if kg == max(0, NG - 6):
                            for j in range(4):
                                e = ei + j
                                P.wait("act", ADD[e - 4])
                                RL[e] = P.dma("act", rsb[:, e % 4, :],
                                              self.h2T[(pb * 4 + j) * 128:(pb * 4 + j + 1) * 128, th * HT:(th + 1) * HT], rld[e % 4])
                        P.wait("sp", MM[n - RC])
                        CL[n] = P.dma("sp", ch(cs, 0, nk * HT).rearrange("p (a t) -> p a t", a=nk),
                                      self.actT[k0 * 128:(k0 + nk) * 128, th * HT:(th + 1) * HT].rearrange("(a p) t -> p a t", p=128),
                                      cld[cs])
                        P.wait("pe", CL[n])
                        for kk in range(nk):
                            k = k0 + kk
                            if th == 0 and k in grp_of_start:
                                P.wait("pe", WL[(pb, grp_of_start[k])])
                            for ob in range(4):
                                if k == 0:
                                    P.wait("pe", FREE[ei - 4 + ob])
                                for t in range(HT // 512):
                                    endg = (th == 1 and k in grp_of_end and pb + 1 < NP)
                                    tk = P.op("pe", "matmul", out=ps[:, ob * HT + t * 512:ob * HT + (t + 1) * 512],
                                              lhsT=wt(k, ob), rhs=ch(cs, kk * HT + t * 512, kk * HT + (t + 1) * 512),
                                              start=(k == 0), stop=(k == FC - 1),
                                              signal=(ob == 3 and t == HT // 512 - 1 and (kk == nk - 1 or endg)))
                            if th == 1 and k in grp_of_end and pb + 1 < NP:
                                g = grp_of_end[k]
                                P.wait("pool", tk)
                                WL[(pb + 1, g)] = P.dma("pool", self.act[:, kb[g] * 512:kb[g + 1] * 512],
                                                        self.wd[pb + 1][:, kb[g] * 512:kb[g + 1] * 512], wld[g])
                        MM[n] = tk
                        n += 1
                    last = tk

                    for ob in (2, 3):
                        e = ei + ob
                        P.wait("act", last)
                        P.wait("act", ADD[e - 4])
                        FREE[e] = P.op("act", "activation", out=tmp[:, ob - 2, :], in_=ps[:, ob * HT:(ob + 1) * HT],
                                       func=AF.Copy, signal=True)
                    for ob in range(4):
                        r0 = (pb * 4 + ob) * 128
                        e = ei + ob
                        P.wait("dve", last)
                        P.wait("dve", RL[e])
                        P.wait("dve", ST[e - 4])
                        if ob < 2:
                            src_ap = ps[:, ob * HT:(ob + 1) * HT]
                        else:
                            P.wait("dve", FREE[e])
                            src_ap = tmp[:, ob - 2, :]
                        ADD[e] = P.op("dve", "tensor_tensor", out=yb[:, e % 4, :], in0=src_ap,
                                      in1=rsb[:, e % 4, :], op=ALU.add, signal=True)
                        if ob < 2:
                            FREE[e] = ADD[e]
                        P.wait("act", ADD[e])
                        ST[e] = P.dma("act", self.yT[r0:r0 + 128, th * HT:(th + 1) * HT], yb[:, e % 4, :], std[e % 4])
                    ei += 4
            P.run()

    def final_phase(self):
        if self.upto >= 8:
            return
        nc = self.nc
        P = Phase(self, "pfin")
        with nc.sbuf_tensor("fin_x", [128, self.cfg.S], F32) as fx:
            d1 = self.dsem("fin1")
            t = P.dma("sp", fx[:, :], self.xT[0:128, :], d1)
            P.wait("sp", t)
            P.dma("sp", self.yT[0:128, :], fx[:, :], d1)
            P.run()


def tile_w(w, C):
    K, N = w.shape
    return np.ascontiguousarray(
        w.reshape(K // 128, 128, N // C, C).transpose(2, 1, 0, 3).reshape(N // C, 128, (K // 128) * C))


def host_consts(c):
    S = c.S
    pos = np.arange(S, dtype=np.float32)
    inv = (np.float32(500000.0) ** (-np.arange(0, 32, 2, dtype=np.float32) / np.float32(32))).astype(np.float32)
    ang = (pos[:, None] * inv[None, :]).astype(np.float32)
    cos = np.cos(ang).astype(np.float32).T
    sin = np.sin(ang).astype(np.float32).T
    rope = np.zeros((128, 2 * S), np.float32)
    rope[0:16, 0:S] = cos
    rope[16:32, 0:S] = cos
    rope[0:16, S:] = sin
    rope[16:32, S:] = sin
    cst = np.zeros((128, 128 * 4 + 64), np.float32)
    cst[:, 0:128] = np.eye(128, dtype=np.float32)
    RT = np.zeros((128, 128), np.float32)
    for p in range(16):
        RT[p + 16, p] = -1.0
        RT[p, p + 16] = 1.0
    cst[:, 128:256] = RT
    tri = np.zeros((128, 128), np.float32)
    kk, qq = np.meshgrid(np.arange(128), np.arange(128), indexing="ij")
    tri[kk > qq] = NEG
    cst[:, 256:384] = tri
    cm = np.zeros((8, 8), np.float32)
    for qt in range(8, 16):
        i = qt // 2
        for j in range(8):
            cm[qt - 8, j] = 0.0 if j < i else (1e30 if j == i else -1e30)
    cst[:, 512:576] = cm.reshape(1, 64)
    return rope, cst


def host_prep(c, inp):
    D, KC = c.D, c.KC
    g = lambda k: np.asarray(inp[k], dtype=np.float32)[0]
    w_in = g("w_in")
    WA = c.WA
    WB = c.HB * 256
    o_qa, o_ka, o_va = 0, WA, 2 * WA
    o_qb, o_kb, o_vb = 3 * WA, 3 * WA + WB, 3 * WA + 2 * WB
    wqk_cols = np.concatenate([w_in[:, o_qa:o_qa + WA], w_in[:, o_ka:o_ka + WA],
                               w_in[:, o_qb:o_qb + WB], w_in[:, o_kb:o_kb + WB]], axis=1)
    wv_cols = np.concatenate([w_in[:, o_va:o_va + WA], w_in[:, o_vb:o_vb + WB]], axis=1)
    shared = {
        "wqk": tile_w(wqk_cols, 128),
        "wv": tile_w(wv_cols, 512),
        "wo": tile_w(g("w_out"), 128),
        "wmq": tile_w(g("w_mq"), 128),
        "wmk": tile_w(g("w_mk"), 128),
        "wmv": tile_w(g("w_mv"), 512),
        "wmo": np.ascontiguousarray(g("w_mo").reshape(4, 128, D).transpose(1, 0, 2).reshape(128, 4 * D)),
        "wg": tile_w(g("w_gate"), 128),
        "wu": tile_w(g("w_up"), 128),
        "wd": tile_w(g("w_down"), 512),
    }
    par = np.zeros((128, c.NPAR), np.float32)
    col = lambda v: v.reshape(-1, 128).T
    par[:, c.c_gmix:c.c_gmix + KC] = col(g("norm_mix_g"))
    par[:, c.c_gcross:c.c_gcross + KC] = col(g("norm_cross_g"))
    par[:, c.c_gmem:c.c_gmem + KC] = col(g("norm_mem_g"))
    par[:, c.c_gffn:c.c_gffn + KC] = col(g("norm_ffn_g"))
    par[:, c.c_qa] = g("q_norm_a")
    par[:, c.c_ka] = g("k_norm_a")
    par[:, c.c_qb] = g("q_norm_b")
    par[:, c.c_kb] = g("k_norm_b")
    par[:, c.c_qm] = g("q_norm_m")
    par[:, c.c_km] = g("k_norm_m")
    par[:, c.c_sub:c.c_sub + 2] = col(g("diff_subln_g"))
    par[:, c.c_lam + 0] = g("lam_q1")
    par[:, c.c_lam + 1] = g("lam_k1")
    par[:, c.c_lam + 2] = g("lam_q2")
    par[:, c.c_lam + 3] = g("lam_k2")
    rope, cst = host_consts(c)
    shared.update(par=par, rope=rope, cst=cst)
    x = np.asarray(inp["x"], dtype=np.float32)
    mem = np.asarray(inp["mem"], dtype=np.float32)
    in_maps = []
    for b in range(x.shape[0]):
        m = dict(shared)
        m["xT"] = np.ascontiguousarray(x[b].T)
        m["memT"] = np.ascontiguousarray(mem[b].T)
        in_maps.append(m)
    return in_maps


def run(cfg, inp, debug=False, upto=99, trace=False):
    K = Kern(cfg, debug=debug, upto=upto)
    nc = K.build()
    in_maps = host_prep(cfg, inp)
    res = run_bass_kernel_spmd(nc, in_maps, core_ids=list(range(len(in_maps))), trace=trace)
    return res


def kernel(**inputs):
    cfg = Cfg()
    res = run(cfg, inputs)
    out = np.stack([np.ascontiguousarray(r["yT"].T) for r in res.results], axis=0)
    return out.astype(np.float32)
```
